# Optimizing a Trainium2 kernel written in Bass

```python
import math
import jax, jax.numpy as jnp
from jax import lax
import numpy as np

D_MODEL = 1024
BATCH = 16
SEQ = 2048
DEPTH = 2

GRID_W = 64
CTX_LEN = 256
N_SUB = 3
D_FF = 2816
EPS = 1e-6
ROPE_THETA = 10000.0
ROPE_DIM = 64
Q_BLOCK = 128
DA_HEADS = 4
DA_DH = 64
DA_DV = 2 * DA_DH
RET_HEADS = 4
RET_DK = 64
RET_DV = 128
RET_CHUNK = 128
GQA_HEADS = 8
GQA_KV = 2
GQA_DH = 64
N_BRANCH = 3
BRANCH_W = 512

PROJ_NAMES = ('a_q', 'a_k', 'a_v', 'b_q', 'b_k', 'b_v', 'b_g', 'c_q', 'c_k', 'c_v', 'gate')
PROJ_SIZES = (DA_HEADS * 2 * DA_DH, DA_HEADS * 2 * DA_DH, DA_HEADS * DA_DV,
              RET_HEADS * RET_DK, RET_HEADS * RET_DK, RET_HEADS * RET_DV, RET_HEADS * RET_DV,
              GQA_HEADS * GQA_DH, GQA_KV * GQA_DH, GQA_KV * GQA_DH, N_BRANCH * D_MODEL)
IN_COLS = sum(PROJ_SIZES)
KV_NAMES = ('a_k', 'a_v', 'b_k', 'b_v', 'c_k', 'c_v')

kernel_name = 'hybrid_gated_diffattn_retention_gqa_block'


def rms_norm(x, g):
    xf = x.astype(jnp.float32)
    y = xf * lax.rsqrt(jnp.mean(xf * xf, axis=-1, keepdims=True) + EPS)
    return (y * g.astype(jnp.float32)).astype(x.dtype)


def axial_rope(n_tok, dim):
    rows = n_tok // GRID_W
    row = jnp.repeat(jnp.arange(rows, dtype=jnp.float32), GRID_W)
    col = jnp.tile(jnp.arange(GRID_W, dtype=jnp.float32), rows)
    n_freq = dim // 4
    inv = ROPE_THETA ** (-jnp.arange(n_freq, dtype=jnp.float32) / n_freq)
    ang = jnp.concatenate([row[:, None] * inv, col[:, None] * inv], axis=-1)
    return jnp.cos(ang), jnp.sin(ang)


def apply_rope(x, cos, sin):
    cos, sin = cos.astype(x.dtype), sin.astype(x.dtype)
    x1, x2 = jnp.split(x, 2, axis=-1)
    return jnp.concatenate([x1 * cos - x2 * sin, x1 * sin + x2 * cos], axis=-1)


def proj_spans():
    spans, lo = {}, 0
    for name, size in zip(PROJ_NAMES, PROJ_SIZES):
        spans[name] = (lo, lo + size)
        lo += size
    return spans


def project(y, w, names):
    spans = proj_spans()
    sel = [spans[nm] for nm in names]
    w_sel = w if names == PROJ_NAMES else jnp.concatenate([w[:, lo:hi] for lo, hi in sel], axis=1)
    p = y @ w_sel
    out, o = {}, 0
    for nm, (lo, hi) in zip(names, sel):
        out[nm] = p[..., o:o + hi - lo]
        o += hi - lo
    return out


def heads(t, n_heads, d):
    b, n, _ = t.shape
    return t.reshape(b, n, n_heads, d).transpose(0, 2, 1, 3)


def merge_heads(t):
    b, h, n, d = t.shape
    return t.transpose(0, 2, 1, 3).reshape(b, n, h * d)


def diff_heads(t):
    b, n, _ = t.shape
    return t.reshape(b, n, DA_HEADS, 2, DA_DH).transpose(0, 2, 3, 1, 4)


def gqa_q_heads(t):
    b, n, _ = t.shape
    return t.reshape(b, n, GQA_KV, GQA_HEADS // GQA_KV, GQA_DH).transpose(0, 2, 3, 1, 4)


def sweep_query_blocks(block_fn, q):
    *lead, n, d = q.shape
    nb = n // Q_BLOCK
    qb = jnp.moveaxis(q.reshape(*lead, nb, Q_BLOCK, d), -3, 0)
    ob = jnp.moveaxis(lax.map(block_fn, qb), 0, -3)
    return ob.reshape(*ob.shape[:-3], n, ob.shape[-1])


def diff_attention(q, k, v, lam):
    scale = DA_DH ** -0.5
    def block(qb):
        s = jnp.einsum('bhcqd,bhctd->bhcqt', qb, k).astype(jnp.float32) * scale
        p = jax.nn.softmax(s, axis=-1)
        a = p[:, :, 0] - lam * p[:, :, 1]
        return jnp.einsum('bhqt,bhtv->bhqv', a.astype(v.dtype), v)
    return sweep_query_blocks(block, q)


def gqa_attention(q, k, v):
    scale = GQA_DH ** -0.5
    def block(qb):
        s = jnp.einsum('bgrqd,bgtd->bgrqt', qb, k).astype(jnp.float32) * scale
        p = jax.nn.softmax(s, axis=-1)
        return jnp.einsum('bgrqt,bgtd->bgrqd', p.astype(v.dtype), v)
    return sweep_query_blocks(block, q)


def retention_chunks(q, k, v, log_gamma, state0):
    b, h, n, _ = q.shape
    dv = v.shape[-1]
    nc = n // RET_CHUNK
    pos = jnp.arange(RET_CHUNK, dtype=jnp.float32)
    rel = pos[:, None] - pos[None, :]
    intra = jnp.where(rel >= 0, jnp.exp(log_gamma[:, None, None] * jnp.maximum(rel, 0.0)), 0.0)
    q_dec = jnp.exp(log_gamma[:, None] * (pos + 1.0))[..., None]
    k_dec = jnp.exp(log_gamma[:, None] * (RET_CHUNK - 1.0 - pos))[..., None]
    s_dec = jnp.exp(log_gamma * RET_CHUNK)[:, None, None]

    def to_chunks(t):
        return jnp.moveaxis(t.astype(jnp.float32).reshape(b, h, nc, RET_CHUNK, t.shape[-1]), 2, 0)

    def step(state, qkv):
        qc, kc, vc = qkv
        att = jnp.einsum('bhid,bhjd->bhij', qc, kc) * intra
        o = jnp.einsum('bhij,bhjv->bhiv', att, vc) + jnp.einsum('bhid,bhdv->bhiv', qc * q_dec, state)
        state = state * s_dec + jnp.einsum('bhjd,bhjv->bhdv', kc * k_dec, vc)
        return state, o

    state, o = lax.scan(step, state0, (to_chunks(q), to_chunks(k), to_chunks(v)))
    return jnp.moveaxis(o, 0, 2).reshape(b, h, n, dv), state


def retention_state(k, v, log_gamma):
    n = k.shape[2]
    w = jnp.exp(log_gamma[:, None] * (n - 1.0 - jnp.arange(n, dtype=jnp.float32)))
    return jnp.einsum('bhjd,bhjv->bhdv', k.astype(jnp.float32) * w[..., None], v.astype(jnp.float32))


def sub_mod(mod, s):
    return mod[:, :, s, 0], mod[:, :, s, 1], mod[:, :, s, 2]


def ffn_sublayer(h, mod3, g_pre, g_post, w_i, w_o):
    shift, scale, gate = mod3
    y = rms_norm(h, g_pre) * (1 + scale) + shift
    a, u = jnp.split(y @ w_i, 2, axis=-1)
    y = (jax.nn.silu(a) * u) @ w_o
    return h + 0.5 * gate * rms_norm(y, g_post)


def merge_branches(outs, gate_pre, b_gate, w_branch, w_out):
    g = jax.nn.sigmoid((gate_pre + b_gate).reshape(*gate_pre.shape[:-1], N_BRANCH, D_MODEL))
    merged = sum(g[..., i, :] * (o @ w_branch[i]) for i, o in enumerate(outs))
    return merged @ w_out


def mixer_sublayer(h, hc, mod3, mod3_c, g_pre, g_post, w_in, b_gate, diff_lambda, diff_norm_g,
                   ret_logit, ret_norm_g, qk_norm_g, w_branch, w_out, lam_init, need_ctx):
    b, n, _ = h.shape
    shift, scale, gate = mod3
    cshift, cscale, cgate = mod3_c
    pl = project(rms_norm(h, g_pre) * (1 + scale) + shift, w_in, PROJ_NAMES)
    pc = project(rms_norm(hc, g_pre) * (1 + cscale) + cshift, w_in, PROJ_NAMES if need_ctx else KV_NAMES)
    cos, sin = axial_rope(n, ROPE_DIM)
    rope = lambda t: apply_rope(t, cos, sin)
    cat = lambda t_lat, t_ctx: jnp.concatenate([t_lat, t_ctx], axis=-2)
    flip = lambda t: t[:, :, ::-1]

    lq1, lk1, lq2, lk2 = diff_lambda.astype(jnp.float32)
    lam = jnp.exp(jnp.sum(lq1 * lk1)) - jnp.exp(jnp.sum(lq2 * lk2)) + lam_init
    ka_c, va_c = diff_heads(pc['a_k']), heads(pc['a_v'], DA_HEADS, DA_DV)
    diff_out = lambda o: merge_heads(rms_norm(o, diff_norm_g) * (1.0 - lam_init))
    oa = diff_out(diff_attention(rope(diff_heads(pl['a_q'])),
                                 cat(rope(diff_heads(pl['a_k'])), ka_c),
                                 cat(heads(pl['a_v'], DA_HEADS, DA_DV), va_c), lam))

    lg = jax.nn.log_sigmoid(ret_logit.astype(jnp.float32))
    kb_c, vb_c = heads(pc['b_k'], RET_HEADS, RET_DK), heads(pc['b_v'], RET_HEADS, RET_DV)
    if need_ctx:
        zero = jnp.zeros((b, RET_HEADS, RET_DK, RET_DV), jnp.float32)
        qb_c = heads(pc['b_q'], RET_HEADS, RET_DK) * RET_DK ** -0.5
        o_cf, s_f = retention_chunks(qb_c, kb_c, vb_c, lg[0], zero)
        o_cb, s_b = retention_chunks(flip(qb_c), flip(kb_c), flip(vb_c), lg[1], zero)
    else:
        s_f = retention_state(kb_c, vb_c, lg[0])
        s_b = retention_state(flip(kb_c), flip(vb_c), lg[1])
    qb = rope(heads(pl['b_q'], RET_HEADS, RET_DK)) * RET_DK ** -0.5
    kb = rope(heads(pl['b_k'], RET_HEADS, RET_DK))
    vb = heads(pl['b_v'], RET_HEADS, RET_DV)
    o_f, _ = retention_chunks(qb, kb, vb, lg[0], s_f)
    o_b, _ = retention_chunks(flip(qb), flip(kb), flip(vb), lg[1], s_b)
    ret_out = lambda o, g: merge_heads(rms_norm(o.astype(h.dtype), ret_norm_g)) * jax.nn.silu(g)
    ob = ret_out(o_f + flip(o_b), pl['b_g'])

    kc_c = rms_norm(heads(pc['c_k'], GQA_KV, GQA_DH), qk_norm_g[1])
    vc_c = heads(pc['c_v'], GQA_KV, GQA_DH)
    def gqa_out(o):
        bb, gg, rr, nn, dd = o.shape
        return o.transpose(0, 3, 1, 2, 4).reshape(bb, nn, gg * rr * dd)
    oc = gqa_out(gqa_attention(rope(rms_norm(gqa_q_heads(pl['c_q']), qk_norm_g[0])),
                               cat(rope(rms_norm(heads(pl['c_k'], GQA_KV, GQA_DH), qk_norm_g[1])), kc_c),
                               cat(heads(pl['c_v'], GQA_KV, GQA_DH), vc_c)))

    out = merge_branches((oa, ob, oc), pl['gate'], b_gate, w_branch, w_out)
    h = h + gate * rms_norm(out, g_post)
    if not need_ctx:
        return h, None

    oa_c = diff_out(diff_attention(diff_heads(pc['a_q']), ka_c, va_c, lam))
    ob_c = ret_out(o_cf + flip(o_cb), pc['b_g'])
    oc_c = gqa_out(gqa_attention(rms_norm(gqa_q_heads(pc['c_q']), qk_norm_g[0]), kc_c, vc_c))
    out_c = merge_branches((oa_c, ob_c, oc_c), pc['gate'], b_gate, w_branch, w_out)
    hc = hc + cgate * rms_norm(out_c, g_post)
    return h, hc


def setup_inputs(seed: int = 0) -> dict:
    key = jax.random.key(seed)
    ks = jax.random.split(key, 18)
    f32 = jnp.float32
    nrm = lambda k, shape, s: jax.random.normal(k, shape, f32) * s
    gamma0 = 1.0 - 2.0 ** (-5.0 - jnp.arange(RET_HEADS, dtype=f32))
    return {
        'x': nrm(ks[0], (BATCH, SEQ, D_MODEL), 1.0),
        'c': nrm(ks[1], (BATCH, D_MODEL), 1.0),
        'ctx': nrm(ks[2], (BATCH, CTX_LEN, D_MODEL), 1.0),
        'c_ctx': nrm(ks[3], (D_MODEL,), 1.0),
        'w_mod': nrm(ks[4], (DEPTH, D_MODEL, 3 * N_SUB * D_MODEL), 0.5 * D_MODEL ** -0.5),
        'b_mod': nrm(ks[5], (DEPTH, 3 * N_SUB * D_MODEL), 0.02),
        'norm_g': 1.0 + nrm(ks[6], (DEPTH, 2 * N_SUB, D_MODEL), 0.05),
        'w_ffn_in': nrm(ks[7], (DEPTH, 2, D_MODEL, 2 * D_FF), D_MODEL ** -0.5),
        'w_ffn_out': nrm(ks[8], (DEPTH, 2, D_FF, D_MODEL), D_FF ** -0.5),
        'w_in': nrm(ks[9], (DEPTH, D_MODEL, IN_COLS), D_MODEL ** -0.5),
        'b_gate': nrm(ks[10], (DEPTH, N_BRANCH * D_MODEL), 0.02),
        'diff_lambda': nrm(ks[11], (DEPTH, 4, DA_DH), 0.1),
        'diff_norm_g': 1.0 + nrm(ks[12], (DEPTH, DA_DV), 0.05),
        'ret_decay_logit': jnp.log(gamma0 / (1.0 - gamma0))[None, None, :] + nrm(ks[13], (DEPTH, 2, RET_HEADS), 0.1),
        'ret_norm_g': 1.0 + nrm(ks[14], (DEPTH, RET_DV), 0.05),
        'qk_norm_g': 1.0 + nrm(ks[15], (DEPTH, 2, GQA_DH), 0.05),
        'w_branch': nrm(ks[16], (DEPTH, N_BRANCH, BRANCH_W, D_MODEL), BRANCH_W ** -0.5),
        'w_out': nrm(ks[17], (DEPTH, D_MODEL, D_MODEL), D_MODEL ** -0.5),
    }


def reference(x, c, ctx, c_ctx, w_mod, b_mod, norm_g, w_ffn_in, w_ffn_out, w_in, b_gate,
              diff_lambda, diff_norm_g, ret_decay_logit, ret_norm_g, qk_norm_g, w_branch, w_out):
    h, hc = x, ctx
    for l in range(DEPTH):
        last = l == DEPTH - 1
        mod = (jax.nn.silu(c) @ w_mod[l] + b_mod[l]).reshape(c.shape[0], 1, N_SUB, 3, D_MODEL)
        mod_c = (jax.nn.silu(c_ctx) @ w_mod[l] + b_mod[l]).reshape(1, 1, N_SUB, 3, D_MODEL)
        lam_init = 0.8 - 0.6 * math.exp(-0.3 * l)
        h = ffn_sublayer(h, sub_mod(mod, 0), norm_g[l, 0], norm_g[l, 1], w_ffn_in[l, 0], w_ffn_out[l, 0])
        hc = ffn_sublayer(hc, sub_mod(mod_c, 0), norm_g[l, 0], norm_g[l, 1], w_ffn_in[l, 0], w_ffn_out[l, 0])
        h, hc = mixer_sublayer(h, hc, sub_mod(mod, 1), sub_mod(mod_c, 1), norm_g[l, 2], norm_g[l, 3],
                               w_in[l], b_gate[l], diff_lambda[l], diff_norm_g[l], ret_decay_logit[l],
                               ret_norm_g[l], qk_norm_g[l], w_branch[l], w_out[l], lam_init, not last)
        h = ffn_sublayer(h, sub_mod(mod, 2), norm_g[l, 4], norm_g[l, 5], w_ffn_in[l, 1], w_ffn_out[l, 1])
        if not last:
            hc = ffn_sublayer(hc, sub_mod(mod_c, 2), norm_g[l, 4], norm_g[l, 5], w_ffn_in[l, 1], w_ffn_out[l, 1])
    return h
```

```python
import math
from contextlib import ExitStack

import numpy as np
import concourse.bass as bass
import concourse.mybir as mybir
from concourse.bass_utils import run_bass_kernel_spmd

F32 = mybir.dt.float32
BF16 = mybir.dt.bfloat16
ALU = mybir.AluOpType
AF = mybir.ActivationFunctionType
AX = mybir.AxisListType

D = 1024
SEQ = 2048
CTX = 256
NTOK = SEQ + CTX
DEPTH = 2
DFF = 2816
NJ = DFF // 128
EPS = 1e-6
N_CORES = 8
BPC = 2

ENGS = ("pe", "act", "dve", "pool", "sp")
N_DMA_SEMS = 24

TILES = [(0, 512, False), (512, 512, False), (1024, 512, False), (1536, 512, False), (2048, 256, True)]


class Buf:
    def __init__(self, name, ap_fn):
        self.name = name
        self.ap_fn = ap_fn
        self.st = {}

    def ap(self):
        return self.ap_fn()

    def __getitem__(self, idx):
        return V(self, idx)

    def k(self, key):
        return V(self, None, key)


class V:
    def __init__(self, buf, idx=None, key=None, post=None):
        self.buf = buf
        self.idx = idx
        self.key = key
        self.post = post

    def k(self, key):
        return V(self.buf, self.idx, key, self.post)

    def f(self, post):
        return V(self.buf, self.idx, self.key, post)

    def bc(self, axis, shape):
        return V(self.buf, self.idx, self.key, lambda a: a.unsqueeze(axis).to_broadcast(list(shape)))

    def ap(self):
        a = self.buf.ap()
        if self.idx is not None:
            a = a[self.idx]
        if self.post is not None:
            a = self.post(a)
        return a

    def dep(self):
        return (self.buf, self.key)


def _dep(x):
    if isinstance(x, Buf):
        return (x, None)
    if isinstance(x, V):
        return x.dep()
    return x


def _ap(x):
    return x.ap() if isinstance(x, (Buf, V)) else x


class Op:
    __slots__ = ("eng", "fn", "waits", "idx", "signal", "dma", "tok")


class Prog:
    def __init__(self, nc):
        self.nc = nc
        self.ops = {e: [] for e in ENGS}
        self.dma_rr = 0
        self.dma_cnt = [0] * N_DMA_SEMS
        self.dma_last = [None] * N_DMA_SEMS
        self.seen = {e: {} for e in ENGS}

    def _need(self, eng, tok, waits):
        if tok is None:
            return
        if tok[0] == 'e':
            _, e2, idx = tok
            if e2 == eng and e2 == 'pe':
                return
            key = ('e', e2)
            if self.seen[eng].get(key, -1) >= idx:
                return
            self.seen[eng][key] = idx
            waits.append(tok)
        else:
            _, k, val = tok
            key = ('d', k)
            if self.seen[eng].get(key, -1) >= val:
                return
            self.seen[eng][key] = val
            waits.append(tok)

    @staticmethod
    def _state(buf, key):
        st = buf.st.get(key)
        if st is None:
            st = {'w': None, 'r': []}
            buf.st[key] = st
        return st

    def op(self, eng, fn, reads=(), writes=(), dma=False):
        o = Op()
        o.eng = eng
        o.fn = fn
        o.idx = len(self.ops[eng])
        o.signal = False
        o.dma = dma
        waits = []
        rd = [_dep(x) for x in reads]
        wr = [_dep(x) for x in writes]
        for b, _k in wr:
            for t in getattr(b, 'alias_toks', ()):
                self._need(eng, t, waits)
        if dma:
            k = self.dma_rr
            self.dma_rr = (self.dma_rr + 1) % N_DMA_SEMS
            self._need(eng, self.dma_last[k], waits)
            self.dma_cnt[k] += 16
            tok = ('d', k, self.dma_cnt[k])
            self.dma_last[k] = tok
        else:
            tok = ('e', eng, o.idx)
        o.tok = tok
        for b, key in rd:
            self._need(eng, self._state(b, key)['w'], waits)
        for b, key in wr:
            st = self._state(b, key)
            self._need(eng, st['w'], waits)
            for r in st['r']:
                self._need(eng, r, waits)
        for b, key in rd:
            rl = self._state(b, key)['r']
            if tok[0] == 'e':
                rl[:] = [r for r in rl if not (r[0] == 'e' and r[1] == tok[1])]
            rl.append(tok)
        for b, key in wr:
            st = self._state(b, key)
            st['w'] = tok
            st['r'] = []
        o.waits = waits
        for w in waits:
            if w[0] == 'e':
                self.ops[w[1]][w[2]].signal = True
        self.ops[eng].append(o)
        return o

    def inherit(self, newbuf, oldbufs):
        best = {}
        def add(t):
            if t is None:
                return
            key = (t[0], t[1])
            if key not in best or best[key][2] < t[2]:
                best[key] = t
        for t in getattr(newbuf, 'alias_toks', ()):
            add(t)
        for ob in oldbufs:
            for t in getattr(ob, 'alias_toks', ()):
                add(t)
            for s2 in ob.st.values():
                add(s2['w'])
                for r in s2['r']:
                    add(r)
        newbuf.alias_toks = list(best.values())

    def final_wait_all(self, eng='sp'):
        waits = []
        for k in range(N_DMA_SEMS):
            self._need(eng, self.dma_last[k], waits)
        for e in ENGS:
            if e != eng:
                nd = [o for o in self.ops[e] if not o.dma and o.fn is not None]
                if nd:
                    self._need(eng, ('e', e, nd[-1].idx), waits)
        o = Op()
        o.eng = eng; o.fn = None; o.idx = len(self.ops[eng]); o.signal = False
        o.dma = False; o.tok = ('e', eng, o.idx); o.waits = waits
        for w in waits:
            if w[0] == 'e':
                self.ops[w[1]][w[2]].signal = True
        self.ops[eng].append(o)

    def emit(self, stack):
        nc = self.nc
        esem = {e: stack.enter_context(nc.semaphore("s_" + e)) for e in ENGS}
        dsem = [stack.enter_context(nc.semaphore("d%d" % k)) for k in range(N_DMA_SEMS)]
        sigval = {}
        for e in ENGS:
            c = 0
            for o in self.ops[e]:
                if o.signal:
                    c += 1
                    sigval[(e, o.idx)] = c
        block = stack.enter_context(nc.Block())

        def run(engname, engobj):
            for o in self.ops[engname]:
                for w in o.waits:
                    if w[0] == 'e':
                        engobj.wait_ge(esem[w[1]], sigval[(w[1], w[2])])
                    else:
                        engobj.wait_ge(dsem[w[1]], w[2])
                if o.fn is None:
                    continue
                ins = o.fn(engobj)
                if o.dma:
                    ins.then_inc(dsem[o.tok[1]], 16)
                elif o.signal:
                    ins.then_inc(esem[engname], 1)

        @block.tensor
        def _(e):
            run("pe", e)

        @block.scalar
        def _(e):
            run("act", e)

        @block.vector
        def _(e):
            run("dve", e)

        @block.gpsimd
        def _(e):
            run("pool", e)

        @block.sync
        def _(e):
            run("sp", e)


class Arena:
    def __init__(self, P, arena_t, nbytes):
        self.P = P
        self.t = arena_t
        self.n = nbytes
        self.live = []
        self.dead = []
        self.peak = 0

    def alloc(self, name, shape, dt, parts=128):
        esz = 2 if dt == BF16 else 4
        nbytes = int(np.prod(shape)) * esz
        size = (nbytes + 63) // 64 * 64
        self.live.sort(key=lambda x: x[0])
        pos = 0
        for (s, e, _) in self.live:
            if s - pos >= size:
                break
            pos = max(pos, e)
        if pos + size > self.n:
            raise MemoryError("arena full allocating %s (%d bytes); live=%s" % (
                name, size, [(b.name, e - s) for s, e, b in self.live]))
        t = self.t
        shape = tuple(shape)

        def apf(pos=pos, nbytes=nbytes):
            a = t[0:parts, pos // 4:(pos + nbytes) // 4]
            if dt != F32:
                a = a.bitcast(dt)
            if len(shape) == 2:
                a = a.rearrange("p (a b) -> p a b", a=shape[0])
            elif len(shape) == 3:
                a = a.rearrange("p (a b c) -> p a b c", a=shape[0], b=shape[1])
            return a

        buf = Buf(name, apf)
        olds = [b for (s, e, b) in self.dead if s < pos + size and e > pos]
        if olds:
            self.P.inherit(buf, olds)
        self.dead = [(s, e, b) for (s, e, b) in self.dead if not (s >= pos and e <= pos + size)]
        self.live.append((pos, pos + size, buf))
        self.peak = max(self.peak, pos + size)
        return buf

    def free(self, *bufs):
        for buf in bufs:
            for i, (s, e, b) in enumerate(self.live):
                if b is buf:
                    self.dead.append(self.live.pop(i))
                    break
            else:
                raise KeyError(buf.name)


class Builder:
    def __init__(self, nsub=6 * DEPTH):
        self.nsub = nsub
        self.nc = bass.Bass("TRN2", target_bir_lowering=False)
        self.P = Prog(self.nc)
        self.stack = ExitStack()

    def op(self, eng, fn, reads=(), writes=(), dma=False):
        return self.P.op(eng, fn, reads=reads, writes=writes, dma=dma)

    def mm(self, out, lhsT, rhs, start=True, stop=True):
        self.op("pe", lambda e: e.matmul(_ap(out), lhsT=_ap(lhsT), rhs=_ap(rhs), start=start, stop=stop),
                reads=[lhsT, rhs], writes=[out])

    def act(self, out, in_, func, bias=None, scale=1.0, extra_reads=()):
        def fn(e):
            kw = {}
            if bias is not None:
                kw['bias'] = _ap(bias)
            return e.activation(out=_ap(out), in_=_ap(in_), func=func, scale=_ap(scale), **kw)
        rd = [in_] + [x for x in (bias, scale) if isinstance(x, (Buf, V))] + list(extra_reads)
        self.op("act", fn, reads=rd, writes=[out])

    def tt(self, eng, out, in0, in1, op):
        self.op(eng, lambda e: e.tensor_tensor(out=_ap(out), in0=_ap(in0), in1=_ap(in1), op=op),
                reads=[in0, in1], writes=[out])

    def stt(self, eng, out, in0, scalar, in1, op0, op1):
        rd = [in0, in1] + ([scalar] if isinstance(scalar, (Buf, V)) else [])
        self.op(eng, lambda e: e.scalar_tensor_tensor(out=_ap(out), in0=_ap(in0), scalar=_ap(scalar),
                                                      in1=_ap(in1), op0=op0, op1=op1),
                reads=rd, writes=[out])

    def ts(self, eng, out, in0, s1, op0, s2=None, op1=None):
        rd = [in0] + [x for x in (s1, s2) if isinstance(x, (Buf, V))]
        def fn(e):
            if op1 is None:
                return e.tensor_scalar(out=_ap(out), in0=_ap(in0), scalar1=_ap(s1), scalar2=None, op0=op0)
            return e.tensor_scalar(out=_ap(out), in0=_ap(in0), scalar1=_ap(s1), scalar2=_ap(s2), op0=op0, op1=op1)
        self.op(eng, fn, reads=rd, writes=[out])

    def copy(self, eng, out, in_):
        if eng == "act":
            self.op("act", lambda e: e.copy(out=_ap(out), in_=_ap(in_)), reads=[in_], writes=[out])
        else:
            self.op(eng, lambda e: e.tensor_copy(out=_ap(out), in_=_ap(in_)), reads=[in_], writes=[out])

    def recip(self, out, in_):
        self.op("dve", lambda e: e.reciprocal(out=_ap(out), in_=_ap(in_)), reads=[in_], writes=[out])

    def memset(self, eng, out, val):
        self.op(eng, lambda e: e.memset(_ap(out), val), writes=[out])

    def dma(self, eng, out, in_, reads=(), writes=()):
        self.op(eng, lambda e: e.dma_start(out=_ap(out), in_=_ap(in_)),
                reads=list(reads), writes=list(writes), dma=True)

    def bank(self):
        return self.free_banks.pop(0)

    def rel(self, *banks):
        for b in banks:
            assert b not in self.free_banks
            self.free_banks.append(b)

    def rstd(self, RS, ps):
        self.act(RS, ps, AF.Sqrt, bias=self.EPSB[:, 0:1])
        self.recip(RS, RS)

    def build(self):
        nc = self.nc
        dt = lambda name, shape, kind=None, dty=F32: (
            nc.dram_tensor(name, list(shape), dty, kind=kind) if kind else nc.dram_tensor(name, list(shape), dty))
        I = {}
        def inp(name, shape):
            I[name] = dt(name, shape, "ExternalInput").ap()
        inp("xT", (BPC, D, SEQ)); inp("ctxT", (BPC, D, CTX)); inp("cT", (128, 8, 3))
        inp("wmod", (DEPTH, D, 9 * D)); inp("bmodT", (DEPTH, 128, 72)); inp("normgT", (DEPTH, 128, 6, 8))
        inp("wi", (DEPTH, 2, NJ // 2, 128, 2 * 8 * 256)); inp("wo", (DEPTH, 2, 4, 128, 2 * NJ * 128))
        inp("wA", (DEPTH, 4, 128, 8 * 640)); inp("wB", (DEPTH, 4, 128, 8 * 512))
        inp("wCk", (DEPTH, 128, 8 * 640)); inp("wCq", (DEPTH, 4, 128, 8 * 256))
        inp("wG", (DEPTH, 3, 2, 128, 8 * 512)); inp("wBr", (DEPTH, 3, 128, 4 * 1024)); inp("wO", (DEPTH, 2, 128, 8 * 512))
        inp("bgT", (DEPTH, 128, 24)); inp("lamb", (DEPTH, 128, 256)); inp("hg", (DEPTH, 128, 6))
        inp("retl", (DEPTH, 128, 8))
        inp("cosF", (128, SEQ)); inp("sinF", (128, SEQ)); inp("cosT", (128, 16 * 64)); inp("sinT", (128, 16 * 64))
        inp("rel", (128, 128)); inp("posf", (128, 128)); inp("posp", (128, 1))
        self.I = I
        self.outT = dt("outT", (BPC, D, SEQ), "ExternalOutput").ap()
        Wb = {}
        for nme in ("wi", "wo", "wA", "wB", "wCk", "wCq", "wG", "wBr", "wO"):
            Wb[nme] = dt(nme + "_bf", I[nme].shape, None, BF16).ap()
        self.Wb = Wb
        self.wbuf = {}
        self.hS = dt("hS", (BPC, D, NTOK), None, F32).ap()
        self.hSbuf = Buf("hS", lambda: self.hS)

        stack = self.stack
        ARENA_BYTES = 206 * 1024
        arena_t = stack.enter_context(nc.sbuf_tensor("arena", [128, ARENA_BYTES // 4], F32))
        self.A = Arena(self.P, arena_t, ARENA_BYTES)
        pst = [stack.enter_context(nc.psum_tensor("ps%d" % i, [128, 512], F32)) for i in range(8)]
        self.PS = [Buf("ps%d" % i, (lambda t=t: t[:, :])) for i, t in enumerate(pst)]
        self.free_banks = list(self.PS)

        self.precast()
        self.consts()
        self.modulation()
        isub = 0
        for b in range(BPC):
            for l in range(DEPTH):
                for s in range(3):
                    if isub >= self.nsub * BPC and False:
                        pass
                    if self.sub_count(b, l, s) >= self.nsub:
                        continue
                    if s == 1:
                        self.mixer(b, l)
                    else:
                        self.ffn(b, l, s)
        self.P.final_wait_all("sp")
        self.P.emit(stack)

    def sub_count(self, b, l, s):
        return l * 3 + s

    def h_src(self, b, l, s, tile):
        t0, nt, isctx = tile
        first = (l == 0 and s == 0)
        if first:
            if isctx:
                return self.I["ctxT"][b].rearrange("(k p) n -> p k n", p=128), []
            return self.I["xT"][b][:, t0:t0 + nt].rearrange("(k p) n -> p k n", p=128), []
        return self.hS[b][:, t0:t0 + nt].rearrange("(k p) n -> p k n", p=128), [self.hSbuf.k((b, t0))]

    def h_dst(self, b, l, s, tile):
        t0, nt, isctx = tile
        last = (l * 3 + s == self.nsub - 1)
        if last and not isctx:
            return self.outT[b][:, t0:t0 + nt].rearrange("(k p) n -> p k n", p=128), []
        return self.hS[b][:, t0:t0 + nt].rearrange("(k p) n -> p k n", p=128), [self.hSbuf.k((b, t0))]

    def precast(self):
        I, Wb = self.I, self.Wb
        def pc(name, idx):
            key = (name,) + tuple(idx)
            buf = Buf("wb_%s_%s" % (name, idx), None)
            self.wbuf[key] = buf
            src = I[name]; dst = Wb[name]
            for i in idx:
                src = src[i]; dst = dst[i]
            self.dma("pool", dst, src, writes=[buf])
        for l in range(DEPTH):
            for jg in range(NJ // 2):
                pc("wi", (l, 0, jg))
            for dg in range(4):
                pc("wo", (l, 0, dg))
            for h in range(4):
                pc("wA", (l, h))
            for h in range(4):
                pc("wB", (l, h))
            pc("wCk", (l,))
            for q in range(4):
                pc("wCq", (l, q))
            for i in range(3):
                for hf in range(2):
                    pc("wG", (l, i, hf))
                pc("wBr", (l, i))
            for hf in range(2):
                pc("wO", (l, hf))
            for jg in range(NJ // 2):
                pc("wi", (l, 1, jg))
            for dg in range(4):
                pc("wo", (l, 1, dg))

    def wload(self, dst, name, idx):
        src = self.Wb[name]
        for i in idx:
            src = src[i]
        d = _ap
        def fn(e, src=src):
            o = _ap(dst)
            nd = len(o.shape)
            if nd == 3:
                o2 = o.rearrange("p a b -> p (a b)")
            elif nd == 4:
                o2 = o.rearrange("p a b c -> p (a b c)")
            else:
                o2 = o
            return e.dma_start(out=o2, in_=src)
        self.op("sp", fn, reads=[self.wbuf[(name,) + tuple(idx)]], writes=[dst], dma=True)

    def consts(self):
        A, I = self.A, self.I
        al = A.alloc
        self.EPSB = al("EPSB", (1,), F32)
        self.memset("dve", self.EPSB, EPS)
        self.LN8 = al("LN8", (1,), F32)
        self.memset("dve", self.LN8, math.log(0.125))
        self.ONESM = al("ONESM", (128,), BF16)
        self.memset("dve", self.ONESM, 1.0 / 1024)
        self.ONES128 = al("ONES128", (128,), BF16)
        self.memset("dve", self.ONES128, 1.0 / 128)
        self.ONES1 = al("ONES1", (128,), BF16)
        self.memset("dve", self.ONES1, 1.0)
        self.ONESL = al("ONESL", (128,), BF16)
        self.ONESR = al("ONESR", (128,), BF16)
        self.memset("dve", self.ONESL, 0.0)
        self.memset("dve", self.ONESR, 0.0)
        self.memset("dve", self.ONESL[:, 0:64], 1.0)
        self.memset("dve", self.ONESR[:, 64:128], 1.0)
        self.BD64 = al("BD64", (128,), BF16)
        self.memset("dve", self.BD64, 0.0)
        self.memset("dve", self.BD64[0:64, 0:64], 1.0 / 64)
        self.memset("dve", self.BD64[64:128, 64:128], 1.0 / 64)
        def ld(name, shape, src):
            b = al(name, shape, F32)
            self.dma("sp", b, src, writes=[b])
            return b
        self.CT = ld("CT", (8, 3), I["cT"])
        self.BMOD = [ld("BMOD%d" % l, (72,), I["bmodT"][l]) for l in range(DEPTH)]
        self.NG = [ld("NG%d" % l, (6, 8), I["normgT"][l]) for l in range(DEPTH)]
        self.BG = [ld("BG%d" % l, (24,), I["bgT"][l]) for l in range(DEPTH)]
        self.HG = [ld("HG%d" % l, (6,), I["hg"][l]) for l in range(DEPTH)]
        self.layer_consts()

    def layer_consts(self):
        A, I = self.A, self.I
        al = A.alloc
        LAMB = [al("LAMB%d" % l, (256,), F32) for l in range(DEPTH)]
        RETL = [al("RETL%d" % l, (8,), F32) for l in range(DEPTH)]
        REL = al("REL", (128,), F32); POSF = al("POSF", (128,), F32); POSP = al("POSP", (1,), F32)
        for l in range(DEPTH):
            self.dma("sp", LAMB[l], I["lamb"][l], writes=[LAMB[l]])
            self.dma("sp", RETL[l], I["retl"][l], writes=[RETL[l]])
        self.dma("sp", REL, I["rel"], writes=[REL])
        self.dma("sp", POSF, I["posf"], writes=[POSF])
        self.dma("sp", POSP, I["posp"], writes=[POSP])
        T1 = al("lcT1", (128,), F32); T2 = al("lcT2", (128,), F32); T3 = al("lcT3", (128,), F32)
        RP = al("RELP", (128,), F32); RN = al("RELN", (128,), F32)
        GE = al("RELGE", (128,), F32); LE = al("RELLE", (128,), F32)
        self.ts("dve", RP, REL, 0.0, ALU.max)
        self.ts("dve", RN, REL, -1.0, ALU.mult, 0.0, ALU.max)
        self.ts("dve", GE, REL, 0.0, ALU.is_ge)
        self.ts("dve", LE, REL, 0.0, ALU.is_le)
        P1 = al("POSF1", (128,), F32)
        P2 = al("POS128", (128,), F32)
        self.ts("dve", P1, POSF, 1.0, ALU.add)
        self.ts("dve", P2, POSF, -1.0, ALU.mult, 128.0, ALU.add)
        PP1 = al("POSP127", (1,), F32)
        self.ts("dve", PP1, POSP, -1.0, ALU.mult, 127.0, ALU.add)
        self.NEGLAM, self.DNG, self.LG = [], [], []
        self.MASK4, self.QDF4, self.QDB4, self.KDF, self.KDB, self.SDEC = {}, {}, {}, {}, {}, {}
        for l in range(DEPTH):
            lam_init = 0.8 - 0.6 * math.exp(-0.3 * l)
            PR = al("lamPR", (2, 64), F32); SM = al("lamSM", (2,), F32)
            self.tt("dve", PR[:, 0, :], LAMB[l][:, 0:64], LAMB[l][:, 64:128], ALU.mult)
            self.tt("dve", PR[:, 1, :], LAMB[l][:, 128:192], LAMB[l][:, 192:256], ALU.mult)
            self.op("dve", lambda e, SM=SM, PR=PR: e.reduce_sum(out=SM.ap(), in_=PR.ap(), axis=AX.X),
                    reads=[PR], writes=[SM])
            self.act(SM, SM, AF.Exp)
            NL = al("NEGLAM%d" % l, (1,), F32)
            self.tt("dve", NL, SM[:, 1:2], SM[:, 0:1], ALU.subtract)
            self.ts("dve", NL, NL, -lam_init, ALU.add)
            self.NEGLAM.append(NL)
            DG = al("DNG%d" % l, (1,), F32)
            self.ts("dve", DG, self.HG[l][:, 0:1], 1.0 - lam_init, ALU.mult)
            self.DNG.append(DG)
            A.free(PR, SM)
            LGt = al("LG%d" % l, (8,), F32)
            self.act(LGt, RETL[l], AF.Exp, scale=-1.0)
            self.ts("dve", LGt, LGt, 1.0, ALU.add)
            self.act(LGt, LGt, AF.Ln)
            self.ts("dve", LGt, LGt, -1.0, ALU.mult)
            self.LG.append(LGt)
            for h in range(4):
                lgf = LGt[:, h:h + 1]; lgb = LGt[:, 4 + h:5 + h]
                M4 = al("MASK_%d_%d" % (l, h), (128,), F32)
                self.act(T1, RP, AF.Exp, scale=lgf)
                self.tt("dve", T1, T1, GE, ALU.mult)
                self.act(T2, RN, AF.Exp, scale=lgb)
                self.tt("dve", T2, T2, LE, ALU.mult)
                self.tt("dve", T3, T1, T2, ALU.add)
                self.ts("dve", M4, T3, 0.125, ALU.mult)
                self.MASK4[(l, h)] = M4
                QF = al("QDF_%d_%d" % (l, h), (128,), F32, parts=64)
                QB = al("QDB_%d_%d" % (l, h), (128,), F32, parts=64)
                self.act(QF, P1[0:64, :], AF.Exp, bias=self.LN8[0:64, 0:1], scale=LGt[0:64, h:h + 1])
                self.act(QB, P2[0:64, :], AF.Exp, bias=self.LN8[0:64, 0:1], scale=LGt[0:64, 4 + h:5 + h])
                self.QDF4[(l, h)] = QF; self.QDB4[(l, h)] = QB
                KF = al("KDF_%d_%d" % (l, h), (1,), F32); KB = al("KDB_%d_%d" % (l, h), (1,), F32)
                self.act(KF, PP1, AF.Exp, scale=lgf)
                self.act(KB, POSP, AF.Exp, scale=lgb)
                self.KDF[(l, h)] = KF; self.KDB[(l, h)] = KB
                SD = al("SDEC_%d_%d" % (l, h), (2,), F32)
                self.act(SD[:, 0:1], lgf, AF.Exp, scale=128.0)
                self.act(SD[:, 1:2], lgb, AF.Exp, scale=128.0)
                self.SDEC[(l, h)] = SD
        A.free(T1, T2, T3, RP, RN, GE, LE, P1, P2, PP1, REL, POSF, POSP)
        A.free(*LAMB); A.free(*RETL)

    def modulation(self):
        A, I = self.A, self.I
        al = A.alloc
        SC = al("SC", (8, 3), BF16)
        self.act(SC, self.CT, AF.Silu)
        WM = [al("WM%d" % i, (8, 512), BF16) for i in range(3)]
        self.MOD = []
        self.AA, self.GG = {}, {}
        for l in range(DEPTH):
            MOD = al("MOD%d" % l, (72, 3), F32)
            ps = self.bank()
            for blk in range(18):
                wm = WM[(l * 18 + blk) % 3]
                src = I["wmod"][l][:, blk * 512:(blk + 1) * 512].rearrange("(k p) n -> p k n", p=128)
                self.dma("pool", wm, src, writes=[wm])
                for n in range(4):
                    c = blk * 4 + n
                    for k in range(8):
                        self.mm(ps[:, c * 3:c * 3 + 3], wm[:, k, n * 128:(n + 1) * 128], SC[:, k, :],
                                start=(k == 0), stop=(k == 7))
            for j in range(3):
                self.op("dve", lambda e, MOD=MOD, ps=ps, l=l, j=j: e.tensor_tensor(
                    out=MOD.ap()[:, :, j],
                    in0=ps.ap()[:, 0:216].rearrange("p (c j) -> p c j", j=3)[:, :, j],
                    in1=self.BMOD[l].ap(), op=ALU.add), reads=[ps, self.BMOD[l]], writes=[MOD])
            self.rel(ps)
            self.MOD.append(MOD)
            for s in range(3):
                AAt = al("AA%d_%d" % (l, s), (3, 8), F32)
                GGt = al("GG%d_%d" % (l, s), (3, 8), F32)
                cs = 0.5 if s != 1 else 1.0
                for j in range(3):
                    sc = MOD[:, (s * 3 + 1) * 8:(s * 3 + 1) * 8 + 8, j]
                    gt = MOD[:, (s * 3 + 2) * 8:(s * 3 + 2) * 8 + 8, j]
                    self.stt("dve", AAt[:, j, :], sc, 1.0, self.NG[l][:, 2 * s, :], ALU.add, ALU.mult)
                    self.stt("dve", GGt[:, j, :], gt, cs, self.NG[l][:, 2 * s + 1, :], ALU.mult, ALU.mult)
                self.AA[(l, s)] = AAt; self.GG[(l, s)] = GGt
        A.free(SC, *WM)

    def shift_ap(self, l, s, j, k):
        c = (s * 3 + 0) * 8 + k
        return self.MOD[l][:, c, j:j + 1]

    def prenorm(self, HT, Yv, nt, l, s, j, SQ, RS, TMP):
        self.tt("pool", SQ[:, :, 0:nt], HT[:, :, 0:nt], HT[:, :, 0:nt], ALU.mult)
        ps = self.bank()
        for k in range(8):
            self.mm(ps[:, 0:nt], self.ONESM, SQ[:, k, 0:nt], start=(k == 0), stop=(k == 7))
        self.rstd(RS[:, 0:nt], ps[:, 0:nt])
        self.rel(ps)
        AAt = self.AA[(l, s)]
        for k in range(8):
            tmp = TMP[k % 2]
            self.stt("dve", tmp[:, 0:nt], HT[:, k, 0:nt], AAt[:, j, k:k + 1], RS[:, 0:nt],
                     ALU.mult, ALU.mult)
            self.act(Yv(k), tmp[:, 0:nt], AF.Identity, bias=self.shift_ap(l, s, j, k))

    def postnorm_residual(self, HT, Y2, SQ, RS, TMP, nt, l, s, j):
        self.tt("pool", SQ[:, :, 0:nt], Y2[:, :, 0:nt], Y2[:, :, 0:nt], ALU.mult)
        ps = self.bank()
        for k in range(8):
            self.mm(ps[:, 0:nt], self.ONESM, SQ[:, k, 0:nt], start=(k == 0), stop=(k == 7))
        self.rstd(RS[:, 0:nt], ps[:, 0:nt])
        self.rel(ps)
        GGt = self.GG[(l, s)]
        for k in range(8):
            tmp = TMP[k % 2]
            self.stt("dve", tmp[:, 0:nt], Y2[:, k, 0:nt], GGt[:, j, k:k + 1], RS[:, 0:nt], ALU.mult, ALU.mult)
            self.tt("pool", HT[:, k, 0:nt], HT[:, k, 0:nt], tmp[:, 0:nt], ALU.add)

    def ffn(self, b, l, s):
        A = self.A
        al = A.alloc
        sf = 0 if s == 0 else 1
        last_layer = (l == DEPTH - 1)
        tiles = [t for t in TILES if not (t[2] and last_layer and s == 2)]
        HT = [al("HT%d" % i, (8, 512), F32) for i in range(2)]
        Y = al("Yt", (8, 512), BF16)
        SQ = al("SQ", (8, 512), BF16)
        RS = al("RS", (512,), F32)
        TMP = [al("TMP%d" % i, (512,), F32) for i in range(2)]
        G = al("G", (NJ, 512), BF16)
        WI = [al("WI%d" % i, (2, 8, 256), BF16) for i in range(3)]
        WO = [al("WO%d" % i, (2, NJ, 128), BF16) for i in range(2)]
        SA = [al("SA%d" % i, (512,), F32) for i in range(2)]
        Y2 = al("Y2", (8, 512), F32)
        wi_rr = 0; wo_rr = 0
        for ti, tile in enumerate(tiles):
            t0, nt, isctx = tile
            j = 2 if isctx else b
            ht = HT[ti % 2]
            src, rd = self.h_src(b, l, s, tile)
            self.dma("sp", ht[:, :, 0:nt], src, reads=rd, writes=[ht])
            self.prenorm(ht, lambda k: Y[:, k, 0:nt], nt, l, s, j, SQ, RS, TMP)
            for jg in range(NJ // 2):
                wi = WI[wi_rr % 3]; wi_rr += 1
                self.wload(wi, "wi", (l, sf, jg))
                for jj in range(2):
                    jx = jg * 2 + jj
                    pa = self.bank(); pu = self.bank()
                    for k in range(8):
                        self.mm(pa[:, 0:nt], wi[:, jj, k, 0:128], Y[:, k, 0:nt], start=(k == 0), stop=(k == 7))
                    for k in range(8):
                        self.mm(pu[:, 0:nt], wi[:, jj, k, 128:256], Y[:, k, 0:nt], start=(k == 0), stop=(k == 7))
                    sa = SA[jx % 2]
                    self.act(sa[:, 0:nt], pa[:, 0:nt], AF.Silu)
                    self.tt("dve", G[:, jx, 0:nt], sa[:, 0:nt], pu[:, 0:nt], ALU.mult)
                    self.rel(pa, pu)
            for dg in range(4):
                wo = WO[wo_rr % 2]; wo_rr += 1
                self.wload(wo, "wo", (l, sf, dg))
                for dd in range(2):
                    dc = dg * 2 + dd
                    po = self.bank()
                    for jx in range(NJ):
                        self.mm(po[:, 0:nt], wo[:, dd, jx, :], G[:, jx, 0:nt], start=(jx == 0), stop=(jx == NJ - 1))
                    self.copy("act", Y2[:, dc, 0:nt], po[:, 0:nt])
                    self.rel(po)
            self.postnorm_residual(ht, Y2, SQ, RS, TMP, nt, l, s, j)
            dst, wr = self.h_dst(b, l, s, tile)
            self.dma("sp", dst, ht[:, :, 0:nt], reads=[ht], writes=wr)
        A.free(*HT, Y, SQ, RS, *TMP, G, *WI, *WO, *SA, Y2)

    def rope_fm(self, out, ps, ps_sw, rows, t0, nt, RT):
        r1, r2 = RT
        self.tt("dve", r1[rows, 0:nt], ps, self.COSF[rows, t0:t0 + nt], ALU.mult)
        self.tt("dve", r2[rows, 0:nt], ps_sw, self.SINF[rows, t0:t0 + nt], ALU.mult)
        self.tt("pool", out, r1[rows, 0:nt], r2[rows, 0:nt], ALU.add)

    def proj_fm(self, W, c0, ncols, Y, t0, nt, ti):
        ps = self.bank()
        for k in range(8):
            self.mm(ps[0:ncols, 0:nt], W[:, k, c0:c0 + ncols], Y[:, k, t0:t0 + nt].k(ti),
                    start=(k == 0), stop=(k == 7))
        return ps

    def proj_tm4(self, W, c0, ncols, Y, chunk0, nch, ti):
        ps = self.bank()
        for cc in range(nch):
            tok0 = (chunk0 + cc) * 128
            for k in range(8):
                self.mm(ps[:, cc * ncols:(cc + 1) * ncols], Y[:, k, tok0:tok0 + 128].k(ti), W[:, k, c0:c0 + ncols],
                        start=(k == 0), stop=(k == 7))
        return ps

    def softmax_av(self, jobs, nq, O, DEN, PTS):
        n = len(jobs)
        pend = None
        for i in range(n + 1):
            if i < n:
                kv, qv, vv, ov = jobs[i]
                S = self.bank()
                self.mm(S[:, 0:nq], kv, qv)
                PT = PTS[i % len(PTS)]
                self.act(PT[:, 0:nq], S[:, 0:nq], AF.Exp, scale=0.125)
                self.rel(S)
                cur = (PT, vv, ov, i)
            else:
                cur = None
            if pend is not None:
                PTp, vvp, ovp, ip = pend
                self.mm(O, vvp, PTp[:, 0:nq], start=(ip == 0), stop=(ip == n - 1))
                self.mm(DEN, ovp, PTp[:, 0:nq], start=(ip == 0), stop=(ip == n - 1))
            pend = cur

    def mixer(self, b, l):
        A = self.A
        al = A.alloc
        I = self.I
        s = 1
        need_ctx = (l < DEPTH - 1)
        TOC = lambda kc: min(kc // 4, 4)
        Y = al("Yfull", (8, NTOK), BF16)
        HT = [al("HT%d" % i, (8, 512), F32) for i in range(2)]
        SQ = al("SQ", (8, 512), BF16)
        RS = al("RS", (512,), F32)
        TMP = [al("TMP%d" % i, (512,), F32) for i in range(2)]
        for ti, tile in enumerate(TILES):
            t0, nt, isctx = tile
            j = 2 if isctx else b
            ht = HT[ti % 2]
            src, rd = self.h_src(b, l, s, tile)
            self.dma("sp", ht[:, :, 0:nt], src, reads=rd, writes=[ht])
            self.prenorm(ht, lambda k, t0=t0, nt=nt, ti=ti: Y[:, k, t0:t0 + nt].k(ti), nt, l, s, j, SQ, RS, TMP)
        A.free(*HT, SQ, *TMP)
        self.COSF = al("COSF", (SEQ,), F32); self.SINF = al("SINF", (SEQ,), F32)
        self.COST = al("COST", (16, 64), F32); self.SINT = al("SINT", (16, 64), F32)
        self.dma("sp", self.COSF, I["cosF"], writes=[self.COSF])
        self.dma("sp", self.SINF, I["sinF"], writes=[self.SINF])
        self.dma("sp", self.COST, I["cosT"].rearrange("p (a b) -> p a b", a=16), writes=[self.COST])
        self.dma("sp", self.SINT, I["sinT"].rearrange("p (a b) -> p a b", a=16), writes=[self.SINT])
        OA = [al("OA%d" % i, (NTOK,), BF16) for i in range(4)]
        OB = [al("OB%d" % i, (NTOK,), BF16) for i in range(4)]
        OC = [al("OC%d" % i, (NTOK,), BF16) for i in range(4)]
        RT = [al("RT%d" % i, (512,), F32) for i in range(2)]
        OD = al("OD", (512,), F32)
        SQ1 = al("SQ1", (512,), BF16)
        qtiles = [t for t in TILES if (not t[2]) or need_ctx]

        WA = al("WA", (8, 640), BF16)
        QT = al("QT", (NTOK,), BF16); KT = al("KT", (NTOK,), BF16); VT = al("VT", (18, 128), BF16)
        OCB = [al("OCB%d" % i, (512,), F32) for i in range(2)]
        PTS = [al("PT%d" % i, (512,), BF16) for i in range(3)]
        RD = al("RD", (512,), F32)
        for hd in range(4):
            self.wload(WA, "wA", (l, hd))
            for ti, (t0, nt, isctx) in enumerate(TILES):
                for which, dst in ((0, QT), (256, KT)):
                    if which == 0 and isctx and not need_ctx:
                        continue
                    ps = self.proj_fm(WA, which, 128, Y, t0, nt, ti)
                    if not isctx:
                        ps2 = self.proj_fm(WA, which + 128, 128, Y, t0, nt, ti)
                        self.rope_fm(dst[:, t0:t0 + nt].k(ti), ps[:, 0:nt], ps2[:, 0:nt], slice(0, 128), t0, nt, RT)
                        self.rel(ps, ps2)
                    else:
                        self.copy("act", dst[:, t0:t0 + nt].k(ti), ps[:, 0:nt])
                        self.rel(ps)
                nch = nt // 128
                ps = self.proj_tm4(WA, 512, 128, Y, t0 // 128, nch, ti)
                self.copy("act", VT[:, t0 // 128:t0 // 128 + nch, :].k(ti),
                          ps[:, 0:nch * 128].f(lambda a: a.rearrange("p (c n) -> p c n", n=128)))
                self.rel(ps)
            for ti, (q0, nq, isctx) in enumerate(qtiles):
                kcs = list(range(16, 18)) if isctx else list(range(18))
                for c in range(2):
                    O = self.bank(); DEN = self.bank()
                    rows = slice(c * 64, (c + 1) * 64)
                    jobs = [(KT[rows, kc * 128:(kc + 1) * 128].k(TOC(kc)), QT[rows, q0:q0 + nq].k(ti),
                             VT[:, kc, :].k(TOC(kc)), self.ONES1) for kc in kcs]
                    self.softmax_av(jobs, nq, O[:, 0:nq], DEN[:, 0:nq], PTS)
                    self.recip(RD[:, 0:nq], DEN[:, 0:nq])
                    self.tt("dve", OCB[c][:, 0:nq], O[:, 0:nq], RD[:, 0:nq], ALU.mult)
                    self.rel(O, DEN)
                self.stt("dve", OD[:, 0:nq], OCB[1][:, 0:nq], self.NEGLAM[l][:, 0:1], OCB[0][:, 0:nq], ALU.mult, ALU.add)
                self.tt("pool", SQ1[:, 0:nq], OD[:, 0:nq], OD[:, 0:nq], ALU.mult)
                ps = self.bank()
                self.mm(ps[:, 0:nq], self.ONES128, SQ1[:, 0:nq])
                self.rstd(RS[:, 0:nq], ps[:, 0:nq])
                self.rel(ps)
                self.stt("dve", OA[hd][:, q0:q0 + nq].k(ti), OD[:, 0:nq], self.DNG[l][:, 0:1], RS[:, 0:nq],
                         ALU.mult, ALU.mult)
        A.free(WA, QT, KT, VT, *OCB, *PTS, RD)

        WB = al("WB", (8, 512), BF16)
        QR = al("QR", (NTOK,), BF16, parts=64); QF = al("QF", (NTOK,), BF16, parts=64)
        QB = al("QB", (NTOK,), BF16, parts=64); KR = al("KR", (NTOK,), BF16, parts=64)
        KTF = al("KTF", (18, 64), BF16); KTB = al("KTB", (18, 64), BF16)
        VT = al("VTr", (18, 128), BF16)
        GS = al("GS", (512,), BF16)
        T3 = RT[0]
        T4 = al("T4", (4, 64), F32); T5 = al("T5", (4, 64), F32)
        SFb = al("SFb", (18, 128), BF16, parts=64); SBb = al("SBb", (18, 128), BF16, parts=64)
        STF = al("STF", (128,), F32, parts=64); STB = al("STB", (128,), F32, parts=64)
        ATM = al("ATM", (4, 128), BF16)
        ORT = RT[1]
        for h in range(4):
            self.wload(WB, "wB", (l, h))
            r64 = slice(0, 64)
            for ti, (t0, nt, isctx) in enumerate(TILES):
                nch = nt // 128
                c0 = t0 // 128
                for which in (0, 128):
                    ps = self.proj_fm(WB, which, 64, Y, t0, nt, ti)
                    if not isctx:
                        ps2 = self.proj_fm(WB, which + 64, 64, Y, t0, nt, ti)
                        self.rope_fm(T3[0:64, 0:nt], ps[0:64, 0:nt], ps2[0:64, 0:nt], r64, t0, nt, RT)
                        self.rel(ps, ps2)
                    else:
                        self.copy("act", T3[0:64, 0:nt], ps[0:64, 0:nt])
                        self.rel(ps)
                    if which == 0:
                        self.copy("act", QR[:, t0:t0 + nt].k(ti), T3[0:64, 0:nt])
                        t3v = T3[0:64, 0:nt].f(lambda a: a.rearrange("p (c n) -> p c n", n=128))
                        self.tt("dve", QF[:, t0:t0 + nt].k(ti).f(lambda a: a.rearrange("p (c n) -> p c n", n=128)),
                                t3v, self.QDF4[(l, h)].k(None).bc(1, (64, nch, 128)), ALU.mult)
                        self.tt("pool", QB[:, t0:t0 + nt].k(ti).f(lambda a: a.rearrange("p (c n) -> p c n", n=128)),
                                t3v, self.QDB4[(l, h)].k(None).bc(1, (64, nch, 128)), ALU.mult)
                    else:
                        self.copy("act", KR[:, t0:t0 + nt].k(ti), T3[0:64, 0:nt])
                ps = self.proj_tm4(WB, 128, 128, Y, c0, nch, ti)
                pv = ps[:, 0:nch * 128].f(lambda a: a.rearrange("p (c n) -> p c n", n=128))
                if not isctx:
                    self.tt("dve", T4[:, 0:nch, :], pv.f(lambda a: a.rearrange("p (c n) -> p c n", n=128)[:, :, 0:64]),
                            self.COST[:, c0:c0 + nch, :], ALU.mult)
                    self.tt("dve", T5[:, 0:nch, :], pv.f(lambda a: a.rearrange("p (c n) -> p c n", n=128)[:, :, 64:128]),
                            self.SINT[:, c0:c0 + nch, :], ALU.mult)
                    self.tt("pool", T4[:, 0:nch, :], T4[:, 0:nch, :], T5[:, 0:nch, :], ALU.add)
                else:
                    self.copy("act", T4[:, 0:nch, :], pv.f(lambda a: a.rearrange("p (c n) -> p c n", n=128)[:, :, 0:64]))
                self.rel(ps)
                self.ts("dve", KTF[:, c0:c0 + nch, :].k(ti), T4[:, 0:nch, :], self.KDF[(l, h)][:, 0:1], ALU.mult)
                self.ts("pool", KTB[:, c0:c0 + nch, :].k(ti), T4[:, 0:nch, :], self.KDB[(l, h)][:, 0:1], ALU.mult)
                ps = self.proj_tm4(WB, 256, 128, Y, c0, nch, ti)
                self.copy("act", VT[:, c0:c0 + nch, :].k(ti),
                          ps[:, 0:nch * 128].f(lambda a: a.rearrange("p (c n) -> p c n", n=128)))
                self.rel(ps)
            SD = self.SDEC[(l, h)]
            def ubank(chs, KTX):
                ps = self.bank()
                for i_, c in enumerate(chs):
                    self.mm(ps[0:64, i_ * 128:(i_ + 1) * 128], KTX[:, c, :].k(TOC(c)), VT[:, c, :].k(TOC(c)))
                return ps
            self.memset("pool", SFb[:, 16, :], 0.0)
            ps = ubank([16, 17], KTF)
            self.copy("act", STF, ps[0:64, 0:128])
            self.copy("act", SFb[:, 17, :], ps[0:64, 0:128])
            self.stt("dve", STF, STF, SD[0:64, 0:1], ps[0:64, 128:256], ALU.mult, ALU.add)
            self.rel(ps)
            self.copy("act", SFb[:, 0, :], STF)
            for g0 in range(0, 16, 4):
                ps = ubank(list(range(g0, g0 + 4)), KTF)
                for i_ in range(4):
                    c = g0 + i_
                    if c == 15:
                        break
                    self.stt("dve", STF, STF, SD[0:64, 0:1], ps[0:64, i_ * 128:(i_ + 1) * 128], ALU.mult, ALU.add)
                    self.copy("act", SFb[:, c + 1, :], STF)
                self.rel(ps)
            self.memset("pool", SBb[:, 17, :], 0.0)
            ps = ubank([17, 16], KTB)
            self.copy("act", STB, ps[0:64, 0:128])
            self.copy("act", SBb[:, 16, :], ps[0:64, 0:128])
            self.stt("dve", STB, STB, SD[0:64, 1:2], ps[0:64, 128:256], ALU.mult, ALU.add)
            self.rel(ps)
            self.copy("act", SBb[:, 15, :], STB)
            for g0 in range(15, -1, -4):
                chs = list(range(g0, g0 - 4, -1))
                ps = ubank(chs, KTB)
                for i_, c in enumerate(chs):
                    if c == 0:
                        break
                    self.stt("dve", STB, STB, SD[0:64, 1:2], ps[0:64, i_ * 128:(i_ + 1) * 128], ALU.mult, ALU.add)
                    self.copy("act", SBb[:, c - 1, :], STB)
                self.rel(ps)
            for ti, (t0, nt, isctx) in enumerate(qtiles):
                nch = nt // 128
                c0 = t0 // 128
                pa = self.bank()
                for cc in range(nch):
                    tk = slice((c0 + cc) * 128, (c0 + cc + 1) * 128)
                    self.mm(pa[:, cc * 128:(cc + 1) * 128], KR[:, tk].k(ti), QR[:, tk].k(ti))
                self.tt("dve", ATM[:, 0:nch, :], pa[:, 0:nch * 128].f(lambda a: a.rearrange("p (c n) -> p c n", n=128)),
                        self.MASK4[(l, h)].k(None).bc(1, (128, nch, 128)), ALU.mult)
                self.rel(pa)
                po = self.bank()
                for cc in range(nch):
                    c = c0 + cc
                    tk = slice(c * 128, (c + 1) * 128)
                    zero_f = (c == 16); zero_b = (c == 17)
                    outv = po[:, cc * 128:(cc + 1) * 128]
                    self.mm(outv, VT[:, c, :].k(ti), ATM[:, cc, :], start=True, stop=(zero_f and zero_b))
                    if not zero_f:
                        self.mm(outv, SFb[:, c, :], QF[:, tk].k(ti), start=False, stop=zero_b)
                    if not zero_b:
                        self.mm(outv, SBb[:, c, :], QB[:, tk].k(ti), start=False, stop=True)
                self.copy("act", ORT[:, 0:nt], po[:, 0:nt])
                self.rel(po)
                self.tt("pool", SQ1[:, 0:nt], ORT[:, 0:nt], ORT[:, 0:nt], ALU.mult)
                ps = self.bank()
                self.mm(ps[:, 0:nt], self.ONES128, SQ1[:, 0:nt])
                self.rstd(RS[:, 0:nt], ps[:, 0:nt])
                self.rel(ps)
                self.stt("dve", OD[:, 0:nt], ORT[:, 0:nt], self.HG[l][:, 1:2], RS[:, 0:nt], ALU.mult, ALU.mult)
                ps = self.proj_fm(WB, 384, 128, Y, t0, nt, ti)
                self.act(GS[:, 0:nt], ps[:, 0:nt], AF.Silu)
                self.rel(ps)
                self.tt("pool", OB[h][:, t0:t0 + nt].k(ti), OD[:, 0:nt], GS[:, 0:nt], ALU.mult)
        A.free(WB, QR, QF, QB, KR, KTF, KTB, VT, GS, T4, T5, SFb, SBb, STF, STB, ATM)

        WCK = al("WCK", (8, 640), BF16)
        KD = [al("KD%d" % g, (NTOK,), BF16) for g in range(2)]
        VZ = [[al("VZ%d%d" % (g, hh), (18, 128), BF16) for hh in range(2)] for g in range(2)]
        SQK = al("SQK", (512,), BF16)
        RSK = al("RSK", (512,), F32)
        QT = al("QTc", (NTOK,), BF16)
        WCQ = al("WCQ", (8, 256), BF16)
        PTS = [al("PT%d" % i, (512,), BF16) for i in range(3)]
        RD = al("RD", (512,), F32)
        self.wload(WCK, "wCk", (l,))

        def qk_tile(W, c0, dst, gcol, t0, nt, ti, isctx):
            ps = self.proj_fm(W, c0, 128, Y, t0, nt, ti)
            self.act(SQK[:, 0:nt], ps[:, 0:nt], AF.Square)
            pss = self.bank()
            self.mm(pss[:, 0:nt], self.BD64, SQK[:, 0:nt])
            self.rstd(RSK[:, 0:nt], pss[:, 0:nt])
            self.rel(pss)
            gq = self.HG[l][:, gcol:gcol + 1]; gsw = self.HG[l][:, gcol + 1:gcol + 2]
            if not isctx:
                ps2 = self.proj_fm(W, c0 + 128, 128, Y, t0, nt, ti)
                r1, r2 = RT
                self.stt("dve", r1[:, 0:nt], ps[:, 0:nt], gq, self.COSF[:, t0:t0 + nt], ALU.mult, ALU.mult)
                self.stt("dve", r2[:, 0:nt], ps2[:, 0:nt], gsw, self.SINF[:, t0:t0 + nt], ALU.mult, ALU.mult)
                self.rel(ps, ps2)
                self.tt("pool", r1[:, 0:nt], r1[:, 0:nt], r2[:, 0:nt], ALU.add)
                self.tt("dve", dst[:, t0:t0 + nt].k(ti), r1[:, 0:nt], RSK[:, 0:nt], ALU.mult)
            else:
                self.stt("dve", dst[:, t0:t0 + nt].k(ti), ps[:, 0:nt], gq, RSK[:, 0:nt], ALU.mult, ALU.mult)
                self.rel(ps)

        for ti, (t0, nt, isctx) in enumerate(TILES):
            nch = nt // 128
            c0 = t0 // 128
            for g in range(2):
                qk_tile(WCK, g * 256, KD[g], 4, t0, nt, ti, isctx)
            ps = self.proj_tm4(WCK, 512, 128, Y, c0, nch, ti)
            pv = ps[:, 0:nch * 128]
            for g in range(2):
                for hh in range(2):
                    self.memset("pool", VZ[g][hh][:, c0:c0 + nch, :].k(ti), 0.0)
            for g in range(2):
                for hh in range(2):
                    self.copy("act" if hh == 0 else "dve", VZ[g][hh][:, c0:c0 + nch, hh * 64:(hh + 1) * 64].k(ti),
                              pv.f(lambda a, g=g: a.rearrange("p (c n) -> p c n", n=128)[:, :, g * 64:(g + 1) * 64]))
            self.rel(ps)
        for qi in range(4):
            g = qi // 2
            self.wload(WCQ, "wCq", (l, qi))
            for ti, (t0, nt, isctx) in enumerate(qtiles):
                qk_tile(WCQ, 0, QT, 2, t0, nt, ti, isctx)
            for ti, (q0, nq, isctx) in enumerate(qtiles):
                kcs = list(range(16, 18)) if isctx else list(range(18))
                O = self.bank(); DEN = self.bank()
                jobs = []
                for hh in range(2):
                    rows = slice(hh * 64, (hh + 1) * 64)
                    for kc in kcs:
                        jobs.append((KD[g][rows, kc * 128:(kc + 1) * 128].k(TOC(kc)), QT[rows, q0:q0 + nq].k(ti),
                                     VZ[g][hh][:, kc, :].k(TOC(kc)), self.ONESL if hh == 0 else self.ONESR))
                self.softmax_av(jobs, nq, O[:, 0:nq], DEN[:, 0:nq], PTS)
                self.recip(RD[:, 0:nq], DEN[:, 0:nq])
                self.tt("dve", OC[qi][:, q0:q0 + nq].k(ti), O[:, 0:nq], RD[:, 0:nq], ALU.mult)
                self.rel(O, DEN)
        A.free(WCK, *KD, VZ[0][0], VZ[0][1], VZ[1][0], VZ[1][1], SQK, RSK, QT, WCQ)
        A.free(self.COSF, self.SINF, self.COST, self.SINT, *RT, *PTS, RD, OD, SQ1)

        WX = [al("WX%d" % i, (8, 512), BF16) for i in range(3)]
        wx_rr = [0]
        def wx():
            w = WX[wx_rr[0] % 3]; wx_rr[0] += 1
            return w
        SG = [al("SG%d" % i, (512,), F32) for i in range(2)]
        TB = al("TB", (512,), F32)
        MA = al("MA", (8, 512), F32)
        MB = al("MB", (8, 512), BF16)
        SQ2 = al("SQ2", (8, 512), BF16)
        HTm = al("HTm", (8, 512), F32)
        TMP = [al("TMPm%d" % i, (512,), F32) for i in range(2)]
        OBR = (OA, OB, OC)
        for ti, tile in enumerate(qtiles):
            t0, nt, isctx = tile
            j = 2 if isctx else b
            src, rd = self.h_src(b, l, s, tile)
            self.dma("sp", HTm[:, :, 0:nt], src, reads=rd, writes=[HTm])
            sgi = 0
            for i in range(3):
                wbr = wx()
                self.wload(wbr, "wBr", (l, i))
                wbrv = lambda kc, dc, wbr=wbr: wbr.k(None).f(
                    lambda a: a.rearrange("p a b -> p (a b)").rearrange("p (kc n) -> p kc n", kc=4)[:, kc, dc * 128:(dc + 1) * 128])
                for hf in range(2):
                    wg = wx()
                    self.wload(wg, "wG", (l, i, hf))
                    for dd in range(4):
                        dc = hf * 4 + dd
                        pg = self.proj_fm(wg, dd * 128, 128, Y, t0, nt, ti)
                        sg = SG[sgi % 2]; sgi += 1
                        self.act(sg[:, 0:nt], pg[:, 0:nt], AF.Sigmoid, bias=self.BG[l][:, i * 8 + dc:i * 8 + dc + 1])
                        self.rel(pg)
                        pb = self.bank()
                        for kc in range(4):
                            self.mm(pb[:, 0:nt], wbrv(kc, dc), OBR[i][kc][:, t0:t0 + nt].k(ti), start=(kc == 0), stop=(kc == 3))
                        if i == 0:
                            self.tt("dve", MA[:, dc, 0:nt], sg[:, 0:nt], pb[:, 0:nt], ALU.mult)
                        else:
                            self.tt("dve", TB[:, 0:nt], sg[:, 0:nt], pb[:, 0:nt], ALU.mult)
                            self.tt("pool", MA[:, dc, 0:nt], MA[:, dc, 0:nt], TB[:, 0:nt], ALU.add)
                        self.rel(pb)
            self.copy("pool", MB[:, :, 0:nt], MA[:, :, 0:nt])
            for hf in range(2):
                wo_ = wx()
                self.wload(wo_, "wO", (l, hf))
                for dd in range(4):
                    dc = hf * 4 + dd
                    po = self.bank()
                    for k in range(8):
                        self.mm(po[:, 0:nt], wo_[:, k, dd * 128:(dd + 1) * 128], MB[:, k, 0:nt], start=(k == 0), stop=(k == 7))
                    self.copy("act", MA[:, dc, 0:nt], po[:, 0:nt])
                    self.rel(po)
            self.postnorm_residual(HTm, MA, SQ2, RS, TMP, nt, l, s, j)
            dst, wr = self.h_dst(b, l, s, tile)
            self.dma("sp", dst, HTm[:, :, 0:nt], reads=[HTm], writes=wr)
        if not need_ctx:
            pass
        A.free(*WX, *SG, TB, MA, MB, SQ2, HTm, *TMP, RS, Y, *OA, *OB, *OC)

def _swap64(w):
    k, n = w.shape
    return w.reshape(k, n // 64, 2, 32)[:, :, ::-1, :].reshape(k, n)


def _blk(w):
    k, n = w.shape
    return np.ascontiguousarray(w.reshape(k // 128, 128, n).transpose(1, 0, 2)).reshape(128, (k // 128) * n)


def rope_tables():
    rows = SEQ // 64
    row = np.repeat(np.arange(rows, dtype=np.float32), 64)
    col = np.tile(np.arange(64, dtype=np.float32), rows)
    n_freq = 16
    inv = (10000.0 ** (-np.arange(n_freq, dtype=np.float32) / n_freq)).astype(np.float32)
    ang = np.concatenate([row[:, None] * inv, col[:, None] * inv], axis=-1).astype(np.float32)
    cos, sin = np.cos(ang).astype(np.float32), np.sin(ang).astype(np.float32)
    d = np.arange(64)
    cos64 = cos[:, d % 32]
    sin64 = sin[:, d % 32] * np.where(d < 32, -1.0, 1.0).astype(np.float32)
    cosF = np.ascontiguousarray(np.concatenate([cos64, cos64], axis=1).T)
    sinF = np.ascontiguousarray(np.concatenate([sin64, sin64], axis=1).T)
    cosT = np.ascontiguousarray(cos64.reshape(16, 128, 64).transpose(1, 0, 2)).reshape(128, 16 * 64)
    sinT = np.ascontiguousarray(sin64.reshape(16, 128, 64).transpose(1, 0, 2)).reshape(128, 16 * 64)
    return cosF, sinF, cosT, sinT


def shared_inputs(w_mod, b_mod, norm_g, w_ffn_in, w_ffn_out, w_in, b_gate, diff_lambda, diff_norm_g,
                  ret_decay_logit, ret_norm_g, qk_norm_g, w_branch, w_out):
    f = np.float32
    S = {}
    S["wmod"] = np.ascontiguousarray(w_mod, dtype=f)
    S["bmodT"] = np.ascontiguousarray(b_mod.reshape(DEPTH, 72, 128).transpose(0, 2, 1), dtype=f)
    S["normgT"] = np.ascontiguousarray(norm_g.reshape(DEPTH, 6, 8, 128).transpose(0, 3, 1, 2), dtype=f)
    wi = np.empty((DEPTH, 2, NJ // 2, 128, 2 * 8 * 256), f)
    wo = np.empty((DEPTH, 2, 4, 128, 2 * NJ * 128), f)
    for l in range(DEPTH):
        for s in range(2):
            w = w_ffn_in[l, s]
            a = w[:, :DFF].reshape(8, 128, NJ, 128); u = w[:, DFF:].reshape(8, 128, NJ, 128)
            au = np.stack([a, u], axis=3)
            au = au.transpose(2, 1, 0, 3, 4)
            au = au.reshape(NJ // 2, 2, 128, 8, 256).transpose(0, 2, 1, 3, 4)
            wi[l, s] = au.reshape(NJ // 2, 128, 2 * 8 * 256)
            w2 = w_ffn_out[l, s].reshape(NJ, 128, 8, 128)
            w2 = w2.transpose(2, 1, 0, 3)
            w2 = w2.reshape(4, 2, 128, NJ, 128).transpose(0, 2, 1, 3, 4)
            wo[l, s] = w2.reshape(4, 128, 2 * NJ * 128)
    S["wi"] = wi; S["wo"] = wo
    sizes = (512, 512, 512, 256, 256, 512, 512, 512, 128, 128, 3072)
    names = ('a_q', 'a_k', 'a_v', 'b_q', 'b_k', 'b_v', 'b_g', 'c_q', 'c_k', 'c_v', 'gate')
    sp = {}; lo = 0
    for nm, sz in zip(names, sizes):
        sp[nm] = (lo, lo + sz); lo += sz
    wA = np.empty((DEPTH, 4, 128, 8 * 640), f); wB = np.empty((DEPTH, 4, 128, 8 * 512), f)
    wCk = np.empty((DEPTH, 128, 8 * 640), f); wCq = np.empty((DEPTH, 4, 128, 8 * 256), f)
    wG = np.empty((DEPTH, 3, 2, 128, 8 * 512), f); wBr = np.empty((DEPTH, 3, 128, 4 * 1024), f)
    wO = np.empty((DEPTH, 2, 128, 8 * 512), f)
    for l in range(DEPTH):
        W = w_in[l]
        col = lambda nm, a, n: W[:, sp[nm][0] + a: sp[nm][0] + a + n]
        for h in range(4):
            q = col('a_q', h * 128, 128); k = col('a_k', h * 128, 128); v = col('a_v', h * 128, 128)
            wA[l, h] = _blk(np.concatenate([q, _swap64(q), k, _swap64(k), v], axis=1))
            q = col('b_q', h * 64, 64); k = col('b_k', h * 64, 64)
            v = col('b_v', h * 128, 128); g = col('b_g', h * 128, 128)
            wB[l, h] = _blk(np.concatenate([q, _swap64(q), k, _swap64(k), v, g], axis=1))
        k0 = col('c_k', 0, 64); k1 = col('c_k', 64, 64); v = col('c_v', 0, 128)
        wCk[l] = _blk(np.concatenate([k0, k0, _swap64(k0), _swap64(k0), k1, k1, _swap64(k1), _swap64(k1), v], axis=1))
        for qi in range(4):
            q = col('c_q', qi * 128, 128)
            wCq[l, qi] = _blk(np.concatenate([q, _swap64(q)], axis=1))
        for i in range(3):
            for hf in range(2):
                wG[l, i, hf] = _blk(col('gate', i * 1024 + hf * 512, 512))
            wb = w_branch[l, i]
            wBr[l, i] = np.ascontiguousarray(wb.reshape(4, 128, 1024).transpose(1, 0, 2)).reshape(128, 4096)
        for hf in range(2):
            wO[l, hf] = _blk(w_out[l][:, hf * 512:(hf + 1) * 512])
    S.update(wA=wA, wB=wB, wCk=wCk, wCq=wCq, wG=wG, wBr=wBr, wO=wO)
    S["bgT"] = np.ascontiguousarray(b_gate.reshape(DEPTH, 24, 128).transpose(0, 2, 1), dtype=f)
    S["lamb"] = np.ascontiguousarray(np.broadcast_to(diff_lambda.reshape(DEPTH, 1, 256), (DEPTH, 128, 256)), dtype=f)
    hg = np.empty((DEPTH, 128, 6), f)
    sw = (np.arange(64) + 32) % 64
    for l in range(DEPTH):
        hg[l, :, 0] = diff_norm_g[l]
        hg[l, :, 1] = ret_norm_g[l]
        hg[l, :, 2] = np.tile(qk_norm_g[l, 0], 2)
        hg[l, :, 3] = np.tile(qk_norm_g[l, 0][sw], 2)
        hg[l, :, 4] = np.tile(qk_norm_g[l, 1], 2)
        hg[l, :, 5] = np.tile(qk_norm_g[l, 1][sw], 2)
    S["hg"] = hg
    S["retl"] = np.ascontiguousarray(np.broadcast_to(ret_decay_logit.reshape(DEPTH, 1, 8), (DEPTH, 128, 8)), dtype=f)
    cosF, sinF, cosT, sinT = rope_tables()
    S.update(cosF=cosF, sinF=sinF, cosT=cosT, sinT=sinT)
    i = np.arange(128, dtype=f)
    S["rel"] = np.ascontiguousarray(i[None, :] - i[:, None])
    S["posf"] = np.ascontiguousarray(np.broadcast_to(i[None, :], (128, 128)))
    S["posp"] = np.ascontiguousarray(i[:, None])
    return S


def core_inputs(x, c, ctx, c_ctx, core):
    f = np.float32
    b0 = core * BPC
    M = {}
    M["xT"] = np.ascontiguousarray(x[b0:b0 + BPC].transpose(0, 2, 1), dtype=f)
    M["ctxT"] = np.ascontiguousarray(ctx[b0:b0 + BPC].transpose(0, 2, 1), dtype=f)
    cc = np.stack([c[b0], c[b0 + 1], c_ctx], axis=1)
    M["cT"] = np.ascontiguousarray(cc.reshape(8, 128, 3).transpose(1, 0, 2), dtype=f)
    return M


_CACHE = {}


def get_program(nsub=6):
    if nsub not in _CACHE:
        bld = Builder(nsub=nsub)
        with bld.stack:
            bld.build()
        _CACHE[nsub] = bld
    return _CACHE[nsub]


def kernel(x, c, ctx, c_ctx, w_mod, b_mod, norm_g, w_ffn_in, w_ffn_out, w_in, b_gate,
           diff_lambda, diff_norm_g, ret_decay_logit, ret_norm_g, qk_norm_g, w_branch, w_out, _nsub=6):
    args = [np.asarray(a) for a in (w_mod, b_mod, norm_g, w_ffn_in, w_ffn_out, w_in, b_gate, diff_lambda,
                                    diff_norm_g, ret_decay_logit, ret_norm_g, qk_norm_g, w_branch, w_out)]
    S = shared_inputs(*args)
    x = np.asarray(x); c = np.asarray(c); ctx = np.asarray(ctx); c_ctx = np.asarray(c_ctx)
    in_maps = []
    for core in range(N_CORES):
        m = dict(S)
        m.update(core_inputs(x, c, ctx, c_ctx, core))
        in_maps.append(m)
    bld = get_program(_nsub)
    res = run_bass_kernel_spmd(bld.nc, in_maps, core_ids=list(range(N_CORES)))
    out = np.empty((N_CORES * BPC, SEQ, D), np.float32)
    for core in range(N_CORES):
        o = res.results[core]["outT"]
        out[core * BPC:(core + 1) * BPC] = o.transpose(0, 2, 1)
    return out
```

```python
import math
from contextlib import ExitStack

import numpy as np
import concourse.bass as bass
import concourse.mybir as mybir
from concourse.bass_utils import run_bass_kernel_spmd

F32 = mybir.dt.float32
BF16 = mybir.dt.bfloat16
ALU = mybir.AluOpType
AF = mybir.ActivationFunctionType
AX = mybir.AxisListType

D = 1024
SEQ = 2048
CTX = 256
NTOK = SEQ + CTX
DEPTH = 2
DFF = 2816
NJ = DFF // 128
EPS = 1e-6
N_CORES = 8
BPC = 2

ENGS = ("pe", "act", "dve", "pool", "sp")
N_DMA_SEMS = 24

TILES = [(0, 512, False), (512, 512, False), (1024, 512, False), (1536, 512, False), (2048, 256, True)]


class Buf:
    def __init__(self, name, ap_fn):
        self.name = name
        self.ap_fn = ap_fn
        self.st = {}

    def ap(self):
        return self.ap_fn()

    def __getitem__(self, idx):
        return V(self, idx)

    def k(self, key):
        return V(self, None, key)


class V:
    def __init__(self, buf, idx=None, key=None, post=None):
        self.buf = buf
        self.idx = idx
        self.key = key
        self.post = post

    def k(self, key):
        return V(self.buf, self.idx, key, self.post)

    def f(self, post):
        return V(self.buf, self.idx, self.key, post)

    def bc(self, axis, shape):
        return V(self.buf, self.idx, self.key, lambda a: a.unsqueeze(axis).to_broadcast(list(shape)))

    def ap(self):
        a = self.buf.ap()
        if self.idx is not None:
            a = a[self.idx]
        if self.post is not None:
            a = self.post(a)
        return a

    def dep(self):
        return (self.buf, self.key)


def _dep(x):
    if isinstance(x, Buf):
        return (x, None)
    if isinstance(x, V):
        return x.dep()
    return x


def _ap(x):
    return x.ap() if isinstance(x, (Buf, V)) else x


class Op:
    __slots__ = ("eng", "fn", "waits", "idx", "signal", "dma", "tok")


class Prog:
    def __init__(self, nc):
        self.nc = nc
        self.ops = {e: [] for e in ENGS}
        self.dma_rr = 0
        self.dma_cnt = [0] * N_DMA_SEMS
        self.dma_last = [None] * N_DMA_SEMS
        self.seen = {e: {} for e in ENGS}

    def _need(self, eng, tok, waits):
        if tok is None:
            return
        if tok[0] == 'e':
            _, e2, idx = tok
            if e2 == eng and e2 == 'pe':
                return
            key = ('e', e2)
            if self.seen[eng].get(key, -1) >= idx:
                return
            self.seen[eng][key] = idx
            waits.append(tok)
        else:
            _, k, val = tok
            key = ('d', k)
            if self.seen[eng].get(key, -1) >= val:
                return
            self.seen[eng][key] = val
            waits.append(tok)

    @staticmethod
    def _state(buf, key):
        st = buf.st.get(key)
        if st is None:
            st = {'w': None, 'r': []}
            buf.st[key] = st
        return st

    def op(self, eng, fn, reads=(), writes=(), dma=False):
        o = Op()
        o.eng = eng
        o.fn = fn
        o.idx = len(self.ops[eng])
        o.signal = False
        o.dma = dma
        waits = []
        rd = [_dep(x) for x in reads]
        wr = [_dep(x) for x in writes]
        for b, _k in wr:
            for t in getattr(b, 'alias_toks', ()):
                self._need(eng, t, waits)
        if dma:
            k = self.dma_rr
            self.dma_rr = (self.dma_rr + 1) % N_DMA_SEMS
            self._need(eng, self.dma_last[k], waits)
            self.dma_cnt[k] += 16
            tok = ('d', k, self.dma_cnt[k])
            self.dma_last[k] = tok
        else:
            tok = ('e', eng, o.idx)
        o.tok = tok
        for b, key in rd:
            self._need(eng, self._state(b, key)['w'], waits)
        for b, key in wr:
            st = self._state(b, key)
            self._need(eng, st['w'], waits)
            for r in st['r']:
                self._need(eng, r, waits)
        for b, key in rd:
            rl = self._state(b, key)['r']
            if tok[0] == 'e':
                rl[:] = [r for r in rl if not (r[0] == 'e' and r[1] == tok[1])]
            rl.append(tok)
        for b, key in wr:
            st = self._state(b, key)
            st['w'] = tok
            st['r'] = []
        o.waits = waits
        for w in waits:
            if w[0] == 'e':
                self.ops[w[1]][w[2]].signal = True
        self.ops[eng].append(o)
        return o

    def inherit(self, newbuf, oldbufs):
        best = {}
        def add(t):
            if t is None:
                return
            key = (t[0], t[1])
            if key not in best or best[key][2] < t[2]:
                best[key] = t
        for t in getattr(newbuf, 'alias_toks', ()):
            add(t)
        for ob in oldbufs:
            for t in getattr(ob, 'alias_toks', ()):
                add(t)
            for s2 in ob.st.values():
                add(s2['w'])
                for r in s2['r']:
                    add(r)
        newbuf.alias_toks = list(best.values())

    def final_wait_all(self, eng='sp'):
        waits = []
        for k in range(N_DMA_SEMS):
            self._need(eng, self.dma_last[k], waits)
        for e in ENGS:
            if e != eng:
                nd = [o for o in self.ops[e] if not o.dma and o.fn is not None]
                if nd:
                    self._need(eng, ('e', e, nd[-1].idx), waits)
        o = Op()
        o.eng = eng; o.fn = None; o.idx = len(self.ops[eng]); o.signal = False
        o.dma = False; o.tok = ('e', eng, o.idx); o.waits = waits
        for w in waits:
            if w[0] == 'e':
                self.ops[w[1]][w[2]].signal = True
        self.ops[eng].append(o)

    def emit(self, stack):
        nc = self.nc
        esem = {e: stack.enter_context(nc.semaphore("s_" + e)) for e in ENGS}
        dsem = [stack.enter_context(nc.semaphore("d%d" % k)) for k in range(N_DMA_SEMS)]
        sigval = {}
        for e in ENGS:
            c = 0
            for o in self.ops[e]:
                if o.signal:
                    c += 1
                    sigval[(e, o.idx)] = c
        block = stack.enter_context(nc.Block())

        def run(engname, engobj):
            for o in self.ops[engname]:
                for w in o.waits:
                    if w[0] == 'e':
                        engobj.wait_ge(esem[w[1]], sigval[(w[1], w[2])])
                    else:
                        engobj.wait_ge(dsem[w[1]], w[2])
                if o.fn is None:
                    continue
                ins = o.fn(engobj)
                if o.dma:
                    ins.then_inc(dsem[o.tok[1]], 16)
                elif o.signal:
                    ins.then_inc(esem[engname], 1)

        @block.tensor
        def _(e):
            run("pe", e)

        @block.scalar
        def _(e):
            run("act", e)

        @block.vector
        def _(e):
            run("dve", e)

        @block.gpsimd
        def _(e):
            run("pool", e)

        @block.sync
        def _(e):
            run("sp", e)


class Arena:
    def __init__(self, P, arena_t, nbytes):
        self.P = P
        self.t = arena_t
        self.n = nbytes
        self.live = []
        self.dead = []
        self.peak = 0

    def alloc(self, name, shape, dt, parts=128):
        esz = 2 if dt == BF16 else 4
        nbytes = int(np.prod(shape)) * esz
        size = (nbytes + 63) // 64 * 64
        self.live.sort(key=lambda x: x[0])
        pos = 0
        for (s, e, _) in self.live:
            if s - pos >= size:
                break
            pos = max(pos, e)
        if pos + size > self.n:
            raise MemoryError("arena full allocating %s (%d bytes); live=%s" % (
                name, size, [(b.name, e - s) for s, e, b in self.live]))
        t = self.t
        shape = tuple(shape)

        def apf(pos=pos, nbytes=nbytes):
            a = t[0:parts, pos // 4:(pos + nbytes) // 4]
            if dt != F32:
                a = a.bitcast(dt)
            if len(shape) == 2:
                a = a.rearrange("p (a b) -> p a b", a=shape[0])
            elif len(shape) == 3:
                a = a.rearrange("p (a b c) -> p a b c", a=shape[0], b=shape[1])
            return a

        buf = Buf(name, apf)
        olds = [b for (s, e, b) in self.dead if s < pos + size and e > pos]
        if olds:
            self.P.inherit(buf, olds)
        self.dead = [(s, e, b) for (s, e, b) in self.dead if not (s >= pos and e <= pos + size)]
        self.live.append((pos, pos + size, buf))
        self.peak = max(self.peak, pos + size)
        return buf

    def free(self, *bufs):
        for buf in bufs:
            for i, (s, e, b) in enumerate(self.live):
                if b is buf:
                    self.dead.append(self.live.pop(i))
                    break
            else:
                raise KeyError(buf.name)


class Builder:
    def __init__(self, nsub=6 * DEPTH):
        self.nsub = nsub
        self.nc = bass.Bass("TRN2", target_bir_lowering=False)
        self.P = Prog(self.nc)
        self.stack = ExitStack()

    def op(self, eng, fn, reads=(), writes=(), dma=False):
        return self.P.op(eng, fn, reads=reads, writes=writes, dma=dma)

    def mark(self, name):
        if not hasattr(self, 'marks'):
            self.marks = []
        self.marks.append((name, {e: len(self.P.ops[e]) for e in ENGS}))

    def mm(self, out, lhsT, rhs, start=True, stop=True):
        self.op("pe", lambda e: e.matmul(_ap(out), lhsT=_ap(lhsT), rhs=_ap(rhs), start=start, stop=stop),
                reads=[lhsT, rhs], writes=[out])

    def act(self, out, in_, func, bias=None, scale=1.0, extra_reads=()):
        def fn(e):
            kw = {}
            if bias is not None:
                kw['bias'] = _ap(bias)
            return e.activation(out=_ap(out), in_=_ap(in_), func=func, scale=_ap(scale), **kw)
        rd = [in_] + [x for x in (bias, scale) if isinstance(x, (Buf, V))] + list(extra_reads)
        self.op("act", fn, reads=rd, writes=[out])

    def tt(self, eng, out, in0, in1, op):
        self.op(eng, lambda e: e.tensor_tensor(out=_ap(out), in0=_ap(in0), in1=_ap(in1), op=op),
                reads=[in0, in1], writes=[out])

    def stt(self, eng, out, in0, scalar, in1, op0, op1):
        rd = [in0, in1] + ([scalar] if isinstance(scalar, (Buf, V)) else [])
        self.op(eng, lambda e: e.scalar_tensor_tensor(out=_ap(out), in0=_ap(in0), scalar=_ap(scalar),
                                                      in1=_ap(in1), op0=op0, op1=op1),
                reads=rd, writes=[out])

    def ts(self, eng, out, in0, s1, op0, s2=None, op1=None):
        rd = [in0] + [x for x in (s1, s2) if isinstance(x, (Buf, V))]
        def fn(e):
            if op1 is None:
                return e.tensor_scalar(out=_ap(out), in0=_ap(in0), scalar1=_ap(s1), scalar2=None, op0=op0)
            return e.tensor_scalar(out=_ap(out), in0=_ap(in0), scalar1=_ap(s1), scalar2=_ap(s2), op0=op0, op1=op1)
        self.op(eng, fn, reads=rd, writes=[out])

    def copy(self, eng, out, in_):
        if eng == "act":
            self.op("act", lambda e: e.copy(out=_ap(out), in_=_ap(in_)), reads=[in_], writes=[out])
        else:
            self.op(eng, lambda e: e.tensor_copy(out=_ap(out), in_=_ap(in_)), reads=[in_], writes=[out])

    def recip(self, out, in_):
        self.op("dve", lambda e: e.reciprocal(out=_ap(out), in_=_ap(in_)), reads=[in_], writes=[out])

    def memset(self, eng, out, val):
        self.op(eng, lambda e: e.memset(_ap(out), val), writes=[out])

    def dma(self, eng, out, in_, reads=(), writes=()):
        self.op(eng, lambda e: e.dma_start(out=_ap(out), in_=_ap(in_)),
                reads=list(reads), writes=list(writes), dma=True)

    def bank(self):
        return self.free_banks.pop(0)

    def rel(self, *banks):
        for b in banks:
            assert b not in self.free_banks
            self.free_banks.append(b)

    def rstd(self, RS, ps):
        self.act(RS, ps, AF.Sqrt, bias=self.EPSB[:, 0:1])
        self.recip(RS, RS)

    def build(self):
        nc = self.nc
        dt = lambda name, shape, kind=None, dty=F32: (
            nc.dram_tensor(name, list(shape), dty, kind=kind) if kind else nc.dram_tensor(name, list(shape), dty))
        I = {}
        def inp(name, shape):
            I[name] = dt(name, shape, "ExternalInput").ap()
        inp("xT", (BPC, D, SEQ)); inp("ctxT", (BPC, D, CTX)); inp("cT", (128, 8, 3))
        inp("wmod", (DEPTH, D, 9 * D)); inp("bmodT", (DEPTH, 128, 72)); inp("normgT", (DEPTH, 128, 6, 8))
        inp("wi", (DEPTH, 2, NJ // 2, 128, 2 * 8 * 256)); inp("wo", (DEPTH, 2, 4, 128, 2 * NJ * 128))
        inp("wA", (DEPTH, 4, 128, 8 * 640)); inp("wB", (DEPTH, 4, 128, 8 * 512))
        inp("wCk", (DEPTH, 128, 8 * 640)); inp("wCq", (DEPTH, 4, 128, 8 * 256))
        inp("wGB", (DEPTH, 3, 2, 128, 12 * 512)); inp("wO", (DEPTH, 2, 128, 8 * 512))
        inp("bgT", (DEPTH, 128, 24)); inp("lamb", (DEPTH, 128, 256)); inp("hg", (DEPTH, 128, 6))
        inp("retl", (DEPTH, 128, 8))
        inp("cosF", (128, SEQ)); inp("sinF", (128, SEQ)); inp("cosT", (128, 16 * 64)); inp("sinT", (128, 16 * 64))
        inp("rel", (128, 128)); inp("posf", (128, 128)); inp("posp", (128, 1))
        self.I = I
        self.outT = dt("outT", (BPC, D, SEQ), "ExternalOutput").ap()
        Wb = {}
        for nme in ("wi", "wo", "wA", "wB", "wCk", "wCq", "wGB", "wO"):
            Wb[nme] = dt(nme + "_bf", I[nme].shape, None, BF16).ap()
        self.Wb = Wb
        self.wbuf = {}
        self.hS = dt("hS", (BPC, D, NTOK), None, F32).ap()
        self.hSbuf = Buf("hS", lambda: self.hS)

        stack = self.stack
        ARENA_BYTES = 206 * 1024
        arena_t = stack.enter_context(nc.sbuf_tensor("arena", [128, ARENA_BYTES // 4], F32))
        self.A = Arena(self.P, arena_t, ARENA_BYTES)
        pst = [stack.enter_context(nc.psum_tensor("ps%d" % i, [128, 512], F32)) for i in range(8)]
        self.PS = [Buf("ps%d" % i, (lambda t=t: t[:, :])) for i, t in enumerate(pst)]
        self.free_banks = list(self.PS)

        self.precast()
        self.consts()
        self.modulation()
        isub = 0
        for b in range(BPC):
            for l in range(DEPTH):
                for s in range(3):
                    if isub >= self.nsub * BPC and False:
                        pass
                    if self.sub_count(b, l, s) >= self.nsub:
                        continue
                    self.mark("b%d l%d s%d" % (b, l, s))
                    if s == 1:
                        self.mixer(b, l)
                    else:
                        self.ffn(b, l, s)
        self.mark("end")
        self.P.final_wait_all("sp")
        self.P.emit(stack)

    def sub_count(self, b, l, s):
        return l * 3 + s

    def h_src(self, b, l, s, tile):
        t0, nt, isctx = tile
        first = (l == 0 and s == 0)
        if first:
            if isctx:
                return self.I["ctxT"][b].rearrange("(k p) n -> p k n", p=128), []
            return self.I["xT"][b][:, t0:t0 + nt].rearrange("(k p) n -> p k n", p=128), []
        return self.hS[b][:, t0:t0 + nt].rearrange("(k p) n -> p k n", p=128), [self.hSbuf.k((b, t0))]

    def h_dst(self, b, l, s, tile):
        t0, nt, isctx = tile
        last = (l * 3 + s == self.nsub - 1)
        if last and not isctx:
            return self.outT[b][:, t0:t0 + nt].rearrange("(k p) n -> p k n", p=128), []
        return self.hS[b][:, t0:t0 + nt].rearrange("(k p) n -> p k n", p=128), [self.hSbuf.k((b, t0))]

    def precast(self):
        I, Wb = self.I, self.Wb
        def pc(name, idx):
            key = (name,) + tuple(idx)
            buf = Buf("wb_%s_%s" % (name, idx), None)
            self.wbuf[key] = buf
            src = I[name]; dst = Wb[name]
            for i in idx:
                src = src[i]; dst = dst[i]
            self.dma("pool", dst, src, writes=[buf])
        for l in range(DEPTH):
            for jg in range(NJ // 2):
                pc("wi", (l, 0, jg))
            for dg in range(4):
                pc("wo", (l, 0, dg))
            for h in range(4):
                pc("wA", (l, h))
            for h in range(4):
                pc("wB", (l, h))
            pc("wCk", (l,))
            for q in range(4):
                pc("wCq", (l, q))
            for i in range(3):
                for hf in range(2):
                    pc("wGB", (l, i, hf))
            for hf in range(2):
                pc("wO", (l, hf))
            for jg in range(NJ // 2):
                pc("wi", (l, 1, jg))
            for dg in range(4):
                pc("wo", (l, 1, dg))

    def wload(self, dst, name, idx):
        src = self.Wb[name]
        for i in idx:
            src = src[i]
        d = _ap
        def fn(e, src=src):
            o = _ap(dst)
            nd = len(o.shape)
            if nd == 3:
                o2 = o.rearrange("p a b -> p (a b)")
            elif nd == 4:
                o2 = o.rearrange("p a b c -> p (a b c)")
            else:
                o2 = o
            return e.dma_start(out=o2, in_=src)
        self.op("sp", fn, reads=[self.wbuf[(name,) + tuple(idx)]], writes=[dst], dma=True)

    def consts(self):
        A, I = self.A, self.I
        al = A.alloc
        self.EPSB = al("EPSB", (1,), F32)
        self.memset("dve", self.EPSB, EPS)
        self.LN8 = al("LN8", (1,), F32)
        self.memset("dve", self.LN8, math.log(0.125))
        self.ONESM = al("ONESM", (128,), BF16)
        self.memset("dve", self.ONESM, 1.0 / 1024)
        self.ONES128 = al("ONES128", (128,), BF16)
        self.memset("dve", self.ONES128, 1.0 / 128)
        self.ONES1 = al("ONES1", (128,), BF16)
        self.memset("dve", self.ONES1, 1.0)
        self.ONESL = al("ONESL", (128,), BF16)
        self.ONESR = al("ONESR", (128,), BF16)
        self.memset("dve", self.ONESL, 0.0)
        self.memset("dve", self.ONESR, 0.0)
        self.memset("dve", self.ONESL[:, 0:64], 1.0)
        self.memset("dve", self.ONESR[:, 64:128], 1.0)
        self.BD64 = al("BD64", (128,), BF16)
        self.memset("dve", self.BD64, 0.0)
        self.memset("dve", self.BD64[0:64, 0:64], 1.0 / 64)
        self.memset("dve", self.BD64[64:128, 64:128], 1.0 / 64)
        def ld(name, shape, src):
            b = al(name, shape, F32)
            self.dma("sp", b, src, writes=[b])
            return b
        self.CT = ld("CT", (8, 3), I["cT"])
        self.BMOD = [ld("BMOD%d" % l, (72,), I["bmodT"][l]) for l in range(DEPTH)]
        self.NG = [ld("NG%d" % l, (6, 8), I["normgT"][l]) for l in range(DEPTH)]
        self.BG = [ld("BG%d" % l, (24,), I["bgT"][l]) for l in range(DEPTH)]
        self.HG = [ld("HG%d" % l, (6,), I["hg"][l]) for l in range(DEPTH)]
        self.layer_consts()

    def layer_consts(self):
        A, I = self.A, self.I
        al = A.alloc
        LAMB = [al("LAMB%d" % l, (256,), F32) for l in range(DEPTH)]
        RETL = [al("RETL%d" % l, (8,), F32) for l in range(DEPTH)]
        REL = al("REL", (128,), F32); POSF = al("POSF", (128,), F32); POSP = al("POSP", (1,), F32)
        for l in range(DEPTH):
            self.dma("sp", LAMB[l], I["lamb"][l], writes=[LAMB[l]])
            self.dma("sp", RETL[l], I["retl"][l], writes=[RETL[l]])
        self.dma("sp", REL, I["rel"], writes=[REL])
        self.dma("sp", POSF, I["posf"], writes=[POSF])
        self.dma("sp", POSP, I["posp"], writes=[POSP])
        T1 = al("lcT1", (128,), F32); T2 = al("lcT2", (128,), F32); T3 = al("lcT3", (128,), F32)
        RP = al("RELP", (128,), F32); RN = al("RELN", (128,), F32)
        GE = al("RELGE", (128,), F32); LE = al("RELLE", (128,), F32)
        self.ts("dve", RP, REL, 0.0, ALU.max)
        self.ts("dve", RN, REL, -1.0, ALU.mult, 0.0, ALU.max)
        self.ts("dve", GE, REL, 0.0, ALU.is_ge)
        self.ts("dve", LE, REL, 0.0, ALU.is_le)
        P1 = al("POSF1", (128,), F32)
        P2 = al("POS128", (128,), F32)
        self.ts("dve", P1, POSF, 1.0, ALU.add)
        self.ts("dve", P2, POSF, -1.0, ALU.mult, 128.0, ALU.add)
        PP1 = al("POSP127", (1,), F32)
        self.ts("dve", PP1, POSP, -1.0, ALU.mult, 127.0, ALU.add)
        self.NEGLAM, self.DNG, self.LG = [], [], []
        self.MASK4, self.QDF4, self.QDB4, self.KDF, self.KDB, self.SDEC = {}, {}, {}, {}, {}, {}
        for l in range(DEPTH):
            lam_init = 0.8 - 0.6 * math.exp(-0.3 * l)
            PR = al("lamPR", (2, 64), F32); SM = al("lamSM", (2,), F32)
            self.tt("dve", PR[:, 0, :], LAMB[l][:, 0:64], LAMB[l][:, 64:128], ALU.mult)
            self.tt("dve", PR[:, 1, :], LAMB[l][:, 128:192], LAMB[l][:, 192:256], ALU.mult)
            self.op("dve", lambda e, SM=SM, PR=PR: e.reduce_sum(out=SM.ap(), in_=PR.ap(), axis=AX.X),
                    reads=[PR], writes=[SM])
            self.act(SM, SM, AF.Exp)
            NL = al("NEGLAM%d" % l, (1,), F32)
            self.tt("dve", NL, SM[:, 1:2], SM[:, 0:1], ALU.subtract)
            self.ts("dve", NL, NL, -lam_init, ALU.add)
            self.NEGLAM.append(NL)
            DG = al("DNG%d" % l, (1,), F32)
            self.ts("dve", DG, self.HG[l][:, 0:1], 1.0 - lam_init, ALU.mult)
            self.DNG.append(DG)
            A.free(PR, SM)
            LGt = al("LG%d" % l, (8,), F32)
            self.act(LGt, RETL[l], AF.Exp, scale=-1.0)
            self.ts("dve", LGt, LGt, 1.0, ALU.add)
            self.act(LGt, LGt, AF.Ln)
            self.ts("dve", LGt, LGt, -1.0, ALU.mult)
            self.LG.append(LGt)
            for h in range(4):
                lgf = LGt[:, h:h + 1]; lgb = LGt[:, 4 + h:5 + h]
                M4 = al("MASK_%d_%d" % (l, h), (128,), F32)
                self.act(T1, RP, AF.Exp, scale=lgf)
                self.tt("dve", T1, T1, GE, ALU.mult)
                self.act(T2, RN, AF.Exp, scale=lgb)
                self.tt("dve", T2, T2, LE, ALU.mult)
                self.tt("dve", T3, T1, T2, ALU.add)
                self.ts("dve", M4, T3, 0.125, ALU.mult)
                self.MASK4[(l, h)] = M4
                QF = al("QDF_%d_%d" % (l, h), (128,), F32, parts=64)
                QB = al("QDB_%d_%d" % (l, h), (128,), F32, parts=64)
                self.act(QF, P1[0:64, :], AF.Exp, bias=self.LN8[0:64, 0:1], scale=LGt[0:64, h:h + 1])
                self.act(QB, P2[0:64, :], AF.Exp, bias=self.LN8[0:64, 0:1], scale=LGt[0:64, 4 + h:5 + h])
                self.QDF4[(l, h)] = QF; self.QDB4[(l, h)] = QB
                KF = al("KDF_%d_%d" % (l, h), (1,), F32); KB = al("KDB_%d_%d" % (l, h), (1,), F32)
                self.act(KF, PP1, AF.Exp, scale=lgf)
                self.act(KB, POSP, AF.Exp, scale=lgb)
                self.KDF[(l, h)] = KF; self.KDB[(l, h)] = KB
                SD = al("SDEC_%d_%d" % (l, h), (2,), F32)
                self.act(SD[:, 0:1], lgf, AF.Exp, scale=128.0)
                self.act(SD[:, 1:2], lgb, AF.Exp, scale=128.0)
                self.SDEC[(l, h)] = SD
        A.free(T1, T2, T3, RP, RN, GE, LE, P1, P2, PP1, REL, POSF, POSP)
        A.free(*LAMB); A.free(*RETL)

    def modulation(self):
        A, I = self.A, self.I
        al = A.alloc
        SC = al("SC", (8, 3), BF16)
        self.act(SC, self.CT, AF.Silu)
        WM = [al("WM%d" % i, (8, 512), BF16) for i in range(3)]
        self.MOD = []
        self.AA, self.GG = {}, {}
        for l in range(DEPTH):
            MOD = al("MOD%d" % l, (72, 3), F32)
            ps = self.bank()
            for blk in range(18):
                wm = WM[(l * 18 + blk) % 3]
                src = I["wmod"][l][:, blk * 512:(blk + 1) * 512].rearrange("(k p) n -> p k n", p=128)
                self.dma("pool", wm, src, writes=[wm])
                for n in range(4):
                    c = blk * 4 + n
                    for k in range(8):
                        self.mm(ps[:, c * 3:c * 3 + 3], wm[:, k, n * 128:(n + 1) * 128], SC[:, k, :],
                                start=(k == 0), stop=(k == 7))
            for j in range(3):
                self.op("dve", lambda e, MOD=MOD, ps=ps, l=l, j=j: e.tensor_tensor(
                    out=MOD.ap()[:, :, j],
                    in0=ps.ap()[:, 0:216].rearrange("p (c j) -> p c j", j=3)[:, :, j],
                    in1=self.BMOD[l].ap(), op=ALU.add), reads=[ps, self.BMOD[l]], writes=[MOD])
            self.rel(ps)
            self.MOD.append(MOD)
            for s in range(3):
                AAt = al("AA%d_%d" % (l, s), (3, 8), F32)
                GGt = al("GG%d_%d" % (l, s), (3, 8), F32)
                cs = 0.5 if s != 1 else 1.0
                for j in range(3):
                    sc = MOD[:, (s * 3 + 1) * 8:(s * 3 + 1) * 8 + 8, j]
                    gt = MOD[:, (s * 3 + 2) * 8:(s * 3 + 2) * 8 + 8, j]
                    self.stt("dve", AAt[:, j, :], sc, 1.0, self.NG[l][:, 2 * s, :], ALU.add, ALU.mult)
                    self.stt("dve", GGt[:, j, :], gt, cs, self.NG[l][:, 2 * s + 1, :], ALU.mult, ALU.mult)
                self.AA[(l, s)] = AAt; self.GG[(l, s)] = GGt
        A.free(SC, *WM)

    def shift_ap(self, l, s, j, k):
        c = (s * 3 + 0) * 8 + k
        return self.MOD[l][:, c, j:j + 1]

    def prenorm(self, HT, Yv, nt, l, s, j, SQ, RS, TMP):
        self.tt("pool", SQ[:, :, 0:nt], HT[:, :, 0:nt], HT[:, :, 0:nt], ALU.mult)
        ps = self.bank()
        for k in range(8):
            self.mm(ps[:, 0:nt], self.ONESM, SQ[:, k, 0:nt], start=(k == 0), stop=(k == 7))
        self.rstd(RS[:, 0:nt], ps[:, 0:nt])
        self.rel(ps)
        AAt = self.AA[(l, s)]
        for k in range(8):
            tmp = TMP[k % 2]
            self.stt("dve", tmp[:, 0:nt], HT[:, k, 0:nt], AAt[:, j, k:k + 1], RS[:, 0:nt],
                     ALU.mult, ALU.mult)
            self.act(Yv(k), tmp[:, 0:nt], AF.Identity, bias=self.shift_ap(l, s, j, k))

    def postnorm_residual(self, HT, Y2, SQ, RS, TMP, nt, l, s, j, part=0):
        if part in (0, 1):
            self.tt("pool", SQ[:, :, 0:nt], Y2[:, :, 0:nt], Y2[:, :, 0:nt], ALU.mult)
        if part == 1:
            return
        ps = self.bank()
        for k in range(8):
            self.mm(ps[:, 0:nt], self.ONESM, SQ[:, k, 0:nt], start=(k == 0), stop=(k == 7))
        self.rstd(RS[:, 0:nt], ps[:, 0:nt])
        self.rel(ps)
        GGt = self.GG[(l, s)]
        for k in range(8):
            tmp = TMP[k % 2]
            self.stt("dve", tmp[:, 0:nt], Y2[:, k, 0:nt], GGt[:, j, k:k + 1], RS[:, 0:nt], ALU.mult, ALU.mult)
            self.tt("pool", HT[:, k, 0:nt], HT[:, k, 0:nt], tmp[:, 0:nt], ALU.add)

    def ffn(self, b, l, s):
        A = self.A
        al = A.alloc
        sf = 0 if s == 0 else 1
        last_layer = (l == DEPTH - 1)
        tiles = [t for t in TILES if not (t[2] and last_layer and s == 2)]
        n = len(tiles)
        HT = [al("HT%d" % i, (8, 512), F32) for i in range(2)]
        Y = [al("Yt%d" % i, (8, 512), BF16) for i in range(2)]
        SQp = al("SQp", (8, 512), BF16); SQe = al("SQe", (8, 512), BF16)
        RSp = al("RSp", (512,), F32); RSe = al("RSe", (512,), F32)
        TMPp = [al("TMPp%d" % i, (512,), F32) for i in range(2)]
        TMPe = [al("TMPe%d" % i, (512,), F32) for i in range(2)]
        G = al("G", (NJ, 512), BF16)
        WI = [al("WI%d" % i, (2, 8, 256), BF16) for i in range(3)]
        WO = [al("WO%d" % i, (2, NJ, 128), BF16) for i in range(2)]
        SA = [al("SA%d" % i, (512,), F32) for i in range(2)]
        Y2 = al("Y2", (8, 512), F32)
        rr = {'wi': 0, 'wo': 0}
        jmod = lambda ti: 2 if tiles[ti][2] else b

        def pre(ti):
            t0, nt, isctx = tiles[ti]
            ht = HT[ti % 2]
            src, rd = self.h_src(b, l, s, tiles[ti])
            self.dma("sp", ht[:, :, 0:nt], src, reads=rd, writes=[ht])
            self.prenorm(ht, lambda k: Y[ti % 2][:, k, 0:nt], nt, l, s, jmod(ti), SQp, RSp, TMPp)

        def gphase(ti, jgs):
            t0, nt, isctx = tiles[ti]
            Yt = Y[ti % 2]
            for jg in jgs:
                wi = WI[rr['wi'] % 3]; rr['wi'] += 1
                self.wload(wi, "wi", (l, sf, jg))
                for jj in range(2):
                    jx = jg * 2 + jj
                    pa = self.bank(); pu = self.bank()
                    for k in range(8):
                        self.mm(pa[:, 0:nt], wi[:, jj, k, 0:128], Yt[:, k, 0:nt], start=(k == 0), stop=(k == 7))
                    for k in range(8):
                        self.mm(pu[:, 0:nt], wi[:, jj, k, 128:256], Yt[:, k, 0:nt], start=(k == 0), stop=(k == 7))
                    sa = SA[jx % 2]
                    self.act(sa[:, 0:nt], pa[:, 0:nt], AF.Silu)
                    self.tt("dve", G[:, jx, 0:nt], sa[:, 0:nt], pu[:, 0:nt], ALU.mult)
                    self.rel(pa, pu)

        def ophase(ti):
            t0, nt, isctx = tiles[ti]
            for dg in range(4):
                wo = WO[rr['wo'] % 2]; rr['wo'] += 1
                self.wload(wo, "wo", (l, sf, dg))
                for dd in range(2):
                    dc = dg * 2 + dd
                    po = self.bank()
                    for jx in range(NJ):
                        self.mm(po[:, 0:nt], wo[:, dd, jx, :], G[:, jx, 0:nt], start=(jx == 0), stop=(jx == NJ - 1))
                    self.copy("act", Y2[:, dc, 0:nt], po[:, 0:nt])
                    self.rel(po)

        def epi(ti, part):
            t0, nt, isctx = tiles[ti]
            ht = HT[ti % 2]
            self.postnorm_residual(ht, Y2, SQe, RSe, TMPe, nt, l, s, jmod(ti), part=part)
            if part == 2:
                dst, wr = self.h_dst(b, l, s, tiles[ti])
                self.dma("pool", dst, ht[:, :, 0:nt], reads=[ht], writes=wr)

        NPRE = 2
        pre(0)
        for ti in range(n):
            gphase(ti, range(NPRE if ti > 0 else 0, NJ // 2))
            if ti + 1 < n:
                pre(ti + 1)
            ophase(ti)
            epi(ti, 1)
            if ti + 1 < n:
                gphase(ti + 1, range(0, NPRE))
            epi(ti, 2)
        A.free(*HT, *Y, SQp, SQe, RSp, RSe, *TMPp, *TMPe, G, *WI, *WO, *SA, Y2)

    def rope_fm(self, out, ps, ps_sw, rows, t0, nt, RT):
        r1, r2 = RT
        self.tt("dve", r1[rows, 0:nt], ps, self.COSF[rows, t0:t0 + nt], ALU.mult)
        self.tt("dve", r2[rows, 0:nt], ps_sw, self.SINF[rows, t0:t0 + nt], ALU.mult)
        self.tt("pool", out, r1[rows, 0:nt], r2[rows, 0:nt], ALU.add)

    def proj_fm(self, W, c0, ncols, Y, t0, nt, ti):
        ps = self.bank()
        for k in range(8):
            self.mm(ps[0:ncols, 0:nt], W[:, k, c0:c0 + ncols], Y[:, k, t0:t0 + nt].k(ti),
                    start=(k == 0), stop=(k == 7))
        return ps

    def proj_tm4(self, W, c0, ncols, Y, chunk0, nch, ti):
        ps = self.bank()
        for cc in range(nch):
            tok0 = (chunk0 + cc) * 128
            for k in range(8):
                self.mm(ps[:, cc * ncols:(cc + 1) * ncols], Y[:, k, tok0:tok0 + 128].k(ti), W[:, k, c0:c0 + ncols],
                        start=(k == 0), stop=(k == 7))
        return ps

    def softmax_av(self, jobs, nq, O, DEN, PTS):
        n = len(jobs)
        pend = None
        for i in range(n + 1):
            if i < n:
                kv, qv, vv, ov = jobs[i]
                S = self.bank()
                self.mm(S[:, 0:nq], kv, qv)
                PT = PTS[i % len(PTS)]
                self.act(PT[:, 0:nq], S[:, 0:nq], AF.Exp, scale=0.125)
                self.rel(S)
                cur = (PT, vv, ov, i)
            else:
                cur = None
            if pend is not None:
                PTp, vvp, ovp, ip = pend
                self.mm(O, vvp, PTp[:, 0:nq], start=(ip == 0), stop=(ip == n - 1))
                self.mm(DEN, ovp, PTp[:, 0:nq], start=(ip == 0), stop=(ip == n - 1))
            pend = cur

    def mixer(self, b, l):
        A = self.A
        al = A.alloc
        I = self.I
        s = 1
        need_ctx = (l < DEPTH - 1)
        TOC = lambda kc: min(kc // 4, 4)
        Y = al("Yfull", (8, NTOK), BF16)
        HT = [al("HT%d" % i, (8, 512), F32) for i in range(2)]
        SQ = al("SQ", (8, 512), BF16)
        RS = al("RS", (512,), F32)
        TMP = [al("TMP%d" % i, (512,), F32) for i in range(2)]
        for ti, tile in enumerate(TILES):
            t0, nt, isctx = tile
            j = 2 if isctx else b
            ht = HT[ti % 2]
            src, rd = self.h_src(b, l, s, tile)
            self.dma("sp", ht[:, :, 0:nt], src, reads=rd, writes=[ht])
            self.prenorm(ht, lambda k, t0=t0, nt=nt, ti=ti: Y[:, k, t0:t0 + nt].k(ti), nt, l, s, j, SQ, RS, TMP)
        A.free(*HT, SQ, *TMP)
        self.COSF = al("COSF", (SEQ,), F32); self.SINF = al("SINF", (SEQ,), F32)
        self.COST = al("COST", (16, 64), F32); self.SINT = al("SINT", (16, 64), F32)
        self.dma("sp", self.COSF, I["cosF"], writes=[self.COSF])
        self.dma("sp", self.SINF, I["sinF"], writes=[self.SINF])
        self.dma("sp", self.COST, I["cosT"].rearrange("p (a b) -> p a b", a=16), writes=[self.COST])
        self.dma("sp", self.SINT, I["sinT"].rearrange("p (a b) -> p a b", a=16), writes=[self.SINT])
        OA = [al("OA%d" % i, (NTOK,), BF16) for i in range(4)]
        OB = [al("OB%d" % i, (NTOK,), BF16) for i in range(4)]
        OC = [al("OC%d" % i, (NTOK,), BF16) for i in range(4)]
        RT = [al("RT%d" % i, (512,), F32) for i in range(2)]
        OD = al("OD", (512,), F32)
        SQ1 = al("SQ1", (512,), BF16)
        qtiles = [t for t in TILES if (not t[2]) or need_ctx]

        self.mark("  M1 b%d l%d" % (b, l))
        WA = al("WA", (8, 640), BF16)
        QT = al("QT", (NTOK,), BF16); KT = al("KT", (NTOK,), BF16); VT = al("VT", (18, 128), BF16)
        OCB = [al("OCB%d" % i, (512,), F32) for i in range(2)]
        PTS = [al("PT%d" % i, (512,), BF16) for i in range(3)]
        RD = al("RD", (512,), F32)
        for hd in range(4):
            self.wload(WA, "wA", (l, hd))
            for ti, (t0, nt, isctx) in enumerate(TILES):
                for which, dst in ((0, QT), (256, KT)):
                    if which == 0 and isctx and not need_ctx:
                        continue
                    ps = self.proj_fm(WA, which, 128, Y, t0, nt, ti)
                    if not isctx:
                        ps2 = self.proj_fm(WA, which + 128, 128, Y, t0, nt, ti)
                        self.rope_fm(dst[:, t0:t0 + nt].k(ti), ps[:, 0:nt], ps2[:, 0:nt], slice(0, 128), t0, nt, RT)
                        self.rel(ps, ps2)
                    else:
                        self.copy("act", dst[:, t0:t0 + nt].k(ti), ps[:, 0:nt])
                        self.rel(ps)
                nch = nt // 128
                ps = self.proj_tm4(WA, 512, 128, Y, t0 // 128, nch, ti)
                self.copy("act", VT[:, t0 // 128:t0 // 128 + nch, :].k(ti),
                          ps[:, 0:nch * 128].f(lambda a: a.rearrange("p (c n) -> p c n", n=128)))
                self.rel(ps)
            for ti, (q0, nq, isctx) in enumerate(qtiles):
                kcs = list(range(16, 18)) if isctx else list(range(18))
                for c in range(2):
                    O = self.bank(); DEN = self.bank()
                    rows = slice(c * 64, (c + 1) * 64)
                    jobs = [(KT[rows, kc * 128:(kc + 1) * 128].k(TOC(kc)), QT[rows, q0:q0 + nq].k(ti),
                             VT[:, kc, :].k(TOC(kc)), self.ONES1) for kc in kcs]
                    self.softmax_av(jobs, nq, O[:, 0:nq], DEN[:, 0:nq], PTS)
                    self.recip(RD[:, 0:nq], DEN[:, 0:nq])
                    self.tt("dve", OCB[c][:, 0:nq], O[:, 0:nq], RD[:, 0:nq], ALU.mult)
                    self.rel(O, DEN)
                self.stt("dve", OD[:, 0:nq], OCB[1][:, 0:nq], self.NEGLAM[l][:, 0:1], OCB[0][:, 0:nq], ALU.mult, ALU.add)
                self.tt("pool", SQ1[:, 0:nq], OD[:, 0:nq], OD[:, 0:nq], ALU.mult)
                ps = self.bank()
                self.mm(ps[:, 0:nq], self.ONES128, SQ1[:, 0:nq])
                self.rstd(RS[:, 0:nq], ps[:, 0:nq])
                self.rel(ps)
                self.stt("dve", OA[hd][:, q0:q0 + nq].k(ti), OD[:, 0:nq], self.DNG[l][:, 0:1], RS[:, 0:nq],
                         ALU.mult, ALU.mult)
        A.free(WA, QT, KT, VT, *OCB, *PTS, RD)

        self.mark("  M2 b%d l%d" % (b, l))
        WB = al("WB", (8, 512), BF16)
        QR = al("QR", (NTOK,), BF16, parts=64); QF = al("QF", (NTOK,), BF16, parts=64)
        QB = al("QB", (NTOK,), BF16, parts=64); KR = al("KR", (NTOK,), BF16, parts=64)
        KTF = al("KTF", (18, 64), BF16); KTB = al("KTB", (18, 64), BF16)
        VT = al("VTr", (18, 128), BF16)
        GS = al("GS", (512,), BF16)
        T3 = RT[0]
        T4 = al("T4", (4, 64), F32); T5 = al("T5", (4, 64), F32)
        SFb = al("SFb", (18, 128), BF16, parts=64); SBb = al("SBb", (18, 128), BF16, parts=64)
        STF = al("STF", (128,), F32, parts=64); STB = al("STB", (128,), F32, parts=64)
        ATM = al("ATM", (4, 128), BF16)
        ORT = RT[1]
        for h in range(4):
            self.wload(WB, "wB", (l, h))
            r64 = slice(0, 64)
            for ti, (t0, nt, isctx) in enumerate(TILES):
                nch = nt // 128
                c0 = t0 // 128
                for which in (0, 128):
                    ps = self.proj_fm(WB, which, 64, Y, t0, nt, ti)
                    if not isctx:
                        ps2 = self.proj_fm(WB, which + 64, 64, Y, t0, nt, ti)
                        self.rope_fm(T3[0:64, 0:nt], ps[0:64, 0:nt], ps2[0:64, 0:nt], r64, t0, nt, RT)
                        self.rel(ps, ps2)
                    else:
                        self.copy("act", T3[0:64, 0:nt], ps[0:64, 0:nt])
                        self.rel(ps)
                    if which == 0:
                        self.copy("act", QR[:, t0:t0 + nt].k(ti), T3[0:64, 0:nt])
                        t3v = T3[0:64, 0:nt].f(lambda a: a.rearrange("p (c n) -> p c n", n=128))
                        self.tt("dve", QF[:, t0:t0 + nt].k(ti).f(lambda a: a.rearrange("p (c n) -> p c n", n=128)),
                                t3v, self.QDF4[(l, h)].k(None).bc(1, (64, nch, 128)), ALU.mult)
                        self.tt("pool", QB[:, t0:t0 + nt].k(ti).f(lambda a: a.rearrange("p (c n) -> p c n", n=128)),
                                t3v, self.QDB4[(l, h)].k(None).bc(1, (64, nch, 128)), ALU.mult)
                    else:
                        self.copy("act", KR[:, t0:t0 + nt].k(ti), T3[0:64, 0:nt])
                ps = self.proj_tm4(WB, 128, 128, Y, c0, nch, ti)
                pv = ps[:, 0:nch * 128].f(lambda a: a.rearrange("p (c n) -> p c n", n=128))
                if not isctx:
                    self.tt("dve", T4[:, 0:nch, :], pv.f(lambda a: a.rearrange("p (c n) -> p c n", n=128)[:, :, 0:64]),
                            self.COST[:, c0:c0 + nch, :], ALU.mult)
                    self.tt("dve", T5[:, 0:nch, :], pv.f(lambda a: a.rearrange("p (c n) -> p c n", n=128)[:, :, 64:128]),
                            self.SINT[:, c0:c0 + nch, :], ALU.mult)
                    self.tt("pool", T4[:, 0:nch, :], T4[:, 0:nch, :], T5[:, 0:nch, :], ALU.add)
                else:
                    self.copy("act", T4[:, 0:nch, :], pv.f(lambda a: a.rearrange("p (c n) -> p c n", n=128)[:, :, 0:64]))
                self.rel(ps)
                self.ts("dve", KTF[:, c0:c0 + nch, :].k(ti), T4[:, 0:nch, :], self.KDF[(l, h)][:, 0:1], ALU.mult)
                self.ts("pool", KTB[:, c0:c0 + nch, :].k(ti), T4[:, 0:nch, :], self.KDB[(l, h)][:, 0:1], ALU.mult)
                ps = self.proj_tm4(WB, 256, 128, Y, c0, nch, ti)
                self.copy("act", VT[:, c0:c0 + nch, :].k(ti),
                          ps[:, 0:nch * 128].f(lambda a: a.rearrange("p (c n) -> p c n", n=128)))
                self.rel(ps)
            SD = self.SDEC[(l, h)]
            def ubank(chs, KTX):
                ps = self.bank()
                for i_, c in enumerate(chs):
                    self.mm(ps[0:64, i_ * 128:(i_ + 1) * 128], KTX[:, c, :].k(TOC(c)), VT[:, c, :].k(TOC(c)))
                return ps
            self.memset("pool", SFb[:, 16, :], 0.0)
            ps = ubank([16, 17], KTF)
            self.copy("act", STF, ps[0:64, 0:128])
            self.copy("act", SFb[:, 17, :], ps[0:64, 0:128])
            self.stt("dve", STF, STF, SD[0:64, 0:1], ps[0:64, 128:256], ALU.mult, ALU.add)
            self.rel(ps)
            self.copy("act", SFb[:, 0, :], STF)
            for g0 in range(0, 16, 4):
                ps = ubank(list(range(g0, g0 + 4)), KTF)
                for i_ in range(4):
                    c = g0 + i_
                    if c == 15:
                        break
                    self.stt("dve", STF, STF, SD[0:64, 0:1], ps[0:64, i_ * 128:(i_ + 1) * 128], ALU.mult, ALU.add)
                    self.copy("act", SFb[:, c + 1, :], STF)
                self.rel(ps)
            self.memset("pool", SBb[:, 17, :], 0.0)
            ps = ubank([17, 16], KTB)
            self.copy("act", STB, ps[0:64, 0:128])
            self.copy("act", SBb[:, 16, :], ps[0:64, 0:128])
            self.stt("dve", STB, STB, SD[0:64, 1:2], ps[0:64, 128:256], ALU.mult, ALU.add)
            self.rel(ps)
            self.copy("act", SBb[:, 15, :], STB)
            for g0 in range(15, -1, -4):
                chs = list(range(g0, g0 - 4, -1))
                ps = ubank(chs, KTB)
                for i_, c in enumerate(chs):
                    if c == 0:
                        break
                    self.stt("dve", STB, STB, SD[0:64, 1:2], ps[0:64, i_ * 128:(i_ + 1) * 128], ALU.mult, ALU.add)
                    self.copy("act", SBb[:, c - 1, :], STB)
                self.rel(ps)
            for ti, (t0, nt, isctx) in enumerate(qtiles):
                nch = nt // 128
                c0 = t0 // 128
                pa = self.bank()
                for cc in range(nch):
                    tk = slice((c0 + cc) * 128, (c0 + cc + 1) * 128)
                    self.mm(pa[:, cc * 128:(cc + 1) * 128], KR[:, tk].k(ti), QR[:, tk].k(ti))
                self.tt("dve", ATM[:, 0:nch, :], pa[:, 0:nch * 128].f(lambda a: a.rearrange("p (c n) -> p c n", n=128)),
                        self.MASK4[(l, h)].k(None).bc(1, (128, nch, 128)), ALU.mult)
                self.rel(pa)
                po = self.bank()
                for cc in range(nch):
                    c = c0 + cc
                    tk = slice(c * 128, (c + 1) * 128)
                    zero_f = (c == 16); zero_b = (c == 17)
                    outv = po[:, cc * 128:(cc + 1) * 128]
                    self.mm(outv, VT[:, c, :].k(ti), ATM[:, cc, :], start=True, stop=(zero_f and zero_b))
                    if not zero_f:
                        self.mm(outv, SFb[:, c, :], QF[:, tk].k(ti), start=False, stop=zero_b)
                    if not zero_b:
                        self.mm(outv, SBb[:, c, :], QB[:, tk].k(ti), start=False, stop=True)
                self.copy("act", ORT[:, 0:nt], po[:, 0:nt])
                self.rel(po)
                self.tt("pool", SQ1[:, 0:nt], ORT[:, 0:nt], ORT[:, 0:nt], ALU.mult)
                ps = self.bank()
                self.mm(ps[:, 0:nt], self.ONES128, SQ1[:, 0:nt])
                self.rstd(RS[:, 0:nt], ps[:, 0:nt])
                self.rel(ps)
                self.stt("dve", OD[:, 0:nt], ORT[:, 0:nt], self.HG[l][:, 1:2], RS[:, 0:nt], ALU.mult, ALU.mult)
                ps = self.proj_fm(WB, 384, 128, Y, t0, nt, ti)
                self.act(GS[:, 0:nt], ps[:, 0:nt], AF.Silu)
                self.rel(ps)
                self.tt("pool", OB[h][:, t0:t0 + nt].k(ti), OD[:, 0:nt], GS[:, 0:nt], ALU.mult)
        A.free(WB, QR, QF, QB, KR, KTF, KTB, VT, GS, T4, T5, SFb, SBb, STF, STB, ATM)

        self.mark("  M3 b%d l%d" % (b, l))
        WCK = al("WCK", (8, 640), BF16)
        KD = [al("KD%d" % g, (NTOK,), BF16) for g in range(2)]
        VZ = [[al("VZ%d%d" % (g, hh), (18, 128), BF16) for hh in range(2)] for g in range(2)]
        SQK = al("SQK", (512,), BF16)
        RSK = al("RSK", (512,), F32)
        QT = al("QTc", (NTOK,), BF16)
        WCQ = al("WCQ", (8, 256), BF16)
        PTS = [al("PT%d" % i, (512,), BF16) for i in range(3)]
        RD = al("RD", (512,), F32)
        self.wload(WCK, "wCk", (l,))

        def qk_tile(W, c0, dst, gcol, t0, nt, ti, isctx):
            ps = self.proj_fm(W, c0, 128, Y, t0, nt, ti)
            self.act(SQK[:, 0:nt], ps[:, 0:nt], AF.Square)
            pss = self.bank()
            self.mm(pss[:, 0:nt], self.BD64, SQK[:, 0:nt])
            self.rstd(RSK[:, 0:nt], pss[:, 0:nt])
            self.rel(pss)
            gq = self.HG[l][:, gcol:gcol + 1]; gsw = self.HG[l][:, gcol + 1:gcol + 2]
            if not isctx:
                ps2 = self.proj_fm(W, c0 + 128, 128, Y, t0, nt, ti)
                r1, r2 = RT
                self.stt("dve", r1[:, 0:nt], ps[:, 0:nt], gq, self.COSF[:, t0:t0 + nt], ALU.mult, ALU.mult)
                self.stt("dve", r2[:, 0:nt], ps2[:, 0:nt], gsw, self.SINF[:, t0:t0 + nt], ALU.mult, ALU.mult)
                self.rel(ps, ps2)
                self.tt("pool", r1[:, 0:nt], r1[:, 0:nt], r2[:, 0:nt], ALU.add)
                self.tt("dve", dst[:, t0:t0 + nt].k(ti), r1[:, 0:nt], RSK[:, 0:nt], ALU.mult)
            else:
                self.stt("dve", dst[:, t0:t0 + nt].k(ti), ps[:, 0:nt], gq, RSK[:, 0:nt], ALU.mult, ALU.mult)
                self.rel(ps)

        for ti, (t0, nt, isctx) in enumerate(TILES):
            nch = nt // 128
            c0 = t0 // 128
            for g in range(2):
                qk_tile(WCK, g * 256, KD[g], 4, t0, nt, ti, isctx)
            ps = self.proj_tm4(WCK, 512, 128, Y, c0, nch, ti)
            pv = ps[:, 0:nch * 128]
            for g in range(2):
                for hh in range(2):
                    self.memset("pool", VZ[g][hh][:, c0:c0 + nch, :].k(ti), 0.0)
            for g in range(2):
                for hh in range(2):
                    self.copy("act" if hh == 0 else "dve", VZ[g][hh][:, c0:c0 + nch, hh * 64:(hh + 1) * 64].k(ti),
                              pv.f(lambda a, g=g: a.rearrange("p (c n) -> p c n", n=128)[:, :, g * 64:(g + 1) * 64]))
            self.rel(ps)
        for qi in range(4):
            g = qi // 2
            self.wload(WCQ, "wCq", (l, qi))
            for ti, (t0, nt, isctx) in enumerate(qtiles):
                qk_tile(WCQ, 0, QT, 2, t0, nt, ti, isctx)
            for ti, (q0, nq, isctx) in enumerate(qtiles):
                kcs = list(range(16, 18)) if isctx else list(range(18))
                O = self.bank(); DEN = self.bank()
                jobs = []
                for hh in range(2):
                    rows = slice(hh * 64, (hh + 1) * 64)
                    for kc in kcs:
                        jobs.append((KD[g][rows, kc * 128:(kc + 1) * 128].k(TOC(kc)), QT[rows, q0:q0 + nq].k(ti),
                                     VZ[g][hh][:, kc, :].k(TOC(kc)), self.ONESL if hh == 0 else self.ONESR))
                self.softmax_av(jobs, nq, O[:, 0:nq], DEN[:, 0:nq], PTS)
                self.recip(RD[:, 0:nq], DEN[:, 0:nq])
                self.tt("dve", OC[qi][:, q0:q0 + nq].k(ti), O[:, 0:nq], RD[:, 0:nq], ALU.mult)
                self.rel(O, DEN)
        A.free(WCK, *KD, VZ[0][0], VZ[0][1], VZ[1][0], VZ[1][1], SQK, RSK, QT, WCQ)
        A.free(self.COSF, self.SINF, self.COST, self.SINT, *RT, *PTS, RD, OD, SQ1)

        self.mark("  M4 b%d l%d" % (b, l))
        WX = [al("WX%d" % i, (12, 512), BF16) for i in range(2)]
        wx_rr = [0]
        def wx():
            w = WX[wx_rr[0] % 2]; wx_rr[0] += 1
            return w
        SG = [al("SG%d" % i, (512,), F32) for i in range(2)]
        TB = al("TB", (512,), F32)
        MA = al("MA", (8, 512), F32)
        MB = al("MB", (8, 512), BF16)
        HTm = al("HTm", (8, 512), F32)
        TMP = [al("TMPm%d" % i, (512,), F32) for i in range(2)]
        OBR = (OA, OB, OC)
        for ti, tile in enumerate(qtiles):
            t0, nt, isctx = tile
            j = 2 if isctx else b
            src, rd = self.h_src(b, l, s, tile)
            self.dma("sp", HTm[:, :, 0:nt], src, reads=rd, writes=[HTm])
            sgi = 0
            for i in range(3):
                for hf in range(2):
                    wg = wx()
                    self.wload(wg, "wGB", (l, i, hf))
                    for dd in range(4):
                        dc = hf * 4 + dd
                        pg = self.proj_fm(wg, dd * 128, 128, Y, t0, nt, ti)
                        sg = SG[sgi % 2]; sgi += 1
                        self.act(sg[:, 0:nt], pg[:, 0:nt], AF.Sigmoid, bias=self.BG[l][:, i * 8 + dc:i * 8 + dc + 1])
                        self.rel(pg)
                        pb = self.bank()
                        for kc in range(4):
                            self.mm(pb[:, 0:nt], wg[:, 8 + kc, dd * 128:(dd + 1) * 128], OBR[i][kc][:, t0:t0 + nt].k(ti),
                                    start=(kc == 0), stop=(kc == 3))
                        if i == 0:
                            self.tt("dve", MA[:, dc, 0:nt], sg[:, 0:nt], pb[:, 0:nt], ALU.mult)
                        else:
                            self.tt("dve", TB[:, 0:nt], sg[:, 0:nt], pb[:, 0:nt], ALU.mult)
                            if i == 1:
                                self.tt("pool", MA[:, dc, 0:nt], MA[:, dc, 0:nt], TB[:, 0:nt], ALU.add)
                            else:
                                self.tt("pool", MB[:, dc, 0:nt], MA[:, dc, 0:nt], TB[:, 0:nt], ALU.add)
                        self.rel(pb)
            for hf in range(2):
                wo_ = wx()
                self.wload(wo_[:, 0:8, :], "wO", (l, hf))
                for dd in range(4):
                    dc = hf * 4 + dd
                    po = self.bank()
                    for k in range(8):
                        self.mm(po[:, 0:nt], wo_[:, k, dd * 128:(dd + 1) * 128], MB[:, k, 0:nt], start=(k == 0), stop=(k == 7))
                    self.copy("act", MA[:, dc, 0:nt], po[:, 0:nt])
                    self.rel(po)
            self.postnorm_residual(HTm, MA, MB, RS, TMP, nt, l, s, j)
            dst, wr = self.h_dst(b, l, s, tile)
            self.dma("pool", dst, HTm[:, :, 0:nt], reads=[HTm], writes=wr)
        if not need_ctx:
            pass
        A.free(*WX, *SG, TB, MA, MB, HTm, *TMP, RS, Y, *OA, *OB, *OC)

def _swap64(w):
    k, n = w.shape
    return w.reshape(k, n // 64, 2, 32)[:, :, ::-1, :].reshape(k, n)


def _blk(w):
    k, n = w.shape
    return np.ascontiguousarray(w.reshape(k // 128, 128, n).transpose(1, 0, 2)).reshape(128, (k // 128) * n)


def rope_tables():
    rows = SEQ // 64
    row = np.repeat(np.arange(rows, dtype=np.float32), 64)
    col = np.tile(np.arange(64, dtype=np.float32), rows)
    n_freq = 16
    inv = (10000.0 ** (-np.arange(n_freq, dtype=np.float32) / n_freq)).astype(np.float32)
    ang = np.concatenate([row[:, None] * inv, col[:, None] * inv], axis=-1).astype(np.float32)
    cos, sin = np.cos(ang).astype(np.float32), np.sin(ang).astype(np.float32)
    d = np.arange(64)
    cos64 = cos[:, d % 32]
    sin64 = sin[:, d % 32] * np.where(d < 32, -1.0, 1.0).astype(np.float32)
    cosF = np.ascontiguousarray(np.concatenate([cos64, cos64], axis=1).T)
    sinF = np.ascontiguousarray(np.concatenate([sin64, sin64], axis=1).T)
    cosT = np.ascontiguousarray(cos64.reshape(16, 128, 64).transpose(1, 0, 2)).reshape(128, 16 * 64)
    sinT = np.ascontiguousarray(sin64.reshape(16, 128, 64).transpose(1, 0, 2)).reshape(128, 16 * 64)
    return cosF, sinF, cosT, sinT


def shared_inputs(w_mod, b_mod, norm_g, w_ffn_in, w_ffn_out, w_in, b_gate, diff_lambda, diff_norm_g,
                  ret_decay_logit, ret_norm_g, qk_norm_g, w_branch, w_out):
    f = np.float32
    S = {}
    S["wmod"] = np.ascontiguousarray(w_mod, dtype=f)
    S["bmodT"] = np.ascontiguousarray(b_mod.reshape(DEPTH, 72, 128).transpose(0, 2, 1), dtype=f)
    S["normgT"] = np.ascontiguousarray(norm_g.reshape(DEPTH, 6, 8, 128).transpose(0, 3, 1, 2), dtype=f)
    wi = np.empty((DEPTH, 2, NJ // 2, 128, 2 * 8 * 256), f)
    wo = np.empty((DEPTH, 2, 4, 128, 2 * NJ * 128), f)
    for l in range(DEPTH):
        for s in range(2):
            w = w_ffn_in[l, s]
            a = w[:, :DFF].reshape(8, 128, NJ, 128); u = w[:, DFF:].reshape(8, 128, NJ, 128)
            au = np.stack([a, u], axis=3)
            au = au.transpose(2, 1, 0, 3, 4)
            au = au.reshape(NJ // 2, 2, 128, 8, 256).transpose(0, 2, 1, 3, 4)
            wi[l, s] = au.reshape(NJ // 2, 128, 2 * 8 * 256)
            w2 = w_ffn_out[l, s].reshape(NJ, 128, 8, 128)
            w2 = w2.transpose(2, 1, 0, 3)
            w2 = w2.reshape(4, 2, 128, NJ, 128).transpose(0, 2, 1, 3, 4)
            wo[l, s] = w2.reshape(4, 128, 2 * NJ * 128)
    S["wi"] = wi; S["wo"] = wo
    sizes = (512, 512, 512, 256, 256, 512, 512, 512, 128, 128, 3072)
    names = ('a_q', 'a_k', 'a_v', 'b_q', 'b_k', 'b_v', 'b_g', 'c_q', 'c_k', 'c_v', 'gate')
    sp = {}; lo = 0
    for nm, sz in zip(names, sizes):
        sp[nm] = (lo, lo + sz); lo += sz
    wA = np.empty((DEPTH, 4, 128, 8 * 640), f); wB = np.empty((DEPTH, 4, 128, 8 * 512), f)
    wCk = np.empty((DEPTH, 128, 8 * 640), f); wCq = np.empty((DEPTH, 4, 128, 8 * 256), f)
    wGB = np.empty((DEPTH, 3, 2, 128, 12 * 512), f)
    wO = np.empty((DEPTH, 2, 128, 8 * 512), f)
    for l in range(DEPTH):
        W = w_in[l]
        col = lambda nm, a, n: W[:, sp[nm][0] + a: sp[nm][0] + a + n]
        for h in range(4):
            q = col('a_q', h * 128, 128); k = col('a_k', h * 128, 128); v = col('a_v', h * 128, 128)
            wA[l, h] = _blk(np.concatenate([q, _swap64(q), k, _swap64(k), v], axis=1))
            q = col('b_q', h * 64, 64); k = col('b_k', h * 64, 64)
            v = col('b_v', h * 128, 128); g = col('b_g', h * 128, 128)
            wB[l, h] = _blk(np.concatenate([q, _swap64(q), k, _swap64(k), v, g], axis=1))
        k0 = col('c_k', 0, 64); k1 = col('c_k', 64, 64); v = col('c_v', 0, 128)
        wCk[l] = _blk(np.concatenate([k0, k0, _swap64(k0), _swap64(k0), k1, k1, _swap64(k1), _swap64(k1), v], axis=1))
        for qi in range(4):
            q = col('c_q', qi * 128, 128)
            wCq[l, qi] = _blk(np.concatenate([q, _swap64(q)], axis=1))
        for i in range(3):
            for hf in range(2):
                wb = w_branch[l, i][:, hf * 512:(hf + 1) * 512]
                wGB[l, i, hf] = np.concatenate([_blk(col('gate', i * 1024 + hf * 512, 512)), _blk(wb)], axis=1)
        for hf in range(2):
            wO[l, hf] = _blk(w_out[l][:, hf * 512:(hf + 1) * 512])
    S.update(wA=wA, wB=wB, wCk=wCk, wCq=wCq, wGB=wGB, wO=wO)
    S["bgT"] = np.ascontiguousarray(b_gate.reshape(DEPTH, 24, 128).transpose(0, 2, 1), dtype=f)
    S["lamb"] = np.ascontiguousarray(np.broadcast_to(diff_lambda.reshape(DEPTH, 1, 256), (DEPTH, 128, 256)), dtype=f)
    hg = np.empty((DEPTH, 128, 6), f)
    sw = (np.arange(64) + 32) % 64
    for l in range(DEPTH):
        hg[l, :, 0] = diff_norm_g[l]
        hg[l, :, 1] = ret_norm_g[l]
        hg[l, :, 2] = np.tile(qk_norm_g[l, 0], 2)
        hg[l, :, 3] = np.tile(qk_norm_g[l, 0][sw], 2)
        hg[l, :, 4] = np.tile(qk_norm_g[l, 1], 2)
        hg[l, :, 5] = np.tile(qk_norm_g[l, 1][sw], 2)
    S["hg"] = hg
    S["retl"] = np.ascontiguousarray(np.broadcast_to(ret_decay_logit.reshape(DEPTH, 1, 8), (DEPTH, 128, 8)), dtype=f)
    cosF, sinF, cosT, sinT = rope_tables()
    S.update(cosF=cosF, sinF=sinF, cosT=cosT, sinT=sinT)
    i = np.arange(128, dtype=f)
    S["rel"] = np.ascontiguousarray(i[None, :] - i[:, None])
    S["posf"] = np.ascontiguousarray(np.broadcast_to(i[None, :], (128, 128)))
    S["posp"] = np.ascontiguousarray(i[:, None])
    return S


def core_inputs(x, c, ctx, c_ctx, core):
    f = np.float32
    b0 = core * BPC
    M = {}
    M["xT"] = np.ascontiguousarray(x[b0:b0 + BPC].transpose(0, 2, 1), dtype=f)
    M["ctxT"] = np.ascontiguousarray(ctx[b0:b0 + BPC].transpose(0, 2, 1), dtype=f)
    cc = np.stack([c[b0], c[b0 + 1], c_ctx], axis=1)
    M["cT"] = np.ascontiguousarray(cc.reshape(8, 128, 3).transpose(1, 0, 2), dtype=f)
    return M


_CACHE = {}


def get_program(nsub=6):
    if nsub not in _CACHE:
        bld = Builder(nsub=nsub)
        with bld.stack:
            bld.build()
        _CACHE[nsub] = bld
    return _CACHE[nsub]


def kernel(x, c, ctx, c_ctx, w_mod, b_mod, norm_g, w_ffn_in, w_ffn_out, w_in, b_gate,
           diff_lambda, diff_norm_g, ret_decay_logit, ret_norm_g, qk_norm_g, w_branch, w_out, _nsub=6):
    args = [np.asarray(a) for a in (w_mod, b_mod, norm_g, w_ffn_in, w_ffn_out, w_in, b_gate, diff_lambda,
                                    diff_norm_g, ret_decay_logit, ret_norm_g, qk_norm_g, w_branch, w_out)]
    S = shared_inputs(*args)
    x = np.asarray(x); c = np.asarray(c); ctx = np.asarray(ctx); c_ctx = np.asarray(c_ctx)
    in_maps = []
    for core in range(N_CORES):
        m = dict(S)
        m.update(core_inputs(x, c, ctx, c_ctx, core))
        in_maps.append(m)
    bld = get_program(_nsub)
    res = run_bass_kernel_spmd(bld.nc, in_maps, core_ids=list(range(N_CORES)))
    out = np.empty((N_CORES * BPC, SEQ, D), np.float32)
    for core in range(N_CORES):
        o = res.results[core]["outT"]
        out[core * BPC:(core + 1) * BPC] = o.transpose(0, 2, 1)
    return out
```

```python
import math
from contextlib import ExitStack

import numpy as np
import concourse.bass as bass
import concourse.mybir as mybir
from concourse.bass_utils import run_bass_kernel_spmd

F32 = mybir.dt.float32
BF16 = mybir.dt.bfloat16
ALU = mybir.AluOpType
AF = mybir.ActivationFunctionType
AX = mybir.AxisListType

D = 1024
SEQ = 2048
CTX = 256
NTOK = SEQ + CTX
DEPTH = 2
DFF = 2816
NJ = DFF // 128
EPS = 1e-6
N_CORES = 8
BPC = 2

ENGS = ("pe", "act", "dve", "pool", "sp")
N_DMA_SEMS = 24

TILES = [(0, 512, False), (512, 512, False), (1024, 512, False), (1536, 512, False), (2048, 256, True)]


class Buf:
    def __init__(self, name, ap_fn):
        self.name = name
        self.ap_fn = ap_fn
        self.st = {}

    def ap(self):
        return self.ap_fn()

    def __getitem__(self, idx):
        return V(self, idx)

    def k(self, key):
        return V(self, None, key)


class V:
    def __init__(self, buf, idx=None, key=None, post=None):
        self.buf = buf
        self.idx = idx
        self.key = key
        self.post = post

    def k(self, key):
        return V(self.buf, self.idx, key, self.post)

    def f(self, post):
        return V(self.buf, self.idx, self.key, post)

    def bc(self, axis, shape):
        return V(self.buf, self.idx, self.key, lambda a: a.unsqueeze(axis).to_broadcast(list(shape)))

    def ap(self):
        a = self.buf.ap()
        if self.idx is not None:
            a = a[self.idx]
        if self.post is not None:
            a = self.post(a)
        return a

    def dep(self):
        return (self.buf, self.key)


def _dep(x):
    if isinstance(x, Buf):
        return (x, None)
    if isinstance(x, V):
        return x.dep()
    return x


def _ap(x):
    return x.ap() if isinstance(x, (Buf, V)) else x


class Op:
    __slots__ = ("eng", "fn", "waits", "idx", "signal", "dma", "tok")


class Prog:
    def __init__(self, nc):
        self.nc = nc
        self.ops = {e: [] for e in ENGS}
        self.dma_rr = 0
        self.dma_cnt = [0] * N_DMA_SEMS
        self.dma_last = [None] * N_DMA_SEMS
        self.seen = {e: {} for e in ENGS}

    def _need(self, eng, tok, waits):
        if tok is None:
            return
        if tok[0] == 'e':
            _, e2, idx = tok
            if e2 == eng and e2 == 'pe':
                return
            key = ('e', e2)
            if self.seen[eng].get(key, -1) >= idx:
                return
            self.seen[eng][key] = idx
            waits.append(tok)
        else:
            _, k, val = tok
            key = ('d', k)
            if self.seen[eng].get(key, -1) >= val:
                return
            self.seen[eng][key] = val
            waits.append(tok)

    @staticmethod
    def _state(buf, key):
        st = buf.st.get(key)
        if st is None:
            st = {'w': None, 'r': []}
            buf.st[key] = st
        return st

    def op(self, eng, fn, reads=(), writes=(), dma=False):
        o = Op()
        o.eng = eng
        o.fn = fn
        o.idx = len(self.ops[eng])
        o.signal = False
        o.dma = dma
        waits = []
        rd = [_dep(x) for x in reads]
        wr = [_dep(x) for x in writes]
        for b, _k in wr:
            for t in getattr(b, 'alias_toks', ()):
                self._need(eng, t, waits)
        if dma:
            k = self.dma_rr
            self.dma_rr = (self.dma_rr + 1) % N_DMA_SEMS
            self._need(eng, self.dma_last[k], waits)
            self.dma_cnt[k] += 16
            tok = ('d', k, self.dma_cnt[k])
            self.dma_last[k] = tok
        else:
            tok = ('e', eng, o.idx)
        o.tok = tok
        for b, key in rd:
            self._need(eng, self._state(b, key)['w'], waits)
        for b, key in wr:
            st = self._state(b, key)
            self._need(eng, st['w'], waits)
            for r in st['r']:
                self._need(eng, r, waits)
        for b, key in rd:
            rl = self._state(b, key)['r']
            if tok[0] == 'e':
                rl[:] = [r for r in rl if not (r[0] == 'e' and r[1] == tok[1])]
            rl.append(tok)
        for b, key in wr:
            st = self._state(b, key)
            st['w'] = tok
            st['r'] = []
        o.waits = waits
        for w in waits:
            if w[0] == 'e':
                self.ops[w[1]][w[2]].signal = True
        self.ops[eng].append(o)
        return o

    def inherit(self, newbuf, oldbufs):
        best = {}
        def add(t):
            if t is None:
                return
            key = (t[0], t[1])
            if key not in best or best[key][2] < t[2]:
                best[key] = t
        for t in getattr(newbuf, 'alias_toks', ()):
            add(t)
        for ob in oldbufs:
            for t in getattr(ob, 'alias_toks', ()):
                add(t)
            for s2 in ob.st.values():
                add(s2['w'])
                for r in s2['r']:
                    add(r)
        newbuf.alias_toks = list(best.values())

    def final_wait_all(self, eng='sp'):
        waits = []
        for k in range(N_DMA_SEMS):
            self._need(eng, self.dma_last[k], waits)
        for e in ENGS:
            if e != eng:
                nd = [o for o in self.ops[e] if not o.dma and o.fn is not None]
                if nd:
                    self._need(eng, ('e', e, nd[-1].idx), waits)
        o = Op()
        o.eng = eng; o.fn = None; o.idx = len(self.ops[eng]); o.signal = False
        o.dma = False; o.tok = ('e', eng, o.idx); o.waits = waits
        for w in waits:
            if w[0] == 'e':
                self.ops[w[1]][w[2]].signal = True
        self.ops[eng].append(o)

    def emit(self, stack):
        nc = self.nc
        esem = {e: stack.enter_context(nc.semaphore("s_" + e)) for e in ENGS}
        dsem = [stack.enter_context(nc.semaphore("d%d" % k)) for k in range(N_DMA_SEMS)]
        sigval = {}
        for e in ENGS:
            c = 0
            for o in self.ops[e]:
                if o.signal:
                    c += 1
                    sigval[(e, o.idx)] = c
        block = stack.enter_context(nc.Block())

        def run(engname, engobj):
            for o in self.ops[engname]:
                for w in o.waits:
                    if w[0] == 'e':
                        engobj.wait_ge(esem[w[1]], sigval[(w[1], w[2])])
                    else:
                        engobj.wait_ge(dsem[w[1]], w[2])
                if o.fn is None:
                    continue
                ins = o.fn(engobj)
                if o.dma:
                    ins.then_inc(dsem[o.tok[1]], 16)
                elif o.signal:
                    ins.then_inc(esem[engname], 1)

        @block.tensor
        def _(e):
            run("pe", e)

        @block.scalar
        def _(e):
            run("act", e)

        @block.vector
        def _(e):
            run("dve", e)

        @block.gpsimd
        def _(e):
            run("pool", e)

        @block.sync
        def _(e):
            run("sp", e)


class Arena:
    def __init__(self, P, arena_t, nbytes):
        self.P = P
        self.t = arena_t
        self.n = nbytes
        self.live = []
        self.dead = []
        self.peak = 0

    def alloc(self, name, shape, dt, parts=128):
        esz = 2 if dt == BF16 else 4
        nbytes = int(np.prod(shape)) * esz
        size = (nbytes + 63) // 64 * 64
        self.live.sort(key=lambda x: x[0])
        pos = 0
        for (s, e, _) in self.live:
            if s - pos >= size:
                break
            pos = max(pos, e)
        if pos + size > self.n:
            raise MemoryError("arena full allocating %s (%d bytes); live=%s" % (
                name, size, [(b.name, e - s) for s, e, b in self.live]))
        t = self.t
        shape = tuple(shape)

        def apf(pos=pos, nbytes=nbytes):
            a = t[0:parts, pos // 4:(pos + nbytes) // 4]
            if dt != F32:
                a = a.bitcast(dt)
            if len(shape) == 2:
                a = a.rearrange("p (a b) -> p a b", a=shape[0])
            elif len(shape) == 3:
                a = a.rearrange("p (a b c) -> p a b c", a=shape[0], b=shape[1])
            return a

        buf = Buf(name, apf)
        olds = [b for (s, e, b) in self.dead if s < pos + size and e > pos]
        if olds:
            self.P.inherit(buf, olds)
        self.dead = [(s, e, b) for (s, e, b) in self.dead if not (s >= pos and e <= pos + size)]
        self.live.append((pos, pos + size, buf))
        self.peak = max(self.peak, pos + size)
        return buf

    def free(self, *bufs):
        for buf in bufs:
            for i, (s, e, b) in enumerate(self.live):
                if b is buf:
                    self.dead.append(self.live.pop(i))
                    break
            else:
                raise KeyError(buf.name)


class Builder:
    def __init__(self, nsub=6 * DEPTH):
        self.nsub = nsub
        self.nc = bass.Bass("TRN2", target_bir_lowering=False)
        self.P = Prog(self.nc)
        self.stack = ExitStack()

    def op(self, eng, fn, reads=(), writes=(), dma=False):
        return self.P.op(eng, fn, reads=reads, writes=writes, dma=dma)

    def mark(self, name):
        if not hasattr(self, 'marks'):
            self.marks = []
        self.marks.append((name, {e: len(self.P.ops[e]) for e in ENGS}))

    def mm(self, out, lhsT, rhs, start=True, stop=True):
        self.op("pe", lambda e: e.matmul(_ap(out), lhsT=_ap(lhsT), rhs=_ap(rhs), start=start, stop=stop),
                reads=[lhsT, rhs], writes=[out])

    def act(self, out, in_, func, bias=None, scale=1.0, extra_reads=()):
        def fn(e):
            kw = {}
            if bias is not None:
                kw['bias'] = _ap(bias)
            return e.activation(out=_ap(out), in_=_ap(in_), func=func, scale=_ap(scale), **kw)
        rd = [in_] + [x for x in (bias, scale) if isinstance(x, (Buf, V))] + list(extra_reads)
        self.op("act", fn, reads=rd, writes=[out])

    def tt(self, eng, out, in0, in1, op):
        self.op(eng, lambda e: e.tensor_tensor(out=_ap(out), in0=_ap(in0), in1=_ap(in1), op=op),
                reads=[in0, in1], writes=[out])

    def stt(self, eng, out, in0, scalar, in1, op0, op1):
        rd = [in0, in1] + ([scalar] if isinstance(scalar, (Buf, V)) else [])
        self.op(eng, lambda e: e.scalar_tensor_tensor(out=_ap(out), in0=_ap(in0), scalar=_ap(scalar),
                                                      in1=_ap(in1), op0=op0, op1=op1),
                reads=rd, writes=[out])

    def ts(self, eng, out, in0, s1, op0, s2=None, op1=None):
        rd = [in0] + [x for x in (s1, s2) if isinstance(x, (Buf, V))]
        def fn(e):
            if op1 is None:
                return e.tensor_scalar(out=_ap(out), in0=_ap(in0), scalar1=_ap(s1), scalar2=None, op0=op0)
            return e.tensor_scalar(out=_ap(out), in0=_ap(in0), scalar1=_ap(s1), scalar2=_ap(s2), op0=op0, op1=op1)
        self.op(eng, fn, reads=rd, writes=[out])

    def copy(self, eng, out, in_):
        if eng == "act":
            self.op("act", lambda e: e.copy(out=_ap(out), in_=_ap(in_)), reads=[in_], writes=[out])
        else:
            self.op(eng, lambda e: e.tensor_copy(out=_ap(out), in_=_ap(in_)), reads=[in_], writes=[out])

    def recip(self, out, in_):
        self.op("dve", lambda e: e.reciprocal(out=_ap(out), in_=_ap(in_)), reads=[in_], writes=[out])

    def memset(self, eng, out, val):
        self.op(eng, lambda e: e.memset(_ap(out), val), writes=[out])

    def dma(self, eng, out, in_, reads=(), writes=()):
        self.op(eng, lambda e: e.dma_start(out=_ap(out), in_=_ap(in_)),
                reads=list(reads), writes=list(writes), dma=True)

    def bank(self):
        return self.free_banks.pop(0)

    def rel(self, *banks):
        for b in banks:
            assert b not in self.free_banks
            self.free_banks.append(b)

    def rstd(self, RS, ps):
        self.act(RS, ps, AF.Sqrt, bias=self.EPSB[:, 0:1])
        self.recip(RS, RS)

    def build(self):
        nc = self.nc
        dt = lambda name, shape, kind=None, dty=F32: (
            nc.dram_tensor(name, list(shape), dty, kind=kind) if kind else nc.dram_tensor(name, list(shape), dty))
        I = {}
        def inp(name, shape):
            I[name] = dt(name, shape, "ExternalInput").ap()
        inp("xT", (BPC, D, SEQ)); inp("ctxT", (BPC, D, CTX)); inp("cT", (128, 8, 3))
        inp("wmod", (DEPTH, D, 9 * D)); inp("bmodT", (DEPTH, 128, 72)); inp("normgT", (DEPTH, 128, 6, 8))
        inp("wi", (DEPTH, 2, NJ // 2, 128, 2 * 8 * 256)); inp("wo", (DEPTH, 2, 4, 128, 2 * NJ * 128))
        inp("wA", (DEPTH, 4, 128, 8 * 640)); inp("wB", (DEPTH, 4, 128, 8 * 512))
        inp("wCk", (DEPTH, 128, 8 * 640)); inp("wCq", (DEPTH, 4, 128, 8 * 256))
        inp("wGB", (DEPTH, 3, 2, 128, 12 * 512)); inp("wO", (DEPTH, 2, 128, 8 * 512))
        inp("bgT", (DEPTH, 128, 24)); inp("lamb", (DEPTH, 128, 256)); inp("hg", (DEPTH, 128, 6))
        inp("retl", (DEPTH, 128, 8))
        inp("cosF", (128, SEQ)); inp("sinF", (128, SEQ)); inp("cosT", (128, 16 * 64)); inp("sinT", (128, 16 * 64))
        inp("rel", (128, 128)); inp("posf", (128, 128)); inp("posp", (128, 1))
        self.I = I
        self.outT = dt("outT", (BPC, D, SEQ), "ExternalOutput").ap()
        Wb = {}
        for nme in ("wi", "wo", "wA", "wB", "wCk", "wCq", "wGB", "wO"):
            Wb[nme] = dt(nme + "_bf", I[nme].shape, None, BF16).ap()
        self.Wb = Wb
        self.wbuf = {}
        self.hS = dt("hS", (BPC, D, NTOK), None, F32).ap()
        self.hSbuf = Buf("hS", lambda: self.hS)

        stack = self.stack
        ARENA_BYTES = 206 * 1024
        arena_t = stack.enter_context(nc.sbuf_tensor("arena", [128, ARENA_BYTES // 4], F32))
        self.A = Arena(self.P, arena_t, ARENA_BYTES)
        pst = [stack.enter_context(nc.psum_tensor("ps%d" % i, [128, 512], F32)) for i in range(8)]
        self.PS = [Buf("ps%d" % i, (lambda t=t: t[:, :])) for i, t in enumerate(pst)]
        self.free_banks = list(self.PS)

        self.precast()
        self.consts()
        self.modulation()
        isub = 0
        for b in range(BPC):
            for l in range(DEPTH):
                for s in range(3):
                    if isub >= self.nsub * BPC and False:
                        pass
                    if self.sub_count(b, l, s) >= self.nsub:
                        continue
                    self.mark("b%d l%d s%d" % (b, l, s))
                    if s == 1:
                        self.mixer(b, l)
                    else:
                        self.ffn(b, l, s)
        self.mark("end")
        self.P.final_wait_all("sp")
        self.P.emit(stack)

    def sub_count(self, b, l, s):
        return l * 3 + s

    def h_src(self, b, l, s, tile):
        t0, nt, isctx = tile
        first = (l == 0 and s == 0)
        if first:
            if isctx:
                return self.I["ctxT"][b].rearrange("(k p) n -> p k n", p=128), []
            return self.I["xT"][b][:, t0:t0 + nt].rearrange("(k p) n -> p k n", p=128), []
        return self.hS[b][:, t0:t0 + nt].rearrange("(k p) n -> p k n", p=128), [self.hSbuf.k((b, t0))]

    def h_dst(self, b, l, s, tile):
        t0, nt, isctx = tile
        last = (l * 3 + s == self.nsub - 1)
        if last and not isctx:
            return self.outT[b][:, t0:t0 + nt].rearrange("(k p) n -> p k n", p=128), []
        return self.hS[b][:, t0:t0 + nt].rearrange("(k p) n -> p k n", p=128), [self.hSbuf.k((b, t0))]

    def precast(self):
        I, Wb = self.I, self.Wb
        def pc(name, idx):
            key = (name,) + tuple(idx)
            buf = Buf("wb_%s_%s" % (name, idx), None)
            self.wbuf[key] = buf
            src = I[name]; dst = Wb[name]
            for i in idx:
                src = src[i]; dst = dst[i]
            self.dma("pool", dst, src, writes=[buf])
        for l in range(DEPTH):
            for jg in range(NJ // 2):
                pc("wi", (l, 0, jg))
            for dg in range(4):
                pc("wo", (l, 0, dg))
            for h in range(4):
                pc("wA", (l, h))
            for h in range(4):
                pc("wB", (l, h))
            pc("wCk", (l,))
            for q in range(4):
                pc("wCq", (l, q))
            for i in range(3):
                for hf in range(2):
                    pc("wGB", (l, i, hf))
            for hf in range(2):
                pc("wO", (l, hf))
            for jg in range(NJ // 2):
                pc("wi", (l, 1, jg))
            for dg in range(4):
                pc("wo", (l, 1, dg))

    def wload(self, dst, name, idx):
        src = self.Wb[name]
        for i in idx:
            src = src[i]
        d = _ap
        def fn(e, src=src):
            o = _ap(dst)
            nd = len(o.shape)
            if nd == 3:
                o2 = o.rearrange("p a b -> p (a b)")
            elif nd == 4:
                o2 = o.rearrange("p a b c -> p (a b c)")
            else:
                o2 = o
            return e.dma_start(out=o2, in_=src)
        self.op("sp", fn, reads=[self.wbuf[(name,) + tuple(idx)]], writes=[dst], dma=True)

    def consts(self):
        A, I = self.A, self.I
        al = A.alloc
        self.EPSB = al("EPSB", (1,), F32)
        self.memset("dve", self.EPSB, EPS)
        self.LN8 = al("LN8", (1,), F32)
        self.memset("dve", self.LN8, math.log(0.125))
        self.ONESM = al("ONESM", (128,), BF16)
        self.memset("dve", self.ONESM, 1.0 / 1024)
        self.ONES128 = al("ONES128", (128,), BF16)
        self.memset("dve", self.ONES128, 1.0 / 128)
        self.ONES1 = al("ONES1", (128,), BF16)
        self.memset("dve", self.ONES1, 1.0)
        self.ONESL = al("ONESL", (128,), BF16)
        self.ONESR = al("ONESR", (128,), BF16)
        self.memset("dve", self.ONESL, 0.0)
        self.memset("dve", self.ONESR, 0.0)
        self.memset("dve", self.ONESL[:, 0:64], 1.0)
        self.memset("dve", self.ONESR[:, 64:128], 1.0)
        self.BD64 = al("BD64", (128,), BF16)
        self.memset("dve", self.BD64, 0.0)
        self.memset("dve", self.BD64[0:64, 0:64], 1.0 / 64)
        self.memset("dve", self.BD64[64:128, 64:128], 1.0 / 64)
        def ld(name, shape, src):
            b = al(name, shape, F32)
            self.dma("sp", b, src, writes=[b])
            return b
        self.CT = ld("CT", (8, 3), I["cT"])
        self.BMOD = [ld("BMOD%d" % l, (72,), I["bmodT"][l]) for l in range(DEPTH)]
        self.NG = [ld("NG%d" % l, (6, 8), I["normgT"][l]) for l in range(DEPTH)]
        self.BG = [ld("BG%d" % l, (24,), I["bgT"][l]) for l in range(DEPTH)]
        self.HG = [ld("HG%d" % l, (6,), I["hg"][l]) for l in range(DEPTH)]
        self.layer_consts()

    def layer_consts(self):
        A, I = self.A, self.I
        al = A.alloc
        LAMB = [al("LAMB%d" % l, (256,), F32) for l in range(DEPTH)]
        RETL = [al("RETL%d" % l, (8,), F32) for l in range(DEPTH)]
        REL = al("REL", (128,), F32); POSF = al("POSF", (128,), F32); POSP = al("POSP", (1,), F32)
        for l in range(DEPTH):
            self.dma("sp", LAMB[l], I["lamb"][l], writes=[LAMB[l]])
            self.dma("sp", RETL[l], I["retl"][l], writes=[RETL[l]])
        self.dma("sp", REL, I["rel"], writes=[REL])
        self.dma("sp", POSF, I["posf"], writes=[POSF])
        self.dma("sp", POSP, I["posp"], writes=[POSP])
        T1 = al("lcT1", (128,), F32); T2 = al("lcT2", (128,), F32); T3 = al("lcT3", (128,), F32)
        RP = al("RELP", (128,), F32); RN = al("RELN", (128,), F32)
        GE = al("RELGE", (128,), F32); LE = al("RELLE", (128,), F32)
        self.ts("dve", RP, REL, 0.0, ALU.max)
        self.ts("dve", RN, REL, -1.0, ALU.mult, 0.0, ALU.max)
        self.ts("dve", GE, REL, 0.0, ALU.is_ge)
        self.ts("dve", LE, REL, 0.0, ALU.is_le)
        P1 = al("POSF1", (128,), F32)
        P2 = al("POS128", (128,), F32)
        self.ts("dve", P1, POSF, 1.0, ALU.add)
        self.ts("dve", P2, POSF, -1.0, ALU.mult, 128.0, ALU.add)
        PP1 = al("POSP127", (1,), F32)
        self.ts("dve", PP1, POSP, -1.0, ALU.mult, 127.0, ALU.add)
        self.NEGLAM, self.DNG, self.LG = [], [], []
        self.MASK4, self.QDF4, self.QDB4, self.KDF, self.KDB, self.SDEC = {}, {}, {}, {}, {}, {}
        for l in range(DEPTH):
            lam_init = 0.8 - 0.6 * math.exp(-0.3 * l)
            PR = al("lamPR", (2, 64), F32); SM = al("lamSM", (2,), F32)
            self.tt("dve", PR[:, 0, :], LAMB[l][:, 0:64], LAMB[l][:, 64:128], ALU.mult)
            self.tt("dve", PR[:, 1, :], LAMB[l][:, 128:192], LAMB[l][:, 192:256], ALU.mult)
            self.op("dve", lambda e, SM=SM, PR=PR: e.reduce_sum(out=SM.ap(), in_=PR.ap(), axis=AX.X),
                    reads=[PR], writes=[SM])
            self.act(SM, SM, AF.Exp)
            NL = al("NEGLAM%d" % l, (1,), F32)
            self.tt("dve", NL, SM[:, 1:2], SM[:, 0:1], ALU.subtract)
            self.ts("dve", NL, NL, -lam_init, ALU.add)
            self.NEGLAM.append(NL)
            DG = al("DNG%d" % l, (1,), F32)
            self.ts("dve", DG, self.HG[l][:, 0:1], 1.0 - lam_init, ALU.mult)
            self.DNG.append(DG)
            A.free(PR, SM)
            LGt = al("LG%d" % l, (8,), F32)
            self.act(LGt, RETL[l], AF.Exp, scale=-1.0)
            self.ts("dve", LGt, LGt, 1.0, ALU.add)
            self.act(LGt, LGt, AF.Ln)
            self.ts("dve", LGt, LGt, -1.0, ALU.mult)
            self.LG.append(LGt)
            for h in range(4):
                lgf = LGt[:, h:h + 1]; lgb = LGt[:, 4 + h:5 + h]
                M4 = al("MASK_%d_%d" % (l, h), (128,), F32)
                self.act(T1, RP, AF.Exp, scale=lgf)
                self.tt("dve", T1, T1, GE, ALU.mult)
                self.act(T2, RN, AF.Exp, scale=lgb)
                self.tt("dve", T2, T2, LE, ALU.mult)
                self.tt("dve", T3, T1, T2, ALU.add)
                self.ts("dve", M4, T3, 0.125, ALU.mult)
                self.MASK4[(l, h)] = M4
                QF = al("QDF_%d_%d" % (l, h), (128,), F32, parts=64)
                QB = al("QDB_%d_%d" % (l, h), (128,), F32, parts=64)
                self.act(QF, P1[0:64, :], AF.Exp, bias=self.LN8[0:64, 0:1], scale=LGt[0:64, h:h + 1])
                self.act(QB, P2[0:64, :], AF.Exp, bias=self.LN8[0:64, 0:1], scale=LGt[0:64, 4 + h:5 + h])
                self.QDF4[(l, h)] = QF; self.QDB4[(l, h)] = QB
                KF = al("KDF_%d_%d" % (l, h), (1,), F32); KB = al("KDB_%d_%d" % (l, h), (1,), F32)
                self.act(KF, PP1, AF.Exp, scale=lgf)
                self.act(KB, POSP, AF.Exp, scale=lgb)
                self.KDF[(l, h)] = KF; self.KDB[(l, h)] = KB
                SD = al("SDEC_%d_%d" % (l, h), (2,), F32)
                self.act(SD[:, 0:1], lgf, AF.Exp, scale=128.0)
                self.act(SD[:, 1:2], lgb, AF.Exp, scale=128.0)
                self.SDEC[(l, h)] = SD
        A.free(T1, T2, T3, RP, RN, GE, LE, P1, P2, PP1, REL, POSF, POSP)
        A.free(*LAMB); A.free(*RETL)

    def modulation(self):
        A, I = self.A, self.I
        al = A.alloc
        SC = al("SC", (8, 3), BF16)
        self.act(SC, self.CT, AF.Silu)
        WM = [al("WM%d" % i, (8, 512), BF16) for i in range(3)]
        self.MOD = []
        self.AA, self.GG = {}, {}
        for l in range(DEPTH):
            MOD = al("MOD%d" % l, (72, 3), F32)
            ps = self.bank()
            for blk in range(18):
                wm = WM[(l * 18 + blk) % 3]
                src = I["wmod"][l][:, blk * 512:(blk + 1) * 512].rearrange("(k p) n -> p k n", p=128)
                self.dma("pool", wm, src, writes=[wm])
                for n in range(4):
                    c = blk * 4 + n
                    for k in range(8):
                        self.mm(ps[:, c * 3:c * 3 + 3], wm[:, k, n * 128:(n + 1) * 128], SC[:, k, :],
                                start=(k == 0), stop=(k == 7))
            for j in range(3):
                self.op("dve", lambda e, MOD=MOD, ps=ps, l=l, j=j: e.tensor_tensor(
                    out=MOD.ap()[:, :, j],
                    in0=ps.ap()[:, 0:216].rearrange("p (c j) -> p c j", j=3)[:, :, j],
                    in1=self.BMOD[l].ap(), op=ALU.add), reads=[ps, self.BMOD[l]], writes=[MOD])
            self.rel(ps)
            self.MOD.append(MOD)
            for s in range(3):
                AAt = al("AA%d_%d" % (l, s), (3, 8), F32)
                GGt = al("GG%d_%d" % (l, s), (3, 8), F32)
                cs = 0.5 if s != 1 else 1.0
                for j in range(3):
                    sc = MOD[:, (s * 3 + 1) * 8:(s * 3 + 1) * 8 + 8, j]
                    gt = MOD[:, (s * 3 + 2) * 8:(s * 3 + 2) * 8 + 8, j]
                    self.stt("dve", AAt[:, j, :], sc, 1.0, self.NG[l][:, 2 * s, :], ALU.add, ALU.mult)
                    self.stt("dve", GGt[:, j, :], gt, cs, self.NG[l][:, 2 * s + 1, :], ALU.mult, ALU.mult)
                self.AA[(l, s)] = AAt; self.GG[(l, s)] = GGt
        A.free(SC, *WM)

    def shift_ap(self, l, s, j, k):
        c = (s * 3 + 0) * 8 + k
        return self.MOD[l][:, c, j:j + 1]

    def prenorm(self, HT, Yv, nt, l, s, j, SQ, RS, TMP):
        self.tt("pool", SQ[:, :, 0:nt], HT[:, :, 0:nt], HT[:, :, 0:nt], ALU.mult)
        ps = self.bank()
        for k in range(8):
            self.mm(ps[:, 0:nt], self.ONESM, SQ[:, k, 0:nt], start=(k == 0), stop=(k == 7))
        self.rstd(RS[:, 0:nt], ps[:, 0:nt])
        self.rel(ps)
        AAt = self.AA[(l, s)]
        for k in range(8):
            tmp = TMP[k % 2]
            self.stt("dve", tmp[:, 0:nt], HT[:, k, 0:nt], AAt[:, j, k:k + 1], RS[:, 0:nt],
                     ALU.mult, ALU.mult)
            self.act(Yv(k), tmp[:, 0:nt], AF.Identity, bias=self.shift_ap(l, s, j, k))

    def postnorm_residual(self, HT, Y2, SQ, RS, TMP, nt, l, s, j, part=0):
        if part in (0, 1):
            self.tt("pool", SQ[:, :, 0:nt], Y2[:, :, 0:nt], Y2[:, :, 0:nt], ALU.mult)
        if part == 1:
            return
        ps = self.bank()
        for k in range(8):
            self.mm(ps[:, 0:nt], self.ONESM, SQ[:, k, 0:nt], start=(k == 0), stop=(k == 7))
        self.rstd(RS[:, 0:nt], ps[:, 0:nt])
        self.rel(ps)
        GGt = self.GG[(l, s)]
        for k in range(8):
            tmp = TMP[k % 2]
            self.stt("dve", tmp[:, 0:nt], Y2[:, k, 0:nt], GGt[:, j, k:k + 1], RS[:, 0:nt], ALU.mult, ALU.mult)
            self.tt("pool", HT[:, k, 0:nt], HT[:, k, 0:nt], tmp[:, 0:nt], ALU.add)

    def ffn(self, b, l, s):
        A = self.A
        al = A.alloc
        sf = 0 if s == 0 else 1
        last_layer = (l == DEPTH - 1)
        tiles = [t for t in TILES if not (t[2] and last_layer and s == 2)]
        n = len(tiles)
        HT = [al("HT%d" % i, (8, 512), F32) for i in range(2)]
        Y = [al("Yt%d" % i, (8, 512), BF16) for i in range(2)]
        SQp = al("SQp", (8, 512), BF16); SQe = al("SQe", (8, 512), BF16)
        RSp = al("RSp", (512,), F32); RSe = al("RSe", (512,), F32)
        TMPp = [al("TMPp%d" % i, (512,), F32) for i in range(2)]
        TMPe = [al("TMPe%d" % i, (512,), F32) for i in range(2)]
        G = al("G", (NJ, 512), BF16)
        WI = [al("WI%d" % i, (2, 8, 256), BF16) for i in range(3)]
        WO = [al("WO%d" % i, (2, NJ, 128), BF16) for i in range(2)]
        SA = [al("SA%d" % i, (512,), F32) for i in range(2)]
        Y2 = al("Y2", (8, 512), F32)
        rr = {'wi': 0, 'wo': 0}
        jmod = lambda ti: 2 if tiles[ti][2] else b

        def pre(ti):
            t0, nt, isctx = tiles[ti]
            ht = HT[ti % 2]
            src, rd = self.h_src(b, l, s, tiles[ti])
            self.dma("sp", ht[:, :, 0:nt], src, reads=rd, writes=[ht])
            self.prenorm(ht, lambda k: Y[ti % 2][:, k, 0:nt], nt, l, s, jmod(ti), SQp, RSp, TMPp)

        def gphase(ti, jgs):
            t0, nt, isctx = tiles[ti]
            Yt = Y[ti % 2]
            for jg in jgs:
                wi = WI[rr['wi'] % 3]; rr['wi'] += 1
                self.wload(wi, "wi", (l, sf, jg))
                for jj in range(2):
                    jx = jg * 2 + jj
                    pa = self.bank(); pu = self.bank()
                    for k in range(8):
                        self.mm(pa[:, 0:nt], wi[:, jj, k, 0:128], Yt[:, k, 0:nt], start=(k == 0), stop=(k == 7))
                    for k in range(8):
                        self.mm(pu[:, 0:nt], wi[:, jj, k, 128:256], Yt[:, k, 0:nt], start=(k == 0), stop=(k == 7))
                    sa = SA[jx % 2]
                    self.act(sa[:, 0:nt], pa[:, 0:nt], AF.Silu)
                    self.tt("dve", G[:, jx, 0:nt], sa[:, 0:nt], pu[:, 0:nt], ALU.mult)
                    self.rel(pa, pu)

        def ophase(ti):
            t0, nt, isctx = tiles[ti]
            for dg in range(4):
                wo = WO[rr['wo'] % 2]; rr['wo'] += 1
                self.wload(wo, "wo", (l, sf, dg))
                for dd in range(2):
                    dc = dg * 2 + dd
                    po = self.bank()
                    for jx in range(NJ):
                        self.mm(po[:, 0:nt], wo[:, dd, jx, :], G[:, jx, 0:nt], start=(jx == 0), stop=(jx == NJ - 1))
                    self.copy("act", Y2[:, dc, 0:nt], po[:, 0:nt])
                    self.rel(po)

        def epi(ti, part):
            t0, nt, isctx = tiles[ti]
            ht = HT[ti % 2]
            self.postnorm_residual(ht, Y2, SQe, RSe, TMPe, nt, l, s, jmod(ti), part=part)
            if part == 2:
                dst, wr = self.h_dst(b, l, s, tiles[ti])
                self.dma("pool", dst, ht[:, :, 0:nt], reads=[ht], writes=wr)

        NPRE = 2
        pre(0)
        for ti in range(n):
            gphase(ti, range(NPRE if ti > 0 else 0, NJ // 2))
            if ti + 1 < n:
                pre(ti + 1)
            ophase(ti)
            epi(ti, 1)
            if ti + 1 < n:
                gphase(ti + 1, range(0, NPRE))
            epi(ti, 2)
        A.free(*HT, *Y, SQp, SQe, RSp, RSe, *TMPp, *TMPe, G, *WI, *WO, *SA, Y2)

    def rope_fm(self, out, ps, ps_sw, rows, t0, nt, RT):
        r1, r2 = RT
        self.tt("dve", r1[rows, 0:nt], ps, self.COSF[rows, t0:t0 + nt], ALU.mult)
        self.tt("dve", r2[rows, 0:nt], ps_sw, self.SINF[rows, t0:t0 + nt], ALU.mult)
        self.tt("pool", out, r1[rows, 0:nt], r2[rows, 0:nt], ALU.add)

    def proj_fm(self, W, c0, ncols, Y, t0, nt, ti):
        ps = self.bank()
        for k in range(8):
            self.mm(ps[0:ncols, 0:nt], W[:, k, c0:c0 + ncols], Y[:, k, t0:t0 + nt].k(ti),
                    start=(k == 0), stop=(k == 7))
        return ps

    def proj_tm4(self, W, c0, ncols, Y, chunk0, nch, ti):
        ps = self.bank()
        for cc in range(nch):
            tok0 = (chunk0 + cc) * 128
            for k in range(8):
                self.mm(ps[:, cc * ncols:(cc + 1) * ncols], Y[:, k, tok0:tok0 + 128].k(ti), W[:, k, c0:c0 + ncols],
                        start=(k == 0), stop=(k == 7))
        return ps

    def softmax_av(self, jobs, nq, O, DEN, PTS, hook=None, every=6):
        n = len(jobs)
        pend = None
        for i in range(n + 1):
            if i < n:
                kv, qv, vv, ov = jobs[i]
                S = self.bank()
                self.mm(S[:, 0:nq], kv, qv)
                PT = PTS[i % len(PTS)]
                self.act(PT[:, 0:nq], S[:, 0:nq], AF.Exp, scale=0.125)
                self.rel(S)
                cur = (PT, vv, ov, i)
            else:
                cur = None
            if pend is not None:
                PTp, vvp, ovp, ip = pend
                self.mm(O, vvp, PTp[:, 0:nq], start=(ip == 0), stop=(ip == n - 1))
                self.mm(DEN, ovp, PTp[:, 0:nq], start=(ip == 0), stop=(ip == n - 1))
            pend = cur
            if i == min(3, n - 1) and hook is not None and hook[0] is not None:
                hook[0]()
                hook[0] = None
            if i % every == every - 1 and i < n:
                yield

    @staticmethod
    def run_interleaved(gens):
        gens = [g for g in gens if g is not None]
        while gens:
            for g in list(gens):
                try:
                    next(g)
                except StopIteration:
                    gens.remove(g)

    def mixer(self, b, l):
        A = self.A
        al = A.alloc
        I = self.I
        s = 1
        need_ctx = (l < DEPTH - 1)
        TOC = lambda kc: min(kc // 4, 4)
        Y = al("Yfull", (8, NTOK), BF16)
        HT = [al("HT%d" % i, (8, 512), F32) for i in range(2)]
        SQ = al("SQ", (8, 512), BF16)
        RS = al("RS", (512,), F32)
        TMP = [al("TMP%d" % i, (512,), F32) for i in range(2)]
        for ti, tile in enumerate(TILES):
            t0, nt, isctx = tile
            j = 2 if isctx else b
            ht = HT[ti % 2]
            src, rd = self.h_src(b, l, s, tile)
            self.dma("sp", ht[:, :, 0:nt], src, reads=rd, writes=[ht])
            self.prenorm(ht, lambda k, t0=t0, nt=nt, ti=ti: Y[:, k, t0:t0 + nt].k(ti), nt, l, s, j, SQ, RS, TMP)
        A.free(*HT, SQ, *TMP)
        self.COSF = al("COSF", (SEQ,), F32); self.SINF = al("SINF", (SEQ,), F32)
        self.COST = al("COST", (16, 64), F32); self.SINT = al("SINT", (16, 64), F32)
        self.dma("sp", self.COSF, I["cosF"], writes=[self.COSF])
        self.dma("sp", self.SINF, I["sinF"], writes=[self.SINF])
        self.dma("sp", self.COST, I["cosT"].rearrange("p (a b) -> p a b", a=16), writes=[self.COST])
        self.dma("sp", self.SINT, I["sinT"].rearrange("p (a b) -> p a b", a=16), writes=[self.SINT])
        OA = [al("OA%d" % i, (NTOK,), BF16) for i in range(4)]
        OB = [al("OB%d" % i, (NTOK,), BF16) for i in range(4)]
        OC = [al("OC%d" % i, (NTOK,), BF16) for i in range(4)]
        RT = [al("RT%d" % i, (512,), F32) for i in range(2)]
        OD = al("OD", (512,), F32)
        SQ1 = al("SQ1", (512,), BF16)
        qtiles = [t for t in TILES if (not t[2]) or need_ctx]

        self.mark("  M1 b%d l%d" % (b, l))
        WA2 = [al("WA%d" % i, (8, 640), BF16) for i in range(2)]
        QT2 = [al("QT%d" % i, (NTOK,), BF16) for i in range(2)]
        KT2 = [al("KT%d" % i, (NTOK,), BF16) for i in range(2)]
        VT2 = [al("VT%d" % i, (18, 128), BF16) for i in range(2)]
        OCB = [al("OCB%d" % i, (512,), F32) for i in range(2)]
        PTS = [al("PT%d" % i, (512,), BF16) for i in range(3)]
        RD = al("RD", (512,), F32)

        def proj_head(hd):
            WA, QT, KT, VT = WA2[hd % 2], QT2[hd % 2], KT2[hd % 2], VT2[hd % 2]
            self.wload(WA, "wA", (l, hd))
            for ti, (t0, nt, isctx) in enumerate(TILES):
                for which, dst in ((0, QT), (256, KT)):
                    if which == 0 and isctx and not need_ctx:
                        continue
                    ps = self.proj_fm(WA, which, 128, Y, t0, nt, ti)
                    if not isctx:
                        ps2 = self.proj_fm(WA, which + 128, 128, Y, t0, nt, ti)
                        self.rope_fm(dst[:, t0:t0 + nt].k(ti), ps[:, 0:nt], ps2[:, 0:nt], slice(0, 128), t0, nt, RT)
                        self.rel(ps, ps2)
                    else:
                        self.copy("act", dst[:, t0:t0 + nt].k(ti), ps[:, 0:nt])
                        self.rel(ps)
                    yield
                nch = nt // 128
                ps = self.proj_tm4(WA, 512, 128, Y, t0 // 128, nch, ti)
                self.copy("act", VT[:, t0 // 128:t0 // 128 + nch, :].k(ti),
                          ps[:, 0:nch * 128].f(lambda a: a.rearrange("p (c n) -> p c n", n=128)))
                self.rel(ps)
                yield

        def attn_head(hd):
            QT, KT, VT = QT2[hd % 2], KT2[hd % 2], VT2[hd % 2]
            hook = [None]
            for ti, (q0, nq, isctx) in enumerate(qtiles):
                kcs = list(range(16, 18)) if isctx else list(range(18))
                for c in range(2):
                    O = self.bank(); DEN = self.bank()
                    rows = slice(c * 64, (c + 1) * 64)
                    jobs = [(KT[rows, kc * 128:(kc + 1) * 128].k(TOC(kc)), QT[rows, q0:q0 + nq].k(ti),
                             VT[:, kc, :].k(TOC(kc)), self.ONES1) for kc in kcs]
                    yield from self.softmax_av(jobs, nq, O[:, 0:nq], DEN[:, 0:nq], PTS, hook=hook)
                    self.recip(RD[:, 0:nq], DEN[:, 0:nq])
                    self.tt("dve", OCB[c][:, 0:nq], O[:, 0:nq], RD[:, 0:nq], ALU.mult)
                    self.rel(O, DEN)
                    yield

                def epilogue(ti=ti, q0=q0, nq=nq):
                    self.stt("dve", OD[:, 0:nq], OCB[1][:, 0:nq], self.NEGLAM[l][:, 0:1], OCB[0][:, 0:nq], ALU.mult, ALU.add)
                    self.tt("pool", SQ1[:, 0:nq], OD[:, 0:nq], OD[:, 0:nq], ALU.mult)
                    ps = self.bank()
                    self.mm(ps[:, 0:nq], self.ONES128, SQ1[:, 0:nq])
                    self.rstd(RS[:, 0:nq], ps[:, 0:nq])
                    self.rel(ps)
                    self.stt("dve", OA[hd][:, q0:q0 + nq].k(ti), OD[:, 0:nq], self.DNG[l][:, 0:1], RS[:, 0:nq],
                             ALU.mult, ALU.mult)
                hook[0] = epilogue
            if hook[0] is not None:
                hook[0]()
                hook[0] = None

        self.run_interleaved([proj_head(0)])
        for hd in range(4):
            self.run_interleaved([attn_head(hd), proj_head(hd + 1) if hd + 1 < 4 else None])
        A.free(*WA2, *QT2, *KT2, *VT2, *OCB, *PTS, RD)

        self.mark("  M2 b%d l%d" % (b, l))
        WB = al("WB", (8, 512), BF16)
        QR = al("QR", (NTOK,), BF16, parts=64); QF = al("QF", (NTOK,), BF16, parts=64)
        QB = al("QB", (NTOK,), BF16, parts=64); KR = al("KR", (NTOK,), BF16, parts=64)
        KTF = al("KTF", (18, 64), BF16); KTB = al("KTB", (18, 64), BF16)
        VT = al("VTr", (18, 128), BF16)
        GS = al("GS", (512,), BF16)
        T3 = RT[0]
        T4 = al("T4", (4, 64), F32); T5 = al("T5", (4, 64), F32)
        SFb = al("SFb", (18, 128), BF16, parts=64); SBb = al("SBb", (18, 128), BF16, parts=64)
        STF = al("STF", (128,), F32, parts=64); STB = al("STB", (128,), F32, parts=64)
        ATM = al("ATM", (4, 128), BF16)
        ORT = RT[1]
        for h in range(4):
            self.wload(WB, "wB", (l, h))
            r64 = slice(0, 64)
            for ti, (t0, nt, isctx) in enumerate(TILES):
                nch = nt // 128
                c0 = t0 // 128
                for which in (0, 128):
                    ps = self.proj_fm(WB, which, 64, Y, t0, nt, ti)
                    if not isctx:
                        ps2 = self.proj_fm(WB, which + 64, 64, Y, t0, nt, ti)
                        self.rope_fm(T3[0:64, 0:nt], ps[0:64, 0:nt], ps2[0:64, 0:nt], r64, t0, nt, RT)
                        self.rel(ps, ps2)
                    else:
                        self.copy("act", T3[0:64, 0:nt], ps[0:64, 0:nt])
                        self.rel(ps)
                    if which == 0:
                        self.copy("act", QR[:, t0:t0 + nt].k(ti), T3[0:64, 0:nt])
                        t3v = T3[0:64, 0:nt].f(lambda a: a.rearrange("p (c n) -> p c n", n=128))
                        self.tt("dve", QF[:, t0:t0 + nt].k(ti).f(lambda a: a.rearrange("p (c n) -> p c n", n=128)),
                                t3v, self.QDF4[(l, h)].k(None).bc(1, (64, nch, 128)), ALU.mult)
                        self.tt("pool", QB[:, t0:t0 + nt].k(ti).f(lambda a: a.rearrange("p (c n) -> p c n", n=128)),
                                t3v, self.QDB4[(l, h)].k(None).bc(1, (64, nch, 128)), ALU.mult)
                    else:
                        self.copy("act", KR[:, t0:t0 + nt].k(ti), T3[0:64, 0:nt])
                ps = self.proj_tm4(WB, 128, 128, Y, c0, nch, ti)
                pv = ps[:, 0:nch * 128].f(lambda a: a.rearrange("p (c n) -> p c n", n=128))
                if not isctx:
                    self.tt("dve", T4[:, 0:nch, :], pv.f(lambda a: a.rearrange("p (c n) -> p c n", n=128)[:, :, 0:64]),
                            self.COST[:, c0:c0 + nch, :], ALU.mult)
                    self.tt("dve", T5[:, 0:nch, :], pv.f(lambda a: a.rearrange("p (c n) -> p c n", n=128)[:, :, 64:128]),
                            self.SINT[:, c0:c0 + nch, :], ALU.mult)
                    self.tt("pool", T4[:, 0:nch, :], T4[:, 0:nch, :], T5[:, 0:nch, :], ALU.add)
                else:
                    self.copy("act", T4[:, 0:nch, :], pv.f(lambda a: a.rearrange("p (c n) -> p c n", n=128)[:, :, 0:64]))
                self.rel(ps)
                self.ts("dve", KTF[:, c0:c0 + nch, :].k(ti), T4[:, 0:nch, :], self.KDF[(l, h)][:, 0:1], ALU.mult)
                self.ts("pool", KTB[:, c0:c0 + nch, :].k(ti), T4[:, 0:nch, :], self.KDB[(l, h)][:, 0:1], ALU.mult)
                ps = self.proj_tm4(WB, 256, 128, Y, c0, nch, ti)
                self.copy("act", VT[:, c0:c0 + nch, :].k(ti),
                          ps[:, 0:nch * 128].f(lambda a: a.rearrange("p (c n) -> p c n", n=128)))
                self.rel(ps)
            SD = self.SDEC[(l, h)]
            def ubank(chs, KTX):
                ps = self.bank()
                for i_, c in enumerate(chs):
                    self.mm(ps[0:64, i_ * 128:(i_ + 1) * 128], KTX[:, c, :].k(TOC(c)), VT[:, c, :].k(TOC(c)))
                return ps
            self.memset("pool", SFb[:, 16, :], 0.0)
            ps = ubank([16, 17], KTF)
            self.copy("act", STF, ps[0:64, 0:128])
            self.copy("act", SFb[:, 17, :], ps[0:64, 0:128])
            self.stt("dve", STF, STF, SD[0:64, 0:1], ps[0:64, 128:256], ALU.mult, ALU.add)
            self.rel(ps)
            self.copy("act", SFb[:, 0, :], STF)
            for g0 in range(0, 16, 4):
                ps = ubank(list(range(g0, g0 + 4)), KTF)
                for i_ in range(4):
                    c = g0 + i_
                    if c == 15:
                        break
                    self.stt("dve", STF, STF, SD[0:64, 0:1], ps[0:64, i_ * 128:(i_ + 1) * 128], ALU.mult, ALU.add)
                    self.copy("act", SFb[:, c + 1, :], STF)
                self.rel(ps)
            self.memset("pool", SBb[:, 17, :], 0.0)
            ps = ubank([17, 16], KTB)
            self.copy("act", STB, ps[0:64, 0:128])
            self.copy("act", SBb[:, 16, :], ps[0:64, 0:128])
            self.stt("dve", STB, STB, SD[0:64, 1:2], ps[0:64, 128:256], ALU.mult, ALU.add)
            self.rel(ps)
            self.copy("act", SBb[:, 15, :], STB)
            for g0 in range(15, -1, -4):
                chs = list(range(g0, g0 - 4, -1))
                ps = ubank(chs, KTB)
                for i_, c in enumerate(chs):
                    if c == 0:
                        break
                    self.stt("dve", STB, STB, SD[0:64, 1:2], ps[0:64, i_ * 128:(i_ + 1) * 128], ALU.mult, ALU.add)
                    self.copy("act", SBb[:, c - 1, :], STB)
                self.rel(ps)
            for ti, (t0, nt, isctx) in enumerate(qtiles):
                nch = nt // 128
                c0 = t0 // 128
                pa = self.bank()
                for cc in range(nch):
                    tk = slice((c0 + cc) * 128, (c0 + cc + 1) * 128)
                    self.mm(pa[:, cc * 128:(cc + 1) * 128], KR[:, tk].k(ti), QR[:, tk].k(ti))
                self.tt("dve", ATM[:, 0:nch, :], pa[:, 0:nch * 128].f(lambda a: a.rearrange("p (c n) -> p c n", n=128)),
                        self.MASK4[(l, h)].k(None).bc(1, (128, nch, 128)), ALU.mult)
                self.rel(pa)
                po = self.bank()
                for cc in range(nch):
                    c = c0 + cc
                    tk = slice(c * 128, (c + 1) * 128)
                    zero_f = (c == 16); zero_b = (c == 17)
                    outv = po[:, cc * 128:(cc + 1) * 128]
                    self.mm(outv, VT[:, c, :].k(ti), ATM[:, cc, :], start=True, stop=(zero_f and zero_b))
                    if not zero_f:
                        self.mm(outv, SFb[:, c, :], QF[:, tk].k(ti), start=False, stop=zero_b)
                    if not zero_b:
                        self.mm(outv, SBb[:, c, :], QB[:, tk].k(ti), start=False, stop=True)
                self.copy("act", ORT[:, 0:nt], po[:, 0:nt])
                self.rel(po)
                self.tt("pool", SQ1[:, 0:nt], ORT[:, 0:nt], ORT[:, 0:nt], ALU.mult)
                ps = self.bank()
                self.mm(ps[:, 0:nt], self.ONES128, SQ1[:, 0:nt])
                self.rstd(RS[:, 0:nt], ps[:, 0:nt])
                self.rel(ps)
                self.stt("dve", OD[:, 0:nt], ORT[:, 0:nt], self.HG[l][:, 1:2], RS[:, 0:nt], ALU.mult, ALU.mult)
                ps = self.proj_fm(WB, 384, 128, Y, t0, nt, ti)
                self.act(GS[:, 0:nt], ps[:, 0:nt], AF.Silu)
                self.rel(ps)
                self.tt("pool", OB[h][:, t0:t0 + nt].k(ti), OD[:, 0:nt], GS[:, 0:nt], ALU.mult)
        A.free(WB, QR, QF, QB, KR, KTF, KTB, VT, GS, T4, T5, SFb, SBb, STF, STB, ATM)

        self.mark("  M3 b%d l%d" % (b, l))
        WCK = al("WCK", (8, 640), BF16)
        KD = [al("KD%d" % g, (NTOK,), BF16) for g in range(2)]
        VZ = [[al("VZ%d%d" % (g, hh), (18, 128), BF16) for hh in range(2)] for g in range(2)]
        SQK = al("SQK", (512,), BF16)
        RSK = al("RSK", (512,), F32)
        QT2 = [al("QTc%d" % i, (NTOK,), BF16) for i in range(2)]
        WCQ2 = [al("WCQ%d" % i, (8, 256), BF16) for i in range(2)]
        PTS = [al("PT%d" % i, (512,), BF16) for i in range(3)]
        RD = al("RD", (512,), F32)
        self.wload(WCK, "wCk", (l,))

        def qk_tile(W, c0, dst, gcol, t0, nt, ti, isctx):
            ps = self.proj_fm(W, c0, 128, Y, t0, nt, ti)
            self.act(SQK[:, 0:nt], ps[:, 0:nt], AF.Square)
            pss = self.bank()
            self.mm(pss[:, 0:nt], self.BD64, SQK[:, 0:nt])
            self.rstd(RSK[:, 0:nt], pss[:, 0:nt])
            self.rel(pss)
            gq = self.HG[l][:, gcol:gcol + 1]; gsw = self.HG[l][:, gcol + 1:gcol + 2]
            if not isctx:
                ps2 = self.proj_fm(W, c0 + 128, 128, Y, t0, nt, ti)
                r1, r2 = RT
                self.stt("dve", r1[:, 0:nt], ps[:, 0:nt], gq, self.COSF[:, t0:t0 + nt], ALU.mult, ALU.mult)
                self.stt("dve", r2[:, 0:nt], ps2[:, 0:nt], gsw, self.SINF[:, t0:t0 + nt], ALU.mult, ALU.mult)
                self.rel(ps, ps2)
                self.tt("pool", r1[:, 0:nt], r1[:, 0:nt], r2[:, 0:nt], ALU.add)
                self.tt("dve", dst[:, t0:t0 + nt].k(ti), r1[:, 0:nt], RSK[:, 0:nt], ALU.mult)
            else:
                self.stt("dve", dst[:, t0:t0 + nt].k(ti), ps[:, 0:nt], gq, RSK[:, 0:nt], ALU.mult, ALU.mult)
                self.rel(ps)

        for ti, (t0, nt, isctx) in enumerate(TILES):
            nch = nt // 128
            c0 = t0 // 128
            for g in range(2):
                qk_tile(WCK, g * 256, KD[g], 4, t0, nt, ti, isctx)
            ps = self.proj_tm4(WCK, 512, 128, Y, c0, nch, ti)
            pv = ps[:, 0:nch * 128]
            for g in range(2):
                for hh in range(2):
                    self.memset("pool", VZ[g][hh][:, c0:c0 + nch, :].k(ti), 0.0)
            for g in range(2):
                for hh in range(2):
                    self.copy("act" if hh == 0 else "dve", VZ[g][hh][:, c0:c0 + nch, hh * 64:(hh + 1) * 64].k(ti),
                              pv.f(lambda a, g=g: a.rearrange("p (c n) -> p c n", n=128)[:, :, g * 64:(g + 1) * 64]))
            self.rel(ps)

        def proj_q(qi):
            WCQ, QT = WCQ2[qi % 2], QT2[qi % 2]
            self.wload(WCQ, "wCq", (l, qi))
            for ti, (t0, nt, isctx) in enumerate(qtiles):
                qk_tile(WCQ, 0, QT, 2, t0, nt, ti, isctx)
                yield

        def attn_q(qi):
            g = qi // 2
            QT = QT2[qi % 2]
            for ti, (q0, nq, isctx) in enumerate(qtiles):
                kcs = list(range(16, 18)) if isctx else list(range(18))
                O = self.bank(); DEN = self.bank()
                jobs = []
                for hh in range(2):
                    rows = slice(hh * 64, (hh + 1) * 64)
                    for kc in kcs:
                        jobs.append((KD[g][rows, kc * 128:(kc + 1) * 128].k(TOC(kc)), QT[rows, q0:q0 + nq].k(ti),
                                     VZ[g][hh][:, kc, :].k(TOC(kc)), self.ONESL if hh == 0 else self.ONESR))
                yield from self.softmax_av(jobs, nq, O[:, 0:nq], DEN[:, 0:nq], PTS)
                self.recip(RD[:, 0:nq], DEN[:, 0:nq])
                self.tt("dve", OC[qi][:, q0:q0 + nq].k(ti), O[:, 0:nq], RD[:, 0:nq], ALU.mult)
                self.rel(O, DEN)
                yield

        self.run_interleaved([proj_q(0)])
        for qi in range(4):
            self.run_interleaved([attn_q(qi), proj_q(qi + 1) if qi + 1 < 4 else None])
        A.free(WCK, *KD, VZ[0][0], VZ[0][1], VZ[1][0], VZ[1][1], SQK, RSK, *QT2, *WCQ2)
        A.free(self.COSF, self.SINF, self.COST, self.SINT, *RT, *PTS, RD, OD, SQ1)

        self.mark("  M4 b%d l%d" % (b, l))
        WX = [al("WX%d" % i, (12, 512), BF16) for i in range(2)]
        wx_rr = [0]
        def wx():
            w = WX[wx_rr[0] % 2]; wx_rr[0] += 1
            return w
        SG = [al("SG%d" % i, (512,), F32) for i in range(2)]
        TB = al("TB", (512,), F32)
        MA = al("MA", (8, 512), F32)
        MB = al("MB", (8, 512), BF16)
        HTm = al("HTm", (8, 512), F32)
        TMP = [al("TMPm%d" % i, (512,), F32) for i in range(2)]
        OBR = (OA, OB, OC)
        for ti, tile in enumerate(qtiles):
            t0, nt, isctx = tile
            j = 2 if isctx else b
            src, rd = self.h_src(b, l, s, tile)
            self.dma("sp", HTm[:, :, 0:nt], src, reads=rd, writes=[HTm])
            sgi = 0
            for i in range(3):
                for hf in range(2):
                    wg = wx()
                    self.wload(wg, "wGB", (l, i, hf))
                    for dd in range(4):
                        dc = hf * 4 + dd
                        pg = self.proj_fm(wg, dd * 128, 128, Y, t0, nt, ti)
                        sg = SG[sgi % 2]; sgi += 1
                        self.act(sg[:, 0:nt], pg[:, 0:nt], AF.Sigmoid, bias=self.BG[l][:, i * 8 + dc:i * 8 + dc + 1])
                        self.rel(pg)
                        pb = self.bank()
                        for kc in range(4):
                            self.mm(pb[:, 0:nt], wg[:, 8 + kc, dd * 128:(dd + 1) * 128], OBR[i][kc][:, t0:t0 + nt].k(ti),
                                    start=(kc == 0), stop=(kc == 3))
                        if i == 0:
                            self.tt("dve", MA[:, dc, 0:nt], sg[:, 0:nt], pb[:, 0:nt], ALU.mult)
                        else:
                            self.tt("dve", TB[:, 0:nt], sg[:, 0:nt], pb[:, 0:nt], ALU.mult)
                            if i == 1:
                                self.tt("pool", MA[:, dc, 0:nt], MA[:, dc, 0:nt], TB[:, 0:nt], ALU.add)
                            else:
                                self.tt("pool", MB[:, dc, 0:nt], MA[:, dc, 0:nt], TB[:, 0:nt], ALU.add)
                        self.rel(pb)
            for hf in range(2):
                wo_ = wx()
                self.wload(wo_[:, 0:8, :], "wO", (l, hf))
                for dd in range(4):
                    dc = hf * 4 + dd
                    po = self.bank()
                    for k in range(8):
                        self.mm(po[:, 0:nt], wo_[:, k, dd * 128:(dd + 1) * 128], MB[:, k, 0:nt], start=(k == 0), stop=(k == 7))
                    self.copy("act", MA[:, dc, 0:nt], po[:, 0:nt])
                    self.rel(po)
            self.postnorm_residual(HTm, MA, MB, RS, TMP, nt, l, s, j)
            dst, wr = self.h_dst(b, l, s, tile)
            self.dma("pool", dst, HTm[:, :, 0:nt], reads=[HTm], writes=wr)
        if not need_ctx:
            pass
        A.free(*WX, *SG, TB, MA, MB, HTm, *TMP, RS, Y, *OA, *OB, *OC)

def _swap64(w):
    k, n = w.shape
    return w.reshape(k, n // 64, 2, 32)[:, :, ::-1, :].reshape(k, n)


def _blk(w):
    k, n = w.shape
    return np.ascontiguousarray(w.reshape(k // 128, 128, n).transpose(1, 0, 2)).reshape(128, (k // 128) * n)


def rope_tables():
    rows = SEQ // 64
    row = np.repeat(np.arange(rows, dtype=np.float32), 64)
    col = np.tile(np.arange(64, dtype=np.float32), rows)
    n_freq = 16
    inv = (10000.0 ** (-np.arange(n_freq, dtype=np.float32) / n_freq)).astype(np.float32)
    ang = np.concatenate([row[:, None] * inv, col[:, None] * inv], axis=-1).astype(np.float32)
    cos, sin = np.cos(ang).astype(np.float32), np.sin(ang).astype(np.float32)
    d = np.arange(64)
    cos64 = cos[:, d % 32]
    sin64 = sin[:, d % 32] * np.where(d < 32, -1.0, 1.0).astype(np.float32)
    cosF = np.ascontiguousarray(np.concatenate([cos64, cos64], axis=1).T)
    sinF = np.ascontiguousarray(np.concatenate([sin64, sin64], axis=1).T)
    cosT = np.ascontiguousarray(cos64.reshape(16, 128, 64).transpose(1, 0, 2)).reshape(128, 16 * 64)
    sinT = np.ascontiguousarray(sin64.reshape(16, 128, 64).transpose(1, 0, 2)).reshape(128, 16 * 64)
    return cosF, sinF, cosT, sinT


def shared_inputs(w_mod, b_mod, norm_g, w_ffn_in, w_ffn_out, w_in, b_gate, diff_lambda, diff_norm_g,
                  ret_decay_logit, ret_norm_g, qk_norm_g, w_branch, w_out):
    f = np.float32
    S = {}
    S["wmod"] = np.ascontiguousarray(w_mod, dtype=f)
    S["bmodT"] = np.ascontiguousarray(b_mod.reshape(DEPTH, 72, 128).transpose(0, 2, 1), dtype=f)
    S["normgT"] = np.ascontiguousarray(norm_g.reshape(DEPTH, 6, 8, 128).transpose(0, 3, 1, 2), dtype=f)
    wi = np.empty((DEPTH, 2, NJ // 2, 128, 2 * 8 * 256), f)
    wo = np.empty((DEPTH, 2, 4, 128, 2 * NJ * 128), f)
    for l in range(DEPTH):
        for s in range(2):
            w = w_ffn_in[l, s]
            a = w[:, :DFF].reshape(8, 128, NJ, 128); u = w[:, DFF:].reshape(8, 128, NJ, 128)
            au = np.stack([a, u], axis=3)
            au = au.transpose(2, 1, 0, 3, 4)
            au = au.reshape(NJ // 2, 2, 128, 8, 256).transpose(0, 2, 1, 3, 4)
            wi[l, s] = au.reshape(NJ // 2, 128, 2 * 8 * 256)
            w2 = w_ffn_out[l, s].reshape(NJ, 128, 8, 128)
            w2 = w2.transpose(2, 1, 0, 3)
            w2 = w2.reshape(4, 2, 128, NJ, 128).transpose(0, 2, 1, 3, 4)
            wo[l, s] = w2.reshape(4, 128, 2 * NJ * 128)
    S["wi"] = wi; S["wo"] = wo
    sizes = (512, 512, 512, 256, 256, 512, 512, 512, 128, 128, 3072)
    names = ('a_q', 'a_k', 'a_v', 'b_q', 'b_k', 'b_v', 'b_g', 'c_q', 'c_k', 'c_v', 'gate')
    sp = {}; lo = 0
    for nm, sz in zip(names, sizes):
        sp[nm] = (lo, lo + sz); lo += sz
    wA = np.empty((DEPTH, 4, 128, 8 * 640), f); wB = np.empty((DEPTH, 4, 128, 8 * 512), f)
    wCk = np.empty((DEPTH, 128, 8 * 640), f); wCq = np.empty((DEPTH, 4, 128, 8 * 256), f)
    wGB = np.empty((DEPTH, 3, 2, 128, 12 * 512), f)
    wO = np.empty((DEPTH, 2, 128, 8 * 512), f)
    for l in range(DEPTH):
        W = w_in[l]
        col = lambda nm, a, n: W[:, sp[nm][0] + a: sp[nm][0] + a + n]
        for h in range(4):
            q = col('a_q', h * 128, 128); k = col('a_k', h * 128, 128); v = col('a_v', h * 128, 128)
            wA[l, h] = _blk(np.concatenate([q, _swap64(q), k, _swap64(k), v], axis=1))
            q = col('b_q', h * 64, 64); k = col('b_k', h * 64, 64)
            v = col('b_v', h * 128, 128); g = col('b_g', h * 128, 128)
            wB[l, h] = _blk(np.concatenate([q, _swap64(q), k, _swap64(k), v, g], axis=1))
        k0 = col('c_k', 0, 64); k1 = col('c_k', 64, 64); v = col('c_v', 0, 128)
        wCk[l] = _blk(np.concatenate([k0, k0, _swap64(k0), _swap64(k0), k1, k1, _swap64(k1), _swap64(k1), v], axis=1))
        for qi in range(4):
            q = col('c_q', qi * 128, 128)
            wCq[l, qi] = _blk(np.concatenate([q, _swap64(q)], axis=1))
        for i in range(3):
            for hf in range(2):
                wb = w_branch[l, i][:, hf * 512:(hf + 1) * 512]
                wGB[l, i, hf] = np.concatenate([_blk(col('gate', i * 1024 + hf * 512, 512)), _blk(wb)], axis=1)
        for hf in range(2):
            wO[l, hf] = _blk(w_out[l][:, hf * 512:(hf + 1) * 512])
    S.update(wA=wA, wB=wB, wCk=wCk, wCq=wCq, wGB=wGB, wO=wO)
    S["bgT"] = np.ascontiguousarray(b_gate.reshape(DEPTH, 24, 128).transpose(0, 2, 1), dtype=f)
    S["lamb"] = np.ascontiguousarray(np.broadcast_to(diff_lambda.reshape(DEPTH, 1, 256), (DEPTH, 128, 256)), dtype=f)
    hg = np.empty((DEPTH, 128, 6), f)
    sw = (np.arange(64) + 32) % 64
    for l in range(DEPTH):
        hg[l, :, 0] = diff_norm_g[l]
        hg[l, :, 1] = ret_norm_g[l]
        hg[l, :, 2] = np.tile(qk_norm_g[l, 0], 2)
        hg[l, :, 3] = np.tile(qk_norm_g[l, 0][sw], 2)
        hg[l, :, 4] = np.tile(qk_norm_g[l, 1], 2)
        hg[l, :, 5] = np.tile(qk_norm_g[l, 1][sw], 2)
    S["hg"] = hg
    S["retl"] = np.ascontiguousarray(np.broadcast_to(ret_decay_logit.reshape(DEPTH, 1, 8), (DEPTH, 128, 8)), dtype=f)
    cosF, sinF, cosT, sinT = rope_tables()
    S.update(cosF=cosF, sinF=sinF, cosT=cosT, sinT=sinT)
    i = np.arange(128, dtype=f)
    S["rel"] = np.ascontiguousarray(i[None, :] - i[:, None])
    S["posf"] = np.ascontiguousarray(np.broadcast_to(i[None, :], (128, 128)))
    S["posp"] = np.ascontiguousarray(i[:, None])
    return S


def core_inputs(x, c, ctx, c_ctx, core):
    f = np.float32
    b0 = core * BPC
    M = {}
    M["xT"] = np.ascontiguousarray(x[b0:b0 + BPC].transpose(0, 2, 1), dtype=f)
    M["ctxT"] = np.ascontiguousarray(ctx[b0:b0 + BPC].transpose(0, 2, 1), dtype=f)
    cc = np.stack([c[b0], c[b0 + 1], c_ctx], axis=1)
    M["cT"] = np.ascontiguousarray(cc.reshape(8, 128, 3).transpose(1, 0, 2), dtype=f)
    return M


_CACHE = {}


def get_program(nsub=6):
    if nsub not in _CACHE:
        bld = Builder(nsub=nsub)
        with bld.stack:
            bld.build()
        _CACHE[nsub] = bld
    return _CACHE[nsub]


def kernel(x, c, ctx, c_ctx, w_mod, b_mod, norm_g, w_ffn_in, w_ffn_out, w_in, b_gate,
           diff_lambda, diff_norm_g, ret_decay_logit, ret_norm_g, qk_norm_g, w_branch, w_out, _nsub=6):
    args = [np.asarray(a) for a in (w_mod, b_mod, norm_g, w_ffn_in, w_ffn_out, w_in, b_gate, diff_lambda,
                                    diff_norm_g, ret_decay_logit, ret_norm_g, qk_norm_g, w_branch, w_out)]
    S = shared_inputs(*args)
    x = np.asarray(x); c = np.asarray(c); ctx = np.asarray(ctx); c_ctx = np.asarray(c_ctx)
    in_maps = []
    for core in range(N_CORES):
        m = dict(S)
        m.update(core_inputs(x, c, ctx, c_ctx, core))
        in_maps.append(m)
    bld = get_program(_nsub)
    res = run_bass_kernel_spmd(bld.nc, in_maps, core_ids=list(range(N_CORES)))
    out = np.empty((N_CORES * BPC, SEQ, D), np.float32)
    for core in range(N_CORES):
        o = res.results[core]["outT"]
        out[core * BPC:(core + 1) * BPC] = o.transpose(0, 2, 1)
    return out
```
